# Optimizing a Trainium2 kernel written in Bass

```python
import math
import jax, jax.numpy as jnp
from jax import lax
import numpy as np

D_MODEL = 1024
BATCH = 8
SEQ = 2048
DEPTH = 4
DEC_BATCH = 128
DEC_SEQ = 1
PAST_LEN = 16384
PAGE_SIZE = 128

CONV_WIDTH = D_MODEL // 2
CONV_HEADS = 8
CONV_K = 3
SSM_WIDTH = D_MODEL // 2
SSM_GROUP = 16
SSM_GROUPS = SSM_WIDTH // SSM_GROUP
SSM_STATE = 64
_FF_RAW = (8 * D_MODEL + 2) // 3
D_FF = ((_FF_RAW + 255) // 256) * 256
N_IN = 3 * CONV_WIDTH + SSM_WIDTH + 2 * D_MODEL
RMS_EPS = 1e-6
DT_MIN = 1e-3
DT_MAX = 1e-1

kernel_name = "gated_conv_s5_hybrid_step"


def _rmsnorm(x, g):
    xf = x.astype(jnp.float32)
    y = xf * lax.rsqrt(jnp.mean(xf * xf, axis=-1, keepdims=True) + RMS_EPS)
    return (y * g.astype(jnp.float32)).astype(x.dtype)


def _conv_causal(cin, buf, w, b):
    if buf is None:
        pad = jnp.pad(cin, ((0, 0), (CONV_K - 1, 0), (0, 0)))
    else:
        pad = jnp.concatenate([buf.astype(cin.dtype), cin], axis=1)
    L = cin.shape[1]
    out = b
    for k in range(CONV_K):
        out = out + w[k] * pad[:, k:k + L]
    return out, pad[:, pad.shape[1] - (CONV_K - 1):]


def _cmul(ar, ai, br, bi):
    return ar * br - ai * bi, ar * bi + ai * br


def _ssm_combine(e1, e2):
    a1r, a1i, b1r, b1i = e1
    a2r, a2i, b2r, b2i = e2
    ar, ai = _cmul(a2r, a2i, a1r, a1i)
    br, bi = _cmul(a2r, a2i, b1r, b1i)
    return ar, ai, br + b2r, bi + b2i


def _s5(u, h0_re, h0_im, lam_re, lam_im, log_dt, b_re, b_im, c_re, c_im, d_skip):
    N, L, _ = u.shape
    f32 = jnp.float32
    uf = u.astype(f32).reshape(N, L, SSM_GROUPS, SSM_GROUP)
    dt = jnp.exp(log_dt.astype(f32))[:, None]
    lr = lam_re.astype(f32)
    li = lam_im.astype(f32)
    mag = jnp.exp(lr * dt)
    ang = li * dt
    abar_r = mag * jnp.cos(ang)
    abar_i = mag * jnp.sin(ang)
    nr = abar_r - 1.0
    ni = abar_i
    den = lr * lr + li * li
    fr = ((nr * lr + ni * li) / den)[..., None]
    fi = ((ni * lr - nr * li) / den)[..., None]
    br = b_re.astype(f32)
    bi = b_im.astype(f32)
    bbar_r = fr * br - fi * bi
    bbar_i = fr * bi + fi * br
    bu_r = jnp.einsum('blgc,gpc->blgp', uf, bbar_r)
    bu_i = jnp.einsum('blgc,gpc->blgp', uf, bbar_i)
    a_r = jnp.broadcast_to(abar_r, bu_r.shape)
    a_i = jnp.broadcast_to(abar_i, bu_i.shape)
    A_r, A_i, h_r, h_i = lax.associative_scan(_ssm_combine, (a_r, a_i, bu_r, bu_i), axis=1)
    if h0_re is not None:
        pr, pi = _cmul(A_r, A_i, h0_re.astype(f32)[:, None], h0_im.astype(f32)[:, None])
        h_r = h_r + pr
        h_i = h_i + pi
    y = (jnp.einsum('blgp,gcp->blgc', h_r, c_re.astype(f32))
         - jnp.einsum('blgp,gcp->blgc', h_i, c_im.astype(f32)))
    y = y + d_skip.astype(f32).reshape(SSM_GROUPS, SSM_GROUP) * uf
    return (y.reshape(N, L, SSM_WIDTH).astype(u.dtype),
            h_r[:, L - 1].astype(u.dtype), h_i[:, L - 1].astype(u.dtype))


def _layer(x, conv_buf, h0_re, h0_im, norm1_g, w_in, conv_w, conv_b, lam_re, lam_im, log_dt,
           b_re, b_im, c_re, c_im, d_skip, w_glu, b_glu, w_branch, w_out, norm2_g, w_gate_up, w_down):
    h = _rmsnorm(x, norm1_g)
    z = h @ w_in
    cuts = [CONV_WIDTH, 2 * CONV_WIDTH, 3 * CONV_WIDTH, 3 * CONV_WIDTH + SSM_WIDTH,
            3 * CONV_WIDTH + SSM_WIDTH + D_MODEL]
    zb, zc, zv, zu, ga, gs = jnp.split(z, cuts, axis=-1)
    conv_out, new_buf = _conv_causal(zc * zv, conv_buf, conv_w, conv_b)
    ya = zb * conv_out
    ys, hr, hi = _s5(zu, h0_re, h0_im, lam_re, lam_im, log_dt, b_re, b_im, c_re, c_im, d_skip)
    gy = jax.nn.gelu(ys, approximate=False)
    ys = gy * jax.nn.sigmoid(gy @ w_glu + b_glu)
    oa = ya @ w_branch[:CONV_WIDTH]
    ob = ys @ w_branch[CONV_WIDTH:]
    m = jax.nn.sigmoid(ga) * oa + jax.nn.sigmoid(gs) * ob
    x = x + m @ w_out
    h2 = _rmsnorm(x, norm2_g)
    gate, up = jnp.split(h2 @ w_gate_up, [D_FF], axis=-1)
    x = x + (jax.nn.silu(gate) * up) @ w_down
    return x, new_buf, hr, hi


def _trunk(x, st_conv, st_re, st_im, norm1_g, w_in, conv_w, conv_b, ssm_lam_re, ssm_lam_im,
           ssm_log_dt, ssm_b_re, ssm_b_im, ssm_c_re, ssm_c_im, ssm_d, w_glu, b_glu, w_branch,
           w_out, norm2_g, w_gate_up, w_down, final_g):
    bufs, res, ims = [], [], []
    for l in range(DEPTH):
        cb = None if st_conv is None else st_conv[l]
        hr0 = None if st_re is None else st_re[l]
        hi0 = None if st_im is None else st_im[l]
        x, nb, nr, ni = _layer(x, cb, hr0, hi0, norm1_g[l], w_in[l], conv_w[l], conv_b[l],
                               ssm_lam_re[l], ssm_lam_im[l], ssm_log_dt[l], ssm_b_re[l], ssm_b_im[l],
                               ssm_c_re[l], ssm_c_im[l], ssm_d[l], w_glu[l], b_glu[l], w_branch[l],
                               w_out[l], norm2_g[l], w_gate_up[l], w_down[l])
        bufs.append(nb)
        res.append(nr)
        ims.append(ni)
    return _rmsnorm(x, final_g), jnp.stack(bufs), jnp.stack(res), jnp.stack(ims)


def setup_inputs(seed: int = 0) -> dict:
    key = jax.random.key(seed)
    ks = jax.random.split(key, 32)
    f32 = jnp.float32
    nrm = lambda k, s, sc: jax.random.normal(k, s, f32) * sc
    lam_im_base = jnp.pi * jnp.arange(SSM_STATE, dtype=f32)
    return {
        "x_prompt": nrm(ks[0], (BATCH, SEQ, D_MODEL), 1.0),
        "x_sample": nrm(ks[1], (DEC_BATCH, DEC_SEQ, D_MODEL), 1.0),
        "state_conv": nrm(ks[2], (DEPTH, DEC_BATCH, CONV_K - 1, CONV_WIDTH), 1.0),
        "state_ssm_re": nrm(ks[3], (DEPTH, DEC_BATCH, SSM_GROUPS, SSM_STATE), 0.5),
        "state_ssm_im": nrm(ks[4], (DEPTH, DEC_BATCH, SSM_GROUPS, SSM_STATE), 0.5),
        "norm1_g": 1.0 + nrm(ks[5], (DEPTH, D_MODEL), 0.02),
        "w_in": nrm(ks[6], (DEPTH, D_MODEL, N_IN), D_MODEL ** -0.5),
        "conv_w": nrm(ks[7], (DEPTH, CONV_K, CONV_WIDTH), CONV_K ** -0.5),
        "conv_b": nrm(ks[8], (DEPTH, CONV_WIDTH), 0.01),
        "ssm_lam_re": -0.5 + nrm(ks[9], (DEPTH, SSM_GROUPS, SSM_STATE), 0.01),
        "ssm_lam_im": lam_im_base + nrm(ks[10], (DEPTH, SSM_GROUPS, SSM_STATE), 0.01),
        "ssm_log_dt": jax.random.uniform(ks[11], (DEPTH, SSM_GROUPS), f32,
                                         math.log(DT_MIN), math.log(DT_MAX)),
        "ssm_b_re": nrm(ks[12], (DEPTH, SSM_GROUPS, SSM_STATE, SSM_GROUP), (2 * SSM_GROUP) ** -0.5),
        "ssm_b_im": nrm(ks[13], (DEPTH, SSM_GROUPS, SSM_STATE, SSM_GROUP), (2 * SSM_GROUP) ** -0.5),
        "ssm_c_re": nrm(ks[14], (DEPTH, SSM_GROUPS, SSM_GROUP, SSM_STATE), SSM_STATE ** -0.5),
        "ssm_c_im": nrm(ks[15], (DEPTH, SSM_GROUPS, SSM_GROUP, SSM_STATE), SSM_STATE ** -0.5),
        "ssm_d": nrm(ks[16], (DEPTH, SSM_WIDTH), 1.0),
        "w_glu": nrm(ks[17], (DEPTH, SSM_WIDTH, SSM_WIDTH), SSM_WIDTH ** -0.5),
        "b_glu": nrm(ks[18], (DEPTH, SSM_WIDTH), 0.01),
        "w_branch": nrm(ks[19], (DEPTH, CONV_WIDTH + SSM_WIDTH, D_MODEL), (D_MODEL // 2) ** -0.5),
        "w_out": nrm(ks[20], (DEPTH, D_MODEL, D_MODEL), (2 * DEPTH * D_MODEL) ** -0.5),
        "norm2_g": 1.0 + nrm(ks[21], (DEPTH, D_MODEL), 0.02),
        "w_gate_up": nrm(ks[22], (DEPTH, D_MODEL, 2 * D_FF), D_MODEL ** -0.5),
        "w_down": nrm(ks[23], (DEPTH, D_FF, D_MODEL), (2 * DEPTH * D_FF) ** -0.5),
        "final_g": 1.0 + nrm(ks[24], (D_MODEL,), 0.02),
    }


def reference(x_prompt, x_sample, state_conv, state_ssm_re, state_ssm_im, norm1_g, w_in, conv_w,
              conv_b, ssm_lam_re, ssm_lam_im, ssm_log_dt, ssm_b_re, ssm_b_im, ssm_c_re, ssm_c_im,
              ssm_d, w_glu, b_glu, w_branch, w_out, norm2_g, w_gate_up, w_down, final_g):
    y_prompt, prompt_conv, prompt_ssm_re, prompt_ssm_im = _trunk(
        x_prompt, None, None, None, norm1_g, w_in, conv_w, conv_b, ssm_lam_re, ssm_lam_im,
        ssm_log_dt, ssm_b_re, ssm_b_im, ssm_c_re, ssm_c_im, ssm_d, w_glu, b_glu, w_branch,
        w_out, norm2_g, w_gate_up, w_down, final_g)
    y_sample, sample_conv, sample_ssm_re, sample_ssm_im = _trunk(
        x_sample, state_conv, state_ssm_re, state_ssm_im, norm1_g, w_in, conv_w, conv_b,
        ssm_lam_re, ssm_lam_im, ssm_log_dt, ssm_b_re, ssm_b_im, ssm_c_re, ssm_c_im, ssm_d,
        w_glu, b_glu, w_branch, w_out, norm2_g, w_gate_up, w_down, final_g)
    return (y_prompt, y_sample, prompt_conv, prompt_ssm_re, prompt_ssm_im,
            sample_conv, sample_ssm_re, sample_ssm_im)
```

```python
import math
from contextlib import ExitStack
import numpy as np
import concourse.bass as bass
import concourse.mybir as mybir
from concourse.bass_utils import run_bass_kernel_spmd

F32 = mybir.dt.float32
BF16 = mybir.dt.bfloat16
ALU = mybir.AluOpType
AF = mybir.ActivationFunctionType

D = 1024
DEPTH = 4
SEQ = 2048
NS = 16
CW = 512
DFF = 2816
T = 8
NC2 = 1040
EPS = 1e-6
PI = math.pi
DBG = {"layers": DEPTH, "phases": None, "halves": 2}


class Op:
    __slots__ = ("eng", "fn", "deps", "needed", "val", "sem")


class Sched:
    ENGS = ("pe", "act", "dve", "pool", "sp")

    def __init__(self):
        self.prog = {e: [] for e in self.ENGS}
        self.last_w = {}
        self.readers = {}
        self.last_dma = {}

    def op(self, eng, fn, reads=(), writes=(), deps=(), dma_sem=None, mode=None):
        d = list(deps)
        forced = None
        if eng == "pe" and dma_sem is None:
            lp = getattr(self, "last_pe", None)
            if lp is not None and mode != self.last_pe_mode:
                forced = lp
        for k in reads:
            w = self.last_w.get(k)
            if w is not None:
                d.append(w)
        for k in writes:
            w = self.last_w.get(k)
            if w is not None:
                d.append(w)
            d.extend(self.readers.get(k, {}).values())
        if dma_sem is not None and dma_sem in self.last_dma:
            d.append(self.last_dma[dma_sem])
        o = Op()
        o.eng = eng
        o.fn = fn
        o.needed = False
        o.val = None
        o.sem = dma_sem
        dd = []
        seen = set()
        for x in d:
            if id(x) in seen:
                continue
            seen.add(id(x))
            if x.sem is None and x.eng == "pe" and eng == "pe" and dma_sem is None:
                continue
            x.needed = True
            dd.append(x)
        if forced is not None and id(forced) not in seen:
            forced.needed = True
            dd.append(forced)
        o.deps = dd
        if eng == "pe" and dma_sem is None:
            self.last_pe = o
            self.last_pe_mode = mode
        if dma_sem is not None:
            o.needed = True
            self.last_dma[dma_sem] = o
        rk = dma_sem if dma_sem is not None else eng
        for k in reads:
            self.readers.setdefault(k, {})[rk] = o
        for k in writes:
            self.last_w[k] = o
            self.readers[k] = {}
        self.prog[eng].append(o)
        return o

    def finalize(self):
        cnt = {e: 0 for e in self.ENGS}
        dcnt = {}
        for e in self.ENGS:
            for o in self.prog[e]:
                if o.sem is not None:
                    dcnt[o.sem] = dcnt.get(o.sem, 0) + 16
                    o.val = dcnt[o.sem]
                elif o.needed:
                    cnt[e] += 1
                    o.val = cnt[e]

    def runner(self, e, sems, dsems):
        def body(eng):
            waited = {}
            for o in self.prog[e]:
                for x in o.deps:
                    if x.sem is not None:
                        key = ("d", x.sem)
                        sh = dsems[x.sem]
                    else:
                        key = ("e", x.eng)
                        sh = sems[x.eng]
                    if waited.get(key, 0) >= x.val:
                        continue
                    waited[key] = x.val
                    eng.wait_ge(sh, x.val)
                ins = o.fn(eng)
                if o.sem is not None:
                    ins.then_inc(dsems[o.sem], 16)
                elif o.needed:
                    ins.then_inc(sems[e], 1)
        return body


def build():
    nc = bass.Bass("TRN2", target_bir_lowering=False)

    def din(name, shape):
        return nc.dram_tensor(name, list(shape), F32, kind="ExternalInput").ap()

    def dout(name, shape):
        return nc.dram_tensor(name, list(shape), F32, kind="ExternalOutput").ap()

    xp = din("xp", [SEQ, D])
    xs = din("xs", [NS, D])
    sconv_in = din("sconv", [DEPTH, NS, 2, CW])
    sre_in = din("sre", [DEPTH, NS, 2048])
    sim_in = din("sim", [DEPTH, NS, 2048])
    norm1_g = din("norm1_g", [DEPTH, D])
    w_in = din("w_in", [DEPTH, D, 4096])
    conv_w = din("conv_w", [DEPTH, 3, CW])
    conv_b = din("conv_b", [DEPTH, CW])
    lam_re = din("ssm_lam_re", [DEPTH, 32, 64])
    lam_im = din("ssm_lam_im", [DEPTH, 32, 64])
    log_dt = din("ssm_log_dt", [DEPTH, 32])
    b_re = din("ssm_b_re", [DEPTH, 32, 64, 16])
    b_im = din("ssm_b_im", [DEPTH, 32, 64, 16])
    c_re = din("ssm_c_re", [DEPTH, 32, 16, 64])
    c_im = din("ssm_c_im", [DEPTH, 32, 16, 64])
    ssm_d = din("ssm_d", [DEPTH, CW])
    w_glu = din("w_glu", [DEPTH, CW, CW])
    b_glu = din("b_glu", [DEPTH, CW])
    w_branch = din("w_branch", [DEPTH, D, D])
    w_out = din("w_out", [DEPTH, D, D])
    norm2_g = din("norm2_g", [DEPTH, D])
    w_gate_up = din("w_gate_up", [DEPTH, D, 2 * DFF])
    w_down = din("w_down", [DEPTH, DFF, D])
    final_g = din("final_g", [1, D])

    yp = dout("yp", [SEQ, D])
    ysm = dout("ysm", [NS, D])
    pconv_o = dout("pconv", [DEPTH, 2, CW])
    pre_o = dout("pre", [DEPTH, 16, 128])
    pim_o = dout("pim", [DEPTH, 16, 128])
    sconv_o = dout("sconv_o", [DEPTH, NS, 2, CW])
    sre_o = dout("sre_o", [DEPTH, NS, 2048])
    sim_o = dout("sim_o", [DEPTH, NS, 2048])

    es = ExitStack()

    def sb(name, shape, dt=F32):
        return es.enter_context(nc.sbuf_tensor(name, list(shape), dt))

    NSLOT = 5
    X = sb("X", [128, 8, NC2])
    H = sb("H", [128, 8, NC2], BF16)
    R = sb("R", [128, 16, NC2], BF16)
    RING = sb("RING", [128, NSLOT, 2048], BF16)
    NSCR = 8
    SCR = sb("SCR", [128, NSCR, 512])
    NSCB = 5
    SCB = sb("SCB", [128, NSCB, 512], BF16)
    GS = sb("GS", [128, 2, 16, 128])
    HBF = sb("HBF", [128, 2, 16, 129], BF16)
    TW = sb("TW", [128, 2, 16, 128])
    R8T = sb("R8T", [128, 16])
    TP = sb("TP", [128, 160])
    HC = sb("HC", [128, DEPTH, 2, 16])
    CONVST = sb("CONVST", [128, DEPTH, 2, 4])
    PT = sb("PT", [128, 256])
    LAMT = sb("LAMT", [128, 2, 64])
    LDT = sb("LDT", [128, 64])
    IDF = sb("IDF", [128, 128])
    IDB = sb("IDB", [128, 128], BF16)
    ONESB = sb("ONESB", [128, 128], BF16)
    MASK2 = sb("MASK2", [128, 2])
    MASK4 = sb("MASK4", [128, 4])
    MASKC = sb("MASKC", [128, 128])
    MV = sb("MV", [128, 9, 16])
    PWR = sb("PWR", [128, 2, 9, 16])
    BXF = sb("BXF", [128, 2, 16, 32])
    BXB = sb("BXB", [128, 2, 16, 32], BF16)
    CXF = sb("CXF", [128, 2, 4, 128])
    WBF = [sb(f"WBF{i}", [128, 8, 2, 128], BF16) for i in range(2)]
    CAF = [sb(f"CAF{i}", [128, 2, 9, 4, 32], BF16) for i in range(2)]
    KF = [sb(f"KF{i}", [128, 8, 128], BF16) for i in range(2)]
    SST = sb("SST", [128, 2, 16, NS])
    SSTB = sb("SSTB", [128, 2, 16, NS], BF16)
    SNEW = sb("SNEW", [128, 2, 16, NS])
    SCV = sb("SCV", [128, 2, 4, NS])
    CINS = sb("CINS", [128, 4, NS])
    PSUM = [es.enter_context(nc.psum_tensor(f"ps{i}", [128, 512], F32)) for i in range(8)]
    NLD = 4
    sems = {e: es.enter_context(nc.semaphore(f"s_{e}")) for e in Sched.ENGS}
    dsem_names = [f"w{i}" for i in range(NSLOT)] + [f"ld{i}" for i in range(NLD)] + [f"st{i}" for i in range(4)]
    dsems = {k: es.enter_context(nc.semaphore(f"d_{k}")) for k in dsem_names}

    S = Sched()
    PHN = {"p": "init"}
    st = {"bank": 0, "scr": 0, "scb": 0, "ld": 0, "st": 0}

    def bank():
        b = st["bank"]
        st["bank"] = (b + 1) % 8
        return b

    def scr(n=1):
        if st.get("ffn"):
            if st.get("in_pf"):
                i = st.get("scr_pf", 0)
                if i + n > 6:
                    i = 0
                st["scr_pf"] = (i + n) % 6
                return i
            assert n == 1
            i = st.get("scr_ffn", 6)
            st["scr_ffn"] = 13 - i
            return i
        i = st["scr"]
        if i + n > NSCR:
            i = 0
        st["scr"] = (i + n) % NSCR
        return i

    def scb():
        i = st["scb"]
        st["scb"] = (i + 1) % NSCB
        return i

    def ldsem():
        i = st["ld"]
        st["ld"] = (i + 1) % NLD
        return f"ld{i}"

    def stsem():
        i = st["st"]
        st["st"] = (i + 1) % 4
        return f"st{i}"

    def sk(i, n=1):
        if isinstance(i, str):
            return [i]
        return [("scr", i + t) for t in range(n)]

    def load(out_ap, in_ap, writes, nonc=False):
        return S.op("sp", lambda e: e.dma_start(out=out_ap, in_=in_ap, allow_slow_non_contiguous=nonc),
                    writes=writes, dma_sem=ldsem())

    def store(out_ap, in_ap, reads, nonc=False, writes=()):
        return S.op("sp", lambda e: e.dma_start(out=out_ap, in_=in_ap, allow_slow_non_contiguous=nonc),
                    reads=reads, writes=writes, dma_sem=stsem())

    STG = R[:, 12:16, :].rearrange("p a b -> p (a b)").bitcast(F32)
    STGK = [("R", t_, c_) for t_ in range(12, 16) for c_ in (0, 512, 1024)]
    STGB = ["STGB"]
    STGC = ["STGC"]

    def pmode(lhsT):
        sh = lhsT.shape
        k_ = sh[0]
        m_ = 1
        for x_ in sh[1:]:
            m_ *= x_
        r_ = lambda v: 32 if v <= 32 else (64 if v <= 64 else 128)
        return (r_(k_), r_(m_), str(lhsT.dtype))

    def mm(out, lhsT, rhs, start, stop, reads, writes, **kw):
        ph_ = PHN["p"]
        return S.op("pe", lambda e: e.matmul(out, lhsT=lhsT, rhs=rhs, start=start, stop=stop, **kw).annotate(ph_),
                    reads=reads, writes=writes, mode=pmode(lhsT))

    def dve(fn, reads, writes):
        return S.op("dve", fn, reads=reads, writes=writes)

    def act(fn, reads, writes):
        return S.op("act", fn, reads=reads, writes=writes)

    def tt(out, a, b, op, reads, writes, eng="dve"):
        return S.op(eng, lambda e: e.tensor_tensor(out=out, in0=a, in1=b, op=op), reads=reads, writes=writes)

    def bc(ap, dims):
        return bass.AP(ap.tensor, ap.offset, [list(ap.ap[0])] + [list(x) for x in dims])

    S.op("pool", lambda e: e.memset(IDF[:], 0.0), writes=["IDF"])
    S.op("pool", lambda e: e.affine_select(out=IDF[:], in_=IDF[:], pattern=[[-1, 128]], compare_op=ALU.not_equal,
                                           fill=1.0, base=0, channel_multiplier=1), reads=["IDF"], writes=["IDF"])
    S.op("pool", lambda e: e.tensor_copy(IDB[:], IDF[:]), reads=["IDF"], writes=["IDB"])
    S.op("pool", lambda e: e.memset(ONESB[:], 1.0), writes=["ONESB"])
    S.op("dve", lambda e: e.tensor_reduce(out=MASK4[:], in_=IDF[:].rearrange("p (a b) -> p a b", b=32),
                                          axis=mybir.AxisListType.X, op=ALU.add), reads=["IDF"], writes=["MASK4"])
    S.op("dve", lambda e: e.tensor_reduce(out=MASK2[:], in_=IDF[:].rearrange("p (a b) -> p a b", b=64),
                                          axis=mybir.AxisListType.X, op=ALU.add), reads=["IDF"], writes=["MASK2"])
    for g2 in range(2):
        S.op("dve", lambda e, g2=g2: e.tensor_copy(
            MASKC[:].rearrange("p (a b c) -> p a b c", b=2, c=16)[:, :, g2, :],
            bc(MASK2[:, g2:g2 + 1], [[0, 4], [0, 16]])), reads=["MASK2"], writes=["MASKC"])
    for m_ in range(9):
        S.op("pool", lambda e, m_=m_: e.memset(MV[:, m_, :], float(m_)), writes=["MV"])
    S.op("pool", lambda e: e.memset(HC[:], 0.0), writes=[("HCq", f_) for f_ in range(4)])
    S.op("pool", lambda e: e.memset(CONVST[:], 0.0), writes=["CONVST"])

    PROW = {"n1": 0, "n2": 32, "fg": 64, "cw": 72, "cb": 120, "d": 136, "bg": 152}
    i0 = scr(2)
    praw = SCR[:, i0:i0 + 2, :].rearrange("p a b -> p (a b)")
    S.op("pool", lambda e: e.memset(praw[:, 0:256], 0.0), writes=sk(i0, 2))
    load(praw[0:32, 0:128], norm1_g.rearrange("l (k p) -> (l k) p", p=128), sk(i0, 2))
    load(praw[32:64, 0:128], norm2_g.rearrange("l (k p) -> (l k) p", p=128), sk(i0, 2))
    load(praw[64:72, 0:128], final_g.rearrange("l (k p) -> (l k) p", p=128), sk(i0, 2))
    load(praw[72:120, 0:128], conv_w.rearrange("l k (f p) -> (l k f) p", p=128), sk(i0, 2))
    load(praw[120:128, 0:128], conv_b.rearrange("l (f p) -> (l f) p", p=128)[0:8], sk(i0, 2))
    load(praw[0:8, 128:256], conv_b.rearrange("l (f p) -> (l f) p", p=128)[8:16], sk(i0, 2))
    load(praw[8:24, 128:256], ssm_d.rearrange("l (f p) -> (l f) p", p=128), sk(i0, 2))
    load(praw[24:40, 128:256], b_glu.rearrange("l (f p) -> (l f) p", p=128), sk(i0, 2))
    for hf in range(2):
        b = bank()
        S.op("pe", lambda e, b=b, hf=hf: e.matmul(PSUM[b][:, 0:128], lhsT=praw[:, hf * 128:(hf + 1) * 128], rhs=IDF[:],
                                                  start=True, stop=True), reads=sk(i0, 2) + ["IDF"], writes=[("ps", b)], mode=pmode(praw[:, hf * 128:(hf + 1) * 128]))
        dve(lambda e, b=b, hf=hf: e.tensor_copy(PT[:, hf * 128:(hf + 1) * 128], PSUM[b][:, 0:128]), [("ps", b)], ["PT"])

    def pcol(name, idx):
        base = {"n1": 0, "n2": 32, "fg": 64, "cw": 72, "cb": 120, "d": 136, "bg": 152}[name]
        r = base + idx
        return PT[:, r:r + 1]

    i1 = scr(1)
    lraw = SCR[:, i1, :]
    load(lraw[0:64, 0:128], lam_re.rearrange("l (q g) p -> (l q) (g p)", g=2), sk(i1))
    load(lraw[0:64, 128:256], lam_im.rearrange("l (q g) p -> (l q) (g p)", g=2), sk(i1))
    load(lraw[0:2, 256:320].rearrange("p (l q) -> p l q", l=4), log_dt.rearrange("l (q g) -> g l q", g=2), sk(i1), nonc=True)
    for ri in range(2):
        b = bank()
        S.op("pe", lambda e, b=b, ri=ri: e.matmul(PSUM[b][:, 0:64], lhsT=lraw[0:64, ri * 128:(ri + 1) * 128], rhs=IDF[0:64, 0:64],
                                                  start=True, stop=True), reads=sk(i1) + ["IDF"], writes=[("ps", b)], mode=pmode(lraw[0:64, ri * 128:(ri + 1) * 128]))
        dve(lambda e, b=b, ri=ri: e.tensor_copy(LAMT[:, ri, :], PSUM[b][:, 0:64]), [("ps", b)], ["LAMT"])
    i2 = scr(1)
    sel = SCR[:, i2, :]
    b = bank()
    S.op("pe", lambda e, b=b: e.matmul(PSUM[b][0:2, 0:128], lhsT=MASK2[:], rhs=IDF[:], start=True, stop=True),
         reads=["MASK2", "IDF"], writes=[("ps", b)], mode=pmode(MASK2[:]))
    dve(lambda e, b=b: e.tensor_copy(sel[0:2, 0:128], PSUM[b][0:2, 0:128]), [("ps", b)], sk(i2))
    b = bank()
    S.op("pe", lambda e, b=b: e.matmul(PSUM[b][:, 0:64], lhsT=sel[0:2, 0:128], rhs=lraw[0:2, 256:320], start=True, stop=True),
         reads=sk(i1) + sk(i2), writes=[("ps", b)], mode=pmode(sel[0:2, 0:128]))
    dve(lambda e, b=b: e.tensor_copy(LDT[:], PSUM[b][:, 0:64]), [("ps", b)], ["LDT"])

    units = []

    def plan_layer(l):
        u = []
        def win(c0):
            return [(w_in[l, :, c0:c0 + 256], 8, 256)]
        for f0 in range(2):
            u.append((win(1536 + 256 * f0), "zu"))
        for f0 in range(2):
            u.append((win(512 + 256 * f0), "zc"))
            u.append((win(1024 + 256 * f0), "zv"))
            u.append((win(256 * f0), "zb"))
        u.append(([(w_glu[l, :, :], 4, 512)], "glu"))
        for jh in range(4):
            u.append((win(2048 + 256 * jh), "ga"))
            u.append((win(3072 + 256 * jh), "gs"))
            u.append(([(w_branch[l, :, 256 * jh:256 * jh + 256], 8, 256)], "wb"))
        for jh in range(4):
            u.append(([(w_out[l, :, 256 * jh:256 * jh + 256], 8, 256)], "wo"))
        for (t0, nt) in ((0, 12), (12, 10)):
            for pr in range(nt // 2):
                c0 = (t0 + 2 * pr) * 128
                u.append(([(w_gate_up[l, :, c0:c0 + 256], 8, 256)], "fg"))
                u.append(([(w_gate_up[l, :, DFF + c0:DFF + c0 + 256], 8, 256)], "fu"))
            for jq in range(4):
                k0 = t0
                rem = nt
                while rem > 0:
                    kk = min(8, rem)
                    u.append(([(w_down[l, k0 * 128:(k0 + kk) * 128, 256 * jq:256 * jq + 256], kk, 256)], "fd"))
                    k0 += kk
                    rem -= kk
        return u

    for half in range(2):
        for l in range(DEPTH):
            units.extend(plan_layer(l))
    ust = {"issued": 0, "next": 0}

    def issue_unit():
        i = ust["issued"]
        if i >= len(units):
            return
        pieces, tag = units[i]
        slot = i % NSLOT
        (ap, kt, ncols) = pieces[0]
        S.op("pool", lambda e, ap=ap, kt=kt, ncols=ncols, slot=slot: e.dma_start(
            out=RING[:, slot, 0:kt * ncols].rearrange("p (k c) -> p k c", k=kt),
            in_=ap.rearrange("(k p) c -> p k c", p=128)), writes=[("w", slot)], dma_sem=f"w{slot}")
        ust["issued"] = i + 1

    for _ in range(NSLOT):
        issue_unit()

    class WU:
        pass

    def next_unit(tag):
        i = ust["next"]
        pieces, t = units[i]
        assert t == tag, (t, tag, i)
        ust["next"] = i + 1
        w = WU()
        w.slot = i % NSLOT
        w.kt = pieces[0][1]
        w.nc = pieces[0][2]
        w.key = ("w", w.slot)
        w.v = RING[:, w.slot, 0:w.kt * w.nc].rearrange("p (k c) -> p k c", k=w.kt)
        return w

    def done_unit(w):
        issue_unit()

    def blocks(half):
        bl = [(0, 512, "p"), (512, 512, "p")]
        if half == 0:
            bl.append((1024, NS, "s"))
        return bl

    def proj(w, mt, src, srckey, c0, n, kts=None):
        b = bank()
        kts = list(range(w.kt)) if kts is None else kts
        for i, (kw, ks) in enumerate(kts if isinstance(kts[0], tuple) else [(k, k) for k in kts]):
            mm(PSUM[b][:, 0:n], w.v[:, kw, mt * 128:(mt + 1) * 128], src(ks)[:, c0:c0 + n], i == 0, i == len(kts) - 1,
               [w.key, srckey(ks, c0)], [("ps", b)])
        return b

    def bkey(name, t, c0):
        return (name, t, c0)

    def rmsnorm(half, gname, gidx0, out_h):
        for (c0, n, kind) in blocks(half):
            b = bank()
            for kt in range(8):
                j = scb()
                act(lambda e, kt=kt, j=j, c0=c0, n=n: e.activation(out=SCB[:, j, 0:n], in_=X[:, kt, c0:c0 + n], func=AF.Square),
                    [bkey("X", kt, c0)], [("scb", j)])
                mm(PSUM[b][:, 0:n], ONESB[:], SCB[:, j, 0:n], kt == 0, kt == 7, [("scb", j), "ONESB"], [("ps", b)])
            i = scr()
            act(lambda e, b=b, i=i, n=n: e.activation(out=SCR[:, i, 0:n], in_=PSUM[b][:, 0:n], func=AF.Ln, bias=EPSC[:, 0:1], scale=1.0 / D),
                [("ps", b), "EPSC"], sk(i))
            act(lambda e, i=i, n=n: e.activation(out=SCR[:, i, 0:n], in_=SCR[:, i, 0:n], func=AF.Exp, scale=-0.5), sk(i), sk(i))
            for kt in range(8):
                if out_h:
                    dve(lambda e, kt=kt, i=i, c0=c0, n=n: e.scalar_tensor_tensor(
                        out=H[:, kt, c0:c0 + n], in0=X[:, kt, c0:c0 + n], scalar=pcol(gname, gidx0 + kt), in1=SCR[:, i, 0:n],
                        op0=ALU.mult, op1=ALU.mult), [bkey("X", kt, c0), "PT"] + sk(i), [bkey("H", kt, c0)])
                else:
                    dve(lambda e, kt=kt, i=i, c0=c0, n=n: e.scalar_tensor_tensor(
                        out=X[:, kt, c0:c0 + n], in0=X[:, kt, c0:c0 + n], scalar=pcol(gname, gidx0 + kt), in1=SCR[:, i, 0:n],
                        op0=ALU.mult, op1=ALU.mult), [bkey("X", kt, c0), "PT"] + sk(i), [bkey("X", kt, c0)])

    EPSC = sb("EPSC", [128, 4])
    S.op("pool", lambda e: e.memset(EPSC[:, 0:1], EPS), writes=["EPSC"])
    S.op("pool", lambda e: e.memset(EPSC[:, 1:2], PI / 2.0), writes=["EPSC"])

    Hs = lambda k: H[:, k, :]
    Hk = lambda k, c0: bkey("H", k, c0)
    Rt = lambda base: (lambda k: R[:, base + k, :])
    Rk = lambda base: (lambda k, c0: bkey("R", base + k, c0))
    YA, YS, UT, MM_ = 0, 4, 8, 8

    def ssm_prep_layer_g(l):
        q0 = l * 16
        i = "TPk"
        t = TP[:, :]
        act(lambda e: e.activation(out=t[:, 0:16], in_=LDT[:, q0:q0 + 16], func=AF.Exp), ["LDT"], sk(i))
        yield
        tt(t[:, 16:32], LAMT[:, 0, q0:q0 + 16], t[:, 0:16], ALU.mult, sk(i) + ["LAMT"], sk(i))
        yield
        tt(t[:, 32:48], LAMT[:, 1, q0:q0 + 16], t[:, 0:16], ALU.mult, sk(i) + ["LAMT"], sk(i))
        yield
        i2 = scr(1)
        u = SCR[:, i2, :]
        k2 = sk(i2)
        dve(lambda e: e.tensor_tensor(out=u[:, 0:144].rearrange("p (m q) -> p m q", q=16), in0=MV[:], in1=bc(t[:, 16:32], [[0, 9], [1, 16]]), op=ALU.mult),
            sk(i) + ["MV"], k2)
        yield
        act(lambda e: e.activation(out=u[:, 0:144], in_=u[:, 0:144], func=AF.Exp), k2, k2)
        yield
        cA, sA, cB, sB, x1, x2 = (u[:, 144 + 16 * z:160 + 16 * z] for z in range(6))
        act(lambda e: e.activation(out=sA, in_=t[:, 32:48], func=AF.Sin, scale=1.0 / 16.0), sk(i), k2)
        yield
        act(lambda e: e.activation(out=cA, in_=t[:, 32:48], func=AF.Sin, bias=EPSC[:, 1:2], scale=1.0 / 16.0), sk(i) + ["EPSC"], k2)
        yield
        cur = (cA, sA)
        nxt = (cB, sB)
        for _sq in range(4):
            c_, s_ = cur
            tt(x1, c_, c_, ALU.mult, k2, k2)
            yield
            tt(x2, s_, s_, ALU.mult, k2, k2)
            yield
            tt(nxt[0], x1, x2, ALU.subtract, k2, k2)
            yield
            tt(x1, c_, s_, ALU.mult, k2, k2)
            yield
            dve(lambda e, o=nxt[1]: e.tensor_scalar(out=o, in0=x1, scalar1=2.0, scalar2=None, op0=ALU.mult), k2, k2)
            yield
            cur, nxt = nxt, cur
        i5 = scr(1)
        w_ = SCR[:, i5, :]
        k5 = sk(i5)
        ER = w_[:, 0:144].rearrange("p (m q) -> p m q", q=16)
        EI = w_[:, 144:288].rearrange("p (m q) -> p m q", q=16)
        y1 = w_[:, 288:352].rearrange("p (m q) -> p m q", q=16)
        y2 = w_[:, 352:416].rearrange("p (m q) -> p m q", q=16)
        S.op("dve", lambda e: e.memset(w_[:, 0:16], 1.0), writes=k5)
        S.op("dve", lambda e: e.memset(w_[:, 144:160], 0.0), writes=k5)
        dve(lambda e: e.tensor_copy(ER[:, 1, :], cur[0]), k2, k5)
        yield
        dve(lambda e: e.tensor_copy(EI[:, 1, :], cur[1]), k2, k5)
        yield

        def cmul_rng(dst0, src0, cnt, mul):
            ar_, ai_ = ER[:, src0:src0 + cnt, :], EI[:, src0:src0 + cnt, :]
            br_ = bc(ER[:, mul, :], [[0, cnt], [1, 16]])
            bi_ = bc(EI[:, mul, :], [[0, cnt], [1, 16]])
            z1, z2 = y1[:, 0:cnt, :], y2[:, 0:cnt, :]
            tt(z1, ar_, br_, ALU.mult, k5, k5)
            tt(z2, ai_, bi_, ALU.mult, k5, k5)
            tt(ER[:, dst0:dst0 + cnt, :], z1, z2, ALU.subtract, k5, k5)
            tt(z1, ar_, bi_, ALU.mult, k5, k5)
            tt(z2, ai_, br_, ALU.mult, k5, k5)
            tt(EI[:, dst0:dst0 + cnt, :], z1, z2, ALU.add, k5, k5)
        cmul_rng(2, 1, 1, 1)
        yield
        cmul_rng(3, 1, 2, 2)
        yield
        cmul_rng(5, 1, 4, 4)
        yield
        pw = PWR[:].rearrange("p r m q -> p r (m q)")
        tt(pw[:, 0, :], u[:, 0:144], w_[:, 0:144], ALU.mult, k2 + k5, ["PWR"])
        yield
        tt(pw[:, 1, :], u[:, 0:144], w_[:, 144:288], ALU.mult, k2 + k5, ["PWR"])
        yield
        dve(lambda e: e.tensor_copy(R8T[:], u[:, 128:144]), k2, ["R8T"])
        yield
        S.op("dve", lambda e: e.tensor_copy(TW[:, 0, :, 0:1], ER[:, 8, :].rearrange("p (q o) -> p q o", o=1)), reads=k5, writes=["TW"])
        S.op("dve", lambda e: e.tensor_copy(TW[:, 1, :, 0:1], EI[:, 8, :].rearrange("p (q o) -> p q o", o=1)), reads=k5, writes=["TW"])
        i6 = scr(4)
        tb = SCR[:, i6:i6 + 4, :].rearrange("p a b -> p (a b)")
        k6 = sk(i6, 4)
        n_ = 1
        while n_ < 128:
            z1 = tb[:, 0:16 * n_].rearrange("p (q k) -> p q k", q=16)
            z2 = tb[:, 1024:1024 + 16 * n_].rearrange("p (q k) -> p q k", q=16)
            ar_, ai_ = TW[:, 0, :, 0:n_], TW[:, 1, :, 0:n_]
            br_ = bc(TW[:, 0, :, n_ - 1:n_], [[128, 16], [0, n_]])
            bi_ = bc(TW[:, 1, :, n_ - 1:n_], [[128, 16], [0, n_]])
            tt(z1, ar_, br_, ALU.mult, ["TW"], k6)
            yield
            tt(z2, ai_, bi_, ALU.mult, ["TW"], k6)
            yield
            tt(TW[:, 0, :, n_:2 * n_], z1, z2, ALU.subtract, k6, ["TW"])
            yield
            tt(z1, ar_, bi_, ALU.mult, ["TW"], k6)
            yield
            tt(z2, ai_, br_, ALU.mult, ["TW"], k6)
            yield
            tt(TW[:, 1, :, n_:2 * n_], z1, z2, ALU.add, k6, ["TW"])
            yield
            n_ *= 2
        lr = LAMT[:, 0, q0:q0 + 16]
        li = LAMT[:, 1, q0:q0 + 16]
        v = t[:, 64:160]
        tt(t[:, 48:64], lr, lr, ALU.mult, ["LAMT"], sk(i))
        yield
        tt(v[:, 0:16], li, li, ALU.mult, ["LAMT"], sk(i))
        yield
        tt(t[:, 48:64], t[:, 48:64], v[:, 0:16], ALU.add, sk(i), sk(i))
        yield
        dve(lambda e: e.reciprocal(t[:, 48:64], t[:, 48:64]), sk(i), sk(i))
        yield
        dve(lambda e: e.tensor_scalar(out=v[:, 16:32], in0=PWR[:, 0, 1, :], scalar1=-1.0, scalar2=None, op0=ALU.add), ["PWR"], sk(i))
        yield
        tt(v[:, 32:48], v[:, 16:32], lr, ALU.mult, sk(i) + ["LAMT"], sk(i))
        yield
        tt(v[:, 48:64], PWR[:, 1, 1, :], li, ALU.mult, ["PWR", "LAMT"], sk(i))
        yield
        tt(v[:, 32:48], v[:, 32:48], v[:, 48:64], ALU.add, sk(i), sk(i))
        yield
        tt(v[:, 32:48], v[:, 32:48], t[:, 48:64], ALU.mult, sk(i), sk(i))
        yield
        tt(v[:, 64:80], PWR[:, 1, 1, :], lr, ALU.mult, ["PWR", "LAMT"], sk(i))
        yield
        tt(v[:, 80:96], v[:, 16:32], li, ALU.mult, sk(i) + ["LAMT"], sk(i))
        yield
        tt(v[:, 64:80], v[:, 64:80], v[:, 80:96], ALU.subtract, sk(i), sk(i))
        yield
        tt(v[:, 64:80], v[:, 64:80], t[:, 48:64], ALU.mult, sk(i), sk(i))
        yield
        fr = v[:, 32:48]
        fi = v[:, 64:80]
        i3 = scr(1)
        bb = STG
        BR = bb[:, 0:256].rearrange("p (q c) -> p q c", c=16)
        BI = bb[:, 256:512].rearrange("p (q c) -> p q c", c=16)
        frb = bc(fr, [[1, 16], [0, 16]])
        fib = bc(fi, [[1, 16], [0, 16]])
        w1 = SCR[:, i3, 0:256].rearrange("p (q c) -> p q c", c=16)
        w2 = SCR[:, i3, 256:512].rearrange("p (q c) -> p q c", c=16)
        kk = sk(i3) + sk(i) + STGB
        tt(w1, BR, frb, ALU.mult, kk, sk(i3))
        yield
        tt(w2, BI, fib, ALU.mult, kk, sk(i3))
        yield
        tt(w1, w1, w2, ALU.subtract, kk, sk(i3))
        yield
        tt(w2, BR, fib, ALU.mult, kk, sk(i3))
        yield
        tt(BR, BI, frb, ALU.mult, kk, STGB)
        yield
        tt(w2, w2, BR, ALU.add, kk, sk(i3))
        yield
        for ri, src in ((0, w1), (1, w2)):
            for g2 in range(2):
                dve(lambda e, ri=ri, src=src, g2=g2: e.tensor_scalar(
                    out=BXF[:, ri, :, g2 * 16:(g2 + 1) * 16], in0=src, scalar1=MASK2[:, g2:g2 + 1], scalar2=None, op0=ALU.mult),
                    sk(i3) + ["MASK2"], ["BXF"])
                yield
        dve(lambda e: e.tensor_copy(BXB[:], BXF[:]), ["BXF"], ["BXB"])
        yield
        cc = STG[:, 512:1536]
        for ri in range(2):
            for f in range(4):
                b = bank()
                S.op("pe", lambda e, b=b, ri=ri, f=f: e.matmul(PSUM[b][:, 0:128], lhsT=cc[:, ri * 512 + f * 128: ri * 512 + (f + 1) * 128],
                                                            rhs=IDF[:], start=True, stop=True), reads=STGK + STGC + ["IDF"], writes=[("ps", b)], mode=pmode(cc[:, ri * 512 + f * 128: ri * 512 + (f + 1) * 128]))
                tt(CXF[:, ri, f, :], PSUM[b][:, 0:128], MASKC[:], ALU.mult, [("ps", b), "MASKC"], ["CXF"])
                yield


    def ssm_prep_layer(l):
        for _ in ssm_prep_layer_g(l):
            pass

    WBPEND = {}

    def ssm_prep_wb_g(l, f, buf, part="all"):
        i = scr(4)
        big = SCR[:, i:i + 4, :].rearrange("p a b -> p (a b)")
        P1 = big[:, 0:1024].rearrange("p (m q c) -> p m q c", m=8, q=4)
        P2 = big[:, 1024:2048].rearrange("p (m q c) -> p m q c", m=8, q=4)
        keys = sk(i, 4)
        def apw(ri):
            a = PWR[:, ri, 0:8, 4 * f:4 * f + 4]
            return bc(a, [[16, 8], [1, 4], [0, 32]])
        def bx(ri):
            a = BXF[:, ri, 4 * f:4 * f + 4, :]
            return bc(a, [[0, 8], [32, 4], [1, 32]])
        jr = [scb(), scb()]
        ji = [scb(), scb()]
        tt(P1, apw(0), bx(0), ALU.mult, ["PWR", "BXF"], keys)
        yield
        tt(P2, apw(1), bx(1), ALU.mult, ["PWR", "BXF"], keys)
        yield
        for mh in range(2):
            dve(lambda e, mh=mh: e.tensor_tensor(out=SCB[:, jr[mh], :].rearrange("p (m q c) -> p m q c", m=4, q=4),
                                                 in0=P1[:, 4 * mh:4 * mh + 4], in1=P2[:, 4 * mh:4 * mh + 4], op=ALU.subtract),
                keys, [("scb", jr[mh])])
            yield
        tt(P1, apw(0), bx(1), ALU.mult, ["PWR", "BXF"] + [("scb", jr[0]), ("scb", jr[1])], keys)
        yield
        tt(P2, apw(1), bx(0), ALU.mult, ["PWR", "BXF"], keys)
        yield
        for mh in range(2):
            dve(lambda e, mh=mh: e.tensor_tensor(out=SCB[:, ji[mh], :].rearrange("p (m q c) -> p m q c", m=4, q=4),
                                                 in0=P1[:, 4 * mh:4 * mh + 4], in1=P2[:, 4 * mh:4 * mh + 4], op=ALU.add),
                keys, [("scb", ji[mh])])
            yield
        WBPEND[(f, buf)] = (jr, ji)
        if part == "all":
            yield from ssm_prep_wb_pe_g(l, f, buf)


    def ssm_prep_wb(l, f, buf, part="all"):
        for _ in ssm_prep_wb_g(l, f, buf, part):
            pass

    def ssm_prep_wb_pe_g(l, f, buf):
        jr, ji = WBPEND.pop((f, buf))
        for mh in range(2):
            for ri, jj in ((0, jr[mh]), (1, ji[mh])):
                b = bank()
                src = SCB[:, jj, :].rearrange("p (m q c) -> p m q c", m=4, q=4)
                for mm_ in range(4):
                    for rg in range(4):
                        S.op("pe", lambda e, b=b, src=src, mm_=mm_, rg=rg: e.matmul(
                            PSUM[b][32 * rg:32 * rg + 32, mm_ * 128:(mm_ + 1) * 128], lhsT=src[:, mm_, rg, :], rhs=IDB[:],
                            start=True, stop=True, tile_position=(0, 32 * rg)),
                            reads=[("scb", jj), "IDB"], writes=[("ps", b)], mode=pmode(src[:, mm_, rg, :]))
                act(lambda e, b=b, ri=ri, mh=mh: e.activation(out=WBF[buf][:, 4 * mh:4 * mh + 4, ri, :],
                                                             in_=PSUM[b][:, :].rearrange("p (m c) -> p m c", m=4), func=AF.Copy),
                    [("ps", b)], [("WBF", buf)])
                yield


    def ssm_prep_wb_pe(l, f, buf):
        for _ in ssm_prep_wb_pe_g(l, f, buf):
            pass

    def ssm_prep_ck_g(l, f, buf, part="all"):
        i = scr(6)
        big = SCR[:, i:i + 6, :].rearrange("p a b -> p (a b)")
        P1 = big[:, 0:1152].rearrange("p (m q c) -> p m q c", m=9, q=4)
        P2 = big[:, 1536:2688].rearrange("p (m q c) -> p m q c", m=9, q=4)
        keys = sk(i, 6)
        def apw(ri):
            return bc(PWR[:, ri, :, 4 * f:4 * f + 4], [[16, 9], [1, 4], [0, 32]])
        def cx(ri):
            return bc(CXF[:, ri, f, :], [[0, 9], [32, 4], [1, 32]])
        tt(P1, apw(0), cx(0), ALU.mult, ["PWR", "CXF"], keys)
        yield
        tt(P2, apw(1), cx(1), ALU.mult, ["PWR", "CXF"], keys)
        yield
        tt(CAF[buf][:, 0], P1, P2, ALU.subtract, keys, [("CAF", buf)])
        yield
        tt(P1, apw(1), cx(0), ALU.mult, ["PWR", "CXF"], keys)
        yield
        tt(P2, apw(0), cx(1), ALU.mult, ["PWR", "CXF"], keys)
        yield
        tt(P1, P1, P2, ALU.add, keys, keys)
        yield
        dve(lambda e: e.tensor_scalar(out=CAF[buf][:, 1], in0=P1, scalar1=-1.0, scalar2=None, op0=ALU.mult), keys, [("CAF", buf)])
        yield
        if part == "all":
            yield from ssm_prep_ck_pe_g(l, f, buf)


    def ssm_prep_ck(l, f, buf, part="all"):
        for _ in ssm_prep_ck_g(l, f, buf, part):
            pass

    def ssm_prep_ck_pe_g(l, f, buf):
        b = bank()
        for rg in range(4):
            q = 4 * f + rg
            for ri in range(2):
                S.op("pe", lambda e, b=b, rg=rg, q=q, ri=ri: e.matmul(
                    PSUM[b][32 * rg:32 * rg + 32, 0:256], lhsT=BXB[:, ri, q, :], rhs=CAF[buf][:, ri, 0:8, rg, :],
                    start=(ri == 0), stop=(ri == 1), tile_position=(0, 32 * rg)),
                    reads=["BXB", ("CAF", buf)], writes=[("ps", b)], mode=pmode(BXB[:, ri, q, :]))
        psv = PSUM[b][:, 0:256]
        dve(lambda e, psv=psv: e.tensor_tensor(
            out=KF[buf][:].rearrange("p t (r c) -> p t r c", r=4),
            in0=bc(psv, [[32, 8], [0, 4], [1, 32]]), in1=bc(MASK4[:], [[0, 8], [1, 4], [0, 32]]), op=ALU.mult),
            [("ps", b), "MASK4"], [("KF", buf)])
        yield
        dve(lambda e: e.scalar_tensor_tensor(out=KF[buf][:, 0, :], in0=IDF[:], scalar=pcol("d", l * 4 + f), in1=KF[buf][:, 0, :],
                                             op0=ALU.mult, op1=ALU.add), ["IDF", "PT", ("KF", buf)], [("KF", buf)])
        yield


    def ssm_prep_ck_pe(l, f, buf):
        for _ in ssm_prep_ck_pe_g(l, f, buf):
            pass

    def sample_loads_g(l):
        for (src_d, ri) in ((sre_in, 0), (sim_in, 1)):
            i = scr(4)
            raw = SCR[:, i:i + 4, :].rearrange("p a b -> p (a b)")
            load(raw[0:NS, :], src_d[l], sk(i, 4))
            for qh in range(2):
                b = bank()
                for qq in range(8):
                    q = qh * 8 + qq
                    S.op("pe", lambda e, b=b, q=q, qq=qq, raw=raw: e.matmul(
                        PSUM[b][:, qq * NS:(qq + 1) * NS], lhsT=raw[0:NS, q * 128:(q + 1) * 128], rhs=IDF[0:NS, 0:NS],
                        start=True, stop=True), reads=sk(i, 4) + ["IDF"], writes=[("ps", b)], mode=pmode(raw[0:NS, q * 128:(q + 1) * 128]))
                dve(lambda e, b=b, ri=ri, qh=qh: e.tensor_copy(SST[:, ri, qh * 8:(qh + 1) * 8, :],
                                                             PSUM[b][:, 0:8 * NS].rearrange("p (q t) -> p q t", t=NS)),
                    [("ps", b)], ["SST"])
                yield
        dve(lambda e: e.tensor_copy(SSTB[:], SST[:]), ["SST"], ["SSTB"])
        yield
        i = scr(2)
        raw = SCR[:, i:i + 2, :].rearrange("p a b -> p (a b)")
        load(raw[0:NS, :], sconv_in[l].rearrange("t k c -> t (k c)"), sk(i, 2))
        b = bank()
        for kf in range(8):
            S.op("pe", lambda e, b=b, kf=kf, raw=raw: e.matmul(
                PSUM[b][:, kf * NS:(kf + 1) * NS], lhsT=raw[0:NS, kf * 128:(kf + 1) * 128], rhs=IDF[0:NS, 0:NS],
                start=True, stop=True), reads=sk(i, 2) + ["IDF"], writes=[("ps", b)], mode=pmode(raw[0:NS, kf * 128:(kf + 1) * 128]))
        dve(lambda e, b=b: e.tensor_copy(SCV[:].rearrange("p k f t -> p (k f t)"), PSUM[b][:, 0:8 * NS]), [("ps", b)], ["SCV"])
        yield
        store(sconv_o[l, :, 0, :], sconv_in[l, :, 1, :], [])


    def sample_loads(l):
        for _ in sample_loads_g(l):
            pass

    SEQ_LH = [(h_, l_) for h_ in range(2) for l_ in range(DEPTH)]

    def prefetch_loads(half, l):
        for ri_, bsrc in ((0, b_re), (1, b_im)):
            for qq_ in range(4):
                load(STG[:, ri_ * 256 + qq_ * 64: ri_ * 256 + (qq_ + 1) * 64].rearrange("p (q c) -> p q c", c=16),
                     bsrc[l].rearrange("(q g) p c -> (g p) q c", g=2)[:, 4 * qq_:4 * qq_ + 4, :], STGK + STGB, nonc=True)
        cc = STG[:, 512:1536]
        for ri, csrc in ((0, c_re), (1, c_im)):
            for dup in range(2):
                load(cc[:, ri * 512:(ri + 1) * 512].rearrange("p (f d x) -> p f d x", f=4, d=2)[:, :, dup, :],
                     csrc[l].rearrange("(f g) c p -> (g c) f p", f=4), STGK + STGC, nonc=True)

    def prefetch_gen(half, l):
        if half == 0:
            yield from sample_loads_g(l)
        yield from ssm_prep_layer_g(l)
        yield from ssm_prep_wb_g(l, 0, 0, "dve")
        yield from ssm_prep_ck_g(l, 0, 0, "dve")
        yield from ssm_prep_ck_g(l, 1, 1, "dve")
        for _ in range(6):
            yield
        yield from ssm_prep_wb_pe_g(l, 0, 0)
        yield from ssm_prep_ck_pe_g(l, 0, 0)
        yield from ssm_prep_ck_pe_g(l, 1, 1)
        yield from ssm_prep_wb_g(l, 1, 1, "dve")
        for _ in range(6):
            yield
        yield from ssm_prep_wb_pe_g(l, 1, 1)

    PF = {"g": None}

    def pf_start(half, l):
        PF["g"] = prefetch_gen(half, l)
        st["ffn"] = True

    def pf_step(n=1):
        g = PF["g"]
        if g is None:
            return
        st["in_pf"] = True
        for _ in range(n):
            try:
                next(g)
            except StopIteration:
                PF["g"] = None
                break
        st["in_pf"] = False

    def pf_finish():
        while PF["g"] is not None:
            pf_step(64)
        st["ffn"] = False

    def prefetch(half, l):
        pf_start(half, l)
        pf_finish()

    def layer(half, l):
        blks = blocks(half)
        has_s = (half == 0)
        PH = DBG["phases"]
        on = lambda name: (PH is None or name in PH)

        u_start = ust["next"]
        n_layer_units = len(units) // (2 * DEPTH)

        def bail(name):
            if DBG.get("stop") != name:
                return False
            while ust["next"] < u_start + n_layer_units:
                i_ = ust["next"]
                done_unit(next_unit(units[i_][1]))
            return True
        PHN["p"] = "after_samp"
        if bail("samp"):
            return
        rmsnorm(half, "n1", l * 8, True)
        PHN["p"] = "after_norm"
        if bail("norm"):
            return

        PHN["p"] = "after_prep"
        if bail("prep"):
            return
        for f0 in range(2):
            w = next_unit("zu")
            for mt in range(2):
                f = 2 * f0 + mt
                for (c0, n, kind) in blks:
                    b = proj(w, mt, Hs, Hk, c0, n)
                    act(lambda e, b=b, f=f, c0=c0, n=n: e.activation(out=R[:, UT + f, c0:c0 + n], in_=PSUM[b][:, 0:n], func=AF.Copy),
                        [("ps", b)], [bkey("R", UT + f, c0)])
            done_unit(w)
        PHN["p"] = "after_zu"
        if bail("zu"):
            return
        for f in range(4):
            buf = f % 2
            bs = [bank() for _ in range(4)]
            for s in range(T):
                m = T - 1 - s
                for rg in range(4):
                    for ri in range(2):
                        mm(PSUM[bs[rg]][:, ri * 128:(ri + 1) * 128], WBF[buf][32 * rg:32 * rg + 32, m, ri, :],
                           R[32 * rg:32 * rg + 32, UT + f, s:1024:T], (s == 0 and ri == 0), (s == T - 1),
                           [("WBF", buf), bkey("R", UT + f, 0), bkey("R", UT + f, 512)], [("ps", bs[rg])],
                           tile_position=(32 * rg, 0), skip_group_check=True)
            if has_s:
                bss = [bank() for _ in range(4)]
                for rg in range(4):
                    for ri in range(2):
                        mm(PSUM[bss[rg]][:, ri * NS:(ri + 1) * NS], WBF[buf][32 * rg:32 * rg + 32, 0, ri, :],
                           R[32 * rg:32 * rg + 32, UT + f, 1024:1024 + NS], (ri == 0), True,
                           [("WBF", buf), bkey("R", UT + f, 1024)], [("ps", bss[rg])],
                           tile_position=(32 * rg, 0), skip_group_check=True)
            for rg in range(4):
                q = 4 * f + rg
                act(lambda e, rg=rg, q=q, bs=bs: e.activation(out=GS[:, :, q, :],
                                                             in_=PSUM[bs[rg]][:, 0:256].rearrange("p (r k) -> p r k", r=2), func=AF.Copy),
                    [("ps", bs[rg])], [("GSq", f)])
                if has_s:
                    act(lambda e, rg=rg, q=q, bss=bss: e.activation(out=SNEW[:, :, q, :],
                                                                   in_=PSUM[bss[rg]][:, 0:2 * NS].rearrange("p (r t) -> p r t", r=2), func=AF.Copy),
                        [("ps", bss[rg])], ["SNEW"])
            if f + 2 < 4:
                ssm_prep_wb(l, f + 2, buf)

        if has_s:
            ar = bc(PWR[:, 0, 1, :], [[1, 16], [0, NS]])
            ai = bc(PWR[:, 1, 1, :], [[1, 16], [0, NS]])
            i = scr()
            t1 = SCR[:, i, 0:256].rearrange("p (q t) -> p q t", t=NS)
            t2 = SCR[:, i, 256:512].rearrange("p (q t) -> p q t", t=NS)
            kk = sk(i)
            tt(t1, SST[:, 0], ar, ALU.mult, ["SST", "PWR"], kk)
            tt(t2, SST[:, 1], ai, ALU.mult, ["SST", "PWR"], kk)
            tt(t1, t1, t2, ALU.subtract, kk, kk)
            tt(SNEW[:, 0], SNEW[:, 0], t1, ALU.add, kk + ["SNEW"], ["SNEW"])
            tt(t1, SST[:, 1], ar, ALU.mult, ["SST", "PWR"], kk)
            tt(t2, SST[:, 0], ai, ALU.mult, ["SST", "PWR"], kk)
            tt(t1, t1, t2, ALU.add, kk, kk)
            tt(SNEW[:, 1], SNEW[:, 1], t1, ALU.add, kk + ["SNEW"], ["SNEW"])
            for (dst, ri) in ((sre_o, 0), (sim_o, 1)):
                i = scr(4)
                stg = SCR[:, i:i + 4, :].rearrange("p a b -> p (a b)")
                for qh in range(4):
                    b = bank()
                    for qq in range(4):
                        q = qh * 4 + qq
                        S.op("pe", lambda e, b=b, q=q, qq=qq, ri=ri: e.matmul(PSUM[b][0:NS, qq * 128:(qq + 1) * 128], lhsT=SNEW[:, ri, q, :], rhs=IDF[:],
                                                                        start=True, stop=True), reads=["SNEW", "IDF"], writes=[("ps", b)], mode=pmode(SNEW[:, ri, q, :]))
                    dve(lambda e, b=b, qh=qh, stg=stg: e.tensor_copy(stg[0:NS, qh * 512:(qh + 1) * 512], PSUM[b][0:NS, :]), [("ps", b)], sk(i, 4))
                store(dst[l], stg[0:NS, :], sk(i, 4))

        def rot_q(qq, sign_in):
            Gr = GS[:, 0, 4 * qq:4 * qq + 4, :]
            Gi = GS[:, 1, 4 * qq:4 * qq + 4, :]
            Cc = TW[:, 0, 4 * qq:4 * qq + 4, :]
            Sn = TW[:, 1, 4 * qq:4 * qq + 4, :]
            j1, j2, j3, j4 = scr(), scr(), scr(), scr()
            v4 = lambda j_: SCR[:, j_, :].rearrange("p (q k) -> p q k", q=4)
            gk = ("GSq", qq)
            tt(v4(j1), Cc, Gr, ALU.mult, ["TW", gk], sk(j1))
            tt(v4(j2), Sn, Gi, ALU.mult, ["TW", gk], sk(j2))
            tt(v4(j3), Cc, Gi, ALU.mult, ["TW", gk], sk(j3))
            tt(v4(j4), Sn, Gr, ALU.mult, ["TW", gk], sk(j4))
            tt(Gr, v4(j1), v4(j2), ALU.add if sign_in else ALU.subtract, sk(j1) + sk(j2) + sk(j3) + sk(j4), [gk])
            tt(Gi, v4(j3), v4(j4), ALU.subtract if sign_in else ALU.add, sk(j3) + sk(j4) + sk(j1) + sk(j2), [gk])

        def scan_q(f):
            gk = ("GSq", f)
            rot_q(f, True)
            for ri in range(2):
                for q in range(4 * f, 4 * f + 4):
                    dve(lambda e, ri=ri, q=q: e.tensor_tensor_scan(out=GS[:, ri, q, :], data0=bc(R8T[:, q:q + 1], [[0, 128]]), data1=GS[:, ri, q, :],
                                                                   initial=HC[:, l, ri, q:q + 1], op0=ALU.mult, op1=ALU.add),
                        [gk, "R8T", ("HCq", f)], [gk])
            rot_q(f, False)
            qs = slice(4 * f, 4 * f + 4)
            dve(lambda e: e.tensor_copy(HBF[:, :, qs, 0:1], HC[:, l, :, qs].rearrange("p r (q o) -> p r q o", o=1)), [("HCq", f)], [("HBFq", f)])
            act(lambda e: e.activation(out=HBF[:, :, qs, 1:129], in_=GS[:, :, qs, :], func=AF.Copy), [gk], [("HBFq", f)])
            dve(lambda e: e.tensor_copy(HC[:, l, :, qs].rearrange("p r (q o) -> p r q o", o=1), GS[:, :, qs, 127:128]), [gk, ("HBFq", f)], [("HCq", f)])

        PHN["p"] = "after_b"
        if bail("b"):
            return
        for f0 in range(2):
            wc = next_unit("zc")
            wv = next_unit("zv")
            wb_ = next_unit("zb")
            for mt in range(2):
                f = 2 * f0 + mt
                w0, w1, w2, cb = pcol("cw", l * 12 + 0 * 4 + f), pcol("cw", l * 12 + 4 + f), pcol("cw", l * 12 + 8 + f), pcol("cb", l * 4 + f)
                prev_cin = None
                for (c0, n, kind) in blks:
                    bc_ = proj(wc, mt, Hs, Hk, c0, n)
                    bv = proj(wv, mt, Hs, Hk, c0, n)
                    bb_ = proj(wb_, mt, Hs, Hk, c0, n)
                    iv = scr()
                    act(lambda e, bv=bv, iv=iv, n=n: e.activation(out=SCR[:, iv, 0:n], in_=PSUM[bv][:, 0:n], func=AF.Copy), [("ps", bv)], sk(iv))
                    ia = scr()
                    if kind == "p":
                        ic = scr(2)
                        cin = SCR[:, ic:ic + 2, :].rearrange("p a b -> p (a b)")
                        ck = sk(ic, 2)
                        if prev_cin is None:
                            dve(lambda e, cin=cin, f=f: e.tensor_copy(cin[:, 0:2], CONVST[:, l, :, f]), ["CONVST"], ck)
                        else:
                            pc, pk = prev_cin
                            dve(lambda e, cin=cin, pc=pc: e.tensor_copy(cin[:, 0:2], pc[:, 512:514]), pk, ck)
                        tt(cin[:, 2:2 + n], PSUM[bc_][:, 0:n], SCR[:, iv, 0:n], ALU.mult, [("ps", bc_)] + sk(iv), ck)
                        acc = SCR[:, ia, 0:n]
                        dve(lambda e, acc=acc, cin=cin, n=n, w2=w2, cb=cb: e.tensor_scalar(out=acc, in0=cin[:, 2:2 + n], scalar1=w2, scalar2=cb, op0=ALU.mult, op1=ALU.add),
                            ck + ["PT"], sk(ia))
                        dve(lambda e, acc=acc, cin=cin, n=n, w1=w1: e.scalar_tensor_tensor(out=acc, in0=cin[:, 1:1 + n], scalar=w1, in1=acc, op0=ALU.mult, op1=ALU.add),
                            ck + sk(ia) + ["PT"], sk(ia))
                        dve(lambda e, acc=acc, cin=cin, n=n, w0=w0: e.scalar_tensor_tensor(out=acc, in0=cin[:, 0:n], scalar=w0, in1=acc, op0=ALU.mult, op1=ALU.add),
                            ck + sk(ia) + ["PT"], sk(ia))
                        tt(R[:, YA + f, c0:c0 + n], acc, PSUM[bb_][:, 0:n], ALU.mult, sk(ia) + [("ps", bb_)], [bkey("R", YA + f, c0)])
                        prev_cin = (cin, ck)
                        if c0 == 512:
                            dve(lambda e, cin=cin, f=f: e.tensor_copy(CONVST[:, l, :, f], cin[:, 512:514]), ck, ["CONVST"])
                    else:
                        tt(CINS[:, f, :], PSUM[bc_][:, 0:n], SCR[:, iv, 0:n], ALU.mult, [("ps", bc_)] + sk(iv), ["CINS"])
                        acc = SCR[:, ia, 0:n]
                        dve(lambda e, acc=acc, f=f, w2=w2, cb=cb: e.tensor_scalar(out=acc, in0=CINS[:, f, :], scalar1=w2, scalar2=cb, op0=ALU.mult, op1=ALU.add),
                            ["CINS", "PT"], sk(ia))
                        dve(lambda e, acc=acc, f=f, w1=w1: e.scalar_tensor_tensor(out=acc, in0=SCV[:, 1, f, :], scalar=w1, in1=acc, op0=ALU.mult, op1=ALU.add),
                            ["SCV", "PT"] + sk(ia), sk(ia))
                        dve(lambda e, acc=acc, f=f, w0=w0: e.scalar_tensor_tensor(out=acc, in0=SCV[:, 0, f, :], scalar=w0, in1=acc, op0=ALU.mult, op1=ALU.add),
                            ["SCV", "PT"] + sk(ia), sk(ia))
                        tt(R[:, YA + f, c0:c0 + n], acc, PSUM[bb_][:, 0:n], ALU.mult, sk(ia) + [("ps", bb_)], [bkey("R", YA + f, c0)])
            done_unit(wc)
            done_unit(wv)
            done_unit(wb_)
            scan_q(f0)
        if has_s:
            b = bank()
            for f in range(4):
                S.op("pe", lambda e, b=b, f=f: e.matmul(PSUM[b][0:NS, f * 128:(f + 1) * 128], lhsT=CINS[:, f, :], rhs=IDF[:],
                                                      start=True, stop=True), reads=["CINS", "IDF"], writes=[("ps", b)], mode=pmode(CINS[:, f, :]))
            i = scr()
            dve(lambda e, b=b, i=i: e.tensor_copy(SCR[0:NS, i, :], PSUM[b][0:NS, :]), [("ps", b)], sk(i))
            store(sconv_o[l, :, 1, :], SCR[0:NS, i, :], sk(i))

        PHN["p"] = "after_conv"
        if bail("conv"):
            return
        PHN["p"] = "after_scan"
        if bail("scan"):
            return
        ybank = {}

        def part_a(f):
            buf = f % 2
            for (c0, n, kind) in blks:
                b = bank()
                ybank[(f, c0)] = b
                if kind == "p":
                    first = True
                    for j in range(T):
                        for tau in range(j + 1):
                            for rg in range(4):
                                mm(PSUM[b][32 * rg:32 * rg + 32, j:512:T], KF[buf][:, tau, 32 * rg:32 * rg + 32], R[:, UT + f, c0 + j - tau:c0 + 512:T],
                                   first, False, [("KF", buf), bkey("R", UT + f, c0)], [("ps", b)], tile_position=(0, 32 * rg), skip_group_check=True)
                            first = False
                else:
                    for rg in range(4):
                        mm(PSUM[b][32 * rg:32 * rg + 32, 0:n], KF[buf][:, 0, 32 * rg:32 * rg + 32], R[:, UT + f, c0:c0 + n], True, False,
                           [("KF", buf), bkey("R", UT + f, c0)], [("ps", b)], tile_position=(0, 32 * rg), skip_group_check=True)

        def part_d(f):
            buf = f % 2
            for (c0, n, kind) in blks:
                b = ybank[(f, c0)]
                if kind == "p":
                    kb = c0 // T
                    for j in range(T):
                        for rg in range(4):
                            for ri in range(2):
                                mm(PSUM[b][32 * rg:32 * rg + 32, j:512:T], CAF[buf][:, ri, j + 1, rg, :], HBF[:, ri, 4 * f + rg, kb:kb + 64],
                                   False, (j == T - 1 and rg == 3 and ri == 1), [("CAF", buf), ("HBFq", f)], [("ps", b)],
                                   tile_position=(0, 32 * rg), skip_group_check=True)
                else:
                    for rg in range(4):
                        for ri in range(2):
                            mm(PSUM[b][32 * rg:32 * rg + 32, 0:n], CAF[buf][:, ri, 1, rg, :], SSTB[:, ri, 4 * f + rg, :],
                               False, (rg == 3 and ri == 1), [("CAF", buf), "SSTB"], [("ps", b)], tile_position=(0, 32 * rg), skip_group_check=True)
            for (c0, n, kind) in blks:
                b = ybank[(f, c0)]
                act(lambda e, b=b, f=f, c0=c0, n=n: e.activation(out=R[:, YS + f, c0:c0 + n], in_=PSUM[b][:, 0:n], func=AF.Gelu),
                    [("ps", b)], [bkey("R", YS + f, c0)])

        part_a(0)
        for f in range(4):
            if f >= 2:
                scan_q(f)
            if f + 1 < 4:
                part_a(f + 1)
            part_d(f)
            if f + 2 < 4:
                ssm_prep_ck(l, f + 2, f % 2)
        PHN["p"] = "after_ad"
        if bail("ad"):
            return
        w = next_unit("glu")
        for mo in range(4):
            for (c0, n, kind) in blks:
                b = proj(w, mo, Rt(YS), Rk(YS), c0, n)
                i = scr()
                act(lambda e, b=b, i=i, n=n, mo=mo: e.activation(out=SCR[:, i, 0:n], in_=PSUM[b][:, 0:n], func=AF.Sigmoid, bias=pcol("bg", l * 4 + mo), scale=1.0),
                    [("ps", b), "PT"], sk(i))
                tt(R[:, UT + mo, c0:c0 + n], R[:, YS + mo, c0:c0 + n], SCR[:, i, 0:n], ALU.mult, [bkey("R", YS + mo, c0)] + sk(i), [bkey("R", UT + mo, c0)])
        done_unit(w)
        MT = [YS + 0, YS + 1, YS + 2, YS + 3, 12, 13, 14, 15]
        PHN["p"] = "after_glu"
        if bail("glu"):
            return
        for jh in range(4):
            wga = next_unit("ga")
            wgs = next_unit("gs")
            wbr = next_unit("wb")
            for mt in range(2):
                j = 2 * jh + mt
                for (c0, n, kind) in blks:
                    bga = proj(wga, mt, Hs, Hk, c0, n)
                    bgs = proj(wgs, mt, Hs, Hk, c0, n)
                    boa = proj(wbr, mt, Rt(YA), Rk(YA), c0, n, kts=[(k, k) for k in range(4)])
                    bob = proj(wbr, mt, Rt(UT), Rk(UT), c0, n, kts=[(4 + k, k) for k in range(4)])
                    ia, ib = scr(), scr()
                    act(lambda e, bga=bga, ia=ia, n=n: e.activation(out=SCR[:, ia, 0:n], in_=PSUM[bga][:, 0:n], func=AF.Sigmoid), [("ps", bga)], sk(ia))
                    act(lambda e, bgs=bgs, ib=ib, n=n: e.activation(out=SCR[:, ib, 0:n], in_=PSUM[bgs][:, 0:n], func=AF.Sigmoid), [("ps", bgs)], sk(ib))
                    tt(SCR[:, ia, 0:n], SCR[:, ia, 0:n], PSUM[boa][:, 0:n], ALU.mult, sk(ia) + [("ps", boa)], sk(ia))
                    tt(SCR[:, ib, 0:n], SCR[:, ib, 0:n], PSUM[bob][:, 0:n], ALU.mult, sk(ib) + [("ps", bob)], sk(ib))
                    tt(R[:, MT[j], c0:c0 + n], SCR[:, ia, 0:n], SCR[:, ib, 0:n], ALU.add, sk(ia) + sk(ib), [bkey("R", MT[j], c0)])
            done_unit(wga)
            done_unit(wgs)
            done_unit(wbr)
        Ms = lambda k: R[:, MT[k], :]
        Mk = lambda k, c0: bkey("R", MT[k], c0)
        PHN["p"] = "after_gates"
        if bail("gates"):
            return
        for jh in range(4):
            w = next_unit("wo")
            for mt in range(2):
                j = 2 * jh + mt
                for (c0, n, kind) in blks:
                    b = proj(w, mt, Ms, Mk, c0, n)
                    tt(X[:, j, c0:c0 + n], X[:, j, c0:c0 + n], PSUM[b][:, 0:n], ALU.add, [bkey("X", j, c0), ("ps", b)], [bkey("X", j, c0)])
            done_unit(w)
        PHN["p"] = "after_wout"
        if bail("wout"):
            return
        PHN["p"] = "ffn"
        idx_ = SEQ_LH.index((half, l))
        if idx_ + 1 < len(SEQ_LH) and SEQ_LH[idx_ + 1][1] < DBG["layers"]:
            prefetch_loads(*SEQ_LH[idx_ + 1])
        rmsnorm(half, "n2", l * 8, True)
        As = lambda k: R[:, k, :]
        Ak = lambda k, c0: bkey("R", k, c0)
        for (t0, nt) in ((0, 12), (12, 10)):
            for pr in range(nt // 2):
                if pr == 1 and t0 == 0:
                    idx0_ = SEQ_LH.index((half, l))
                    nx0_ = SEQ_LH[idx0_ + 1] if (idx0_ + 1 < len(SEQ_LH) and SEQ_LH[idx0_ + 1][1] < DBG["layers"]) else None
                    if nx0_ is not None:
                        pf_start(nx0_[0], nx0_[1])
                wg = next_unit("fg")
                wu = next_unit("fu")
                for mt in range(2):
                    ti = 2 * pr + mt
                    for (c0, n, kind) in blks:
                        bg = proj(wg, mt, Hs, Hk, c0, n)
                        bu = proj(wu, mt, Hs, Hk, c0, n)
                        i = scr()
                        act(lambda e, bg=bg, i=i, n=n: e.activation(out=SCR[:, i, 0:n], in_=PSUM[bg][:, 0:n], func=AF.Silu), [("ps", bg)], sk(i))
                        tt(R[:, ti, c0:c0 + n], SCR[:, i, 0:n], PSUM[bu][:, 0:n], ALU.mult, sk(i) + [("ps", bu)], [bkey("R", ti, c0)])
                        pf_step(PFN)
                done_unit(wg)
                done_unit(wu)
            idx_ = SEQ_LH.index((half, l))
            nxt_ = SEQ_LH[idx_ + 1] if (idx_ + 1 < len(SEQ_LH) and SEQ_LH[idx_ + 1][1] < DBG["layers"]) else None
            for jq in range(4):
                ws = []
                rem = nt
                while rem > 0:
                    ws.append(next_unit("fd"))
                    rem -= ws[-1].kt
                for mt in range(2):
                    j = 2 * jq + mt
                    for (c0, n, kind) in blks:
                        b = bank()
                        kidx = 0
                        for wi, w in enumerate(ws):
                            for kw in range(w.kt):
                                mm(PSUM[b][:, 0:n], w.v[:, kw, mt * 128:(mt + 1) * 128], R[:, kidx, c0:c0 + n], kidx == 0, kidx == nt - 1,
                                   [w.key, bkey("R", kidx, c0)], [("ps", b)])
                                kidx += 1
                        tt(X[:, j, c0:c0 + n], X[:, j, c0:c0 + n], PSUM[b][:, 0:n], ALU.add, [bkey("X", j, c0), ("ps", b)], [bkey("X", j, c0)])
                        pf_step(PFN)
                for w in ws:
                    done_unit(w)

        pf_finish()

    PFN = 1

    def DBG_skip_layer():
        n_layer_units = len(units) // (2 * DEPTH)
        for _ in range(n_layer_units):
            i_ = ust["next"]
            done_unit(next_unit(units[i_][1]))

    prefetch_loads(0, 0)
    prefetch(0, 0)
    for half in range(2):
        tiles = [(xp, half * 1024 + 128 * t, 128, 128 * t) for t in range(8)]
        if half == 0:
            tiles.append((xs, 0, NS, 1024))
        for (src, r0, nr, c0) in tiles:
            i = scr(2)
            stg = SCR[:, i:i + 2, :].rearrange("p a b -> p (a b)")
            load(stg[0:nr, :], src[r0:r0 + nr, :], sk(i, 2))
            for kh in range(2):
                b = bank()
                for kk in range(4):
                    kt = kh * 4 + kk
                    S.op("pe", lambda e, b=b, kk=kk, kt=kt, stg=stg, nr=nr: e.matmul(
                        PSUM[b][:, kk * 128:kk * 128 + nr], lhsT=stg[0:nr, kt * 128:(kt + 1) * 128], rhs=IDF[0:nr, 0:nr], start=True, stop=True),
                        reads=sk(i, 2) + ["IDF"], writes=[("ps", b)], mode=pmode(stg[0:nr, kt * 128:(kt + 1) * 128]))
                cb0 = (c0 // 512) * 512
                dve(lambda e, b=b, kh=kh, c0=c0, nr=nr: e.tensor_copy(X[:, kh * 4:kh * 4 + 4, c0:c0 + nr],
                                                                   PSUM[b][:, :].rearrange("p (k c) -> p k c", k=4)[:, :, 0:nr]),
                    [("ps", b)], [bkey("X", kh * 4 + kk, cb0) for kk in range(4)])
        for l in range(DEPTH):
            if l < DBG["layers"]:
                layer(half, l)
            else:
                DBG_skip_layer()
        rmsnorm(half, "fg", 0, False)
        for (dst, r0, nr, c0) in [(yp, half * 1024 + 128 * t, 128, 128 * t) for t in range(8)] + ([(ysm, 0, NS, 1024)] if half == 0 else []):
            i = scr(2)
            stg = SCR[:, i:i + 2, :].rearrange("p a b -> p (a b)")
            cb0 = (c0 // 512) * 512
            for kh in range(2):
                b = bank()
                for kk in range(4):
                    kt = kh * 4 + kk
                    S.op("pe", lambda e, b=b, kk=kk, kt=kt, c0=c0, nr=nr: e.matmul(
                        PSUM[b][0:nr, kk * 128:(kk + 1) * 128], lhsT=X[:, kt, c0:c0 + nr], rhs=IDF[:], start=True, stop=True),
                        reads=[bkey("X", kt, cb0), "IDF"], writes=[("ps", b)], mode=pmode(X[:, kt, c0:c0 + nr]))
                dve(lambda e, b=b, kh=kh, stg=stg, nr=nr: e.tensor_copy(stg[0:nr, kh * 512:(kh + 1) * 512], PSUM[b][0:nr, :]), [("ps", b)], sk(i, 2))
            store(dst[r0:r0 + nr, :], stg[0:nr, :], sk(i, 2))

    for l in range(DEPTH):
        b = bank()
        S.op("pe", lambda e, b=b, l=l: e.matmul(PSUM[b][0:32, 0:128], lhsT=HC[:, l].rearrange("p r q -> p (r q)"), rhs=IDF[:], start=True, stop=True),
             reads=[("HCq", f_) for f_ in range(4)] + ["IDF"], writes=[("ps", b)], mode=pmode(HC[:, l].rearrange("p r q -> p (r q)")))
        S.op("pe", lambda e, b=b, l=l: e.matmul(PSUM[b][0:8, 128:256], lhsT=CONVST[:, l].rearrange("p k f -> p (k f)"), rhs=IDF[:], start=True, stop=True),
             reads=["CONVST", "IDF"], writes=[("ps", b)], mode=pmode(CONVST[:, l].rearrange("p k f -> p (k f)")))
        i = scr()
        dve(lambda e, b=b, i=i: e.tensor_copy(SCR[0:32, i, 0:256], PSUM[b][0:32, 0:256]), [("ps", b)], sk(i))
        store(pre_o[l], SCR[0:16, i, 0:128], sk(i))
        store(pim_o[l], SCR[16:32, i, 0:128], sk(i))
        store(pconv_o[l].rearrange("k (f p) -> (k f) p", p=128), SCR[0:8, i, 128:256], sk(i))
    fin = [S.last_dma[f"st{i}"] for i in range(4) if f"st{i}" in S.last_dma]
    S.op("sp", lambda e: (e.engine_nop() if hasattr(e, "engine_nop") else None), deps=fin)
    assert ust["next"] == len(units), (ust, len(units))

    S.finalize()
    with nc.Block() as block:
        block.sync(S.runner("sp", sems, dsems))
        block.tensor(S.runner("pe", sems, dsems))
        block.scalar(S.runner("act", sems, dsems))
        block.vector(S.runner("dve", sems, dsems))
        block.gpsimd(S.runner("pool", sems, dsems))
    es.close()
    return nc


_NC = None


def kernel(**inp):
    global _NC
    if _NC is None:
        _NC = build()
    nc = _NC
    f = lambda a: np.ascontiguousarray(np.asarray(a, dtype=np.float32))
    shared = {k: f(inp[k]) for k in ("norm1_g", "w_in", "conv_w", "conv_b", "ssm_lam_re", "ssm_lam_im", "ssm_log_dt",
                                     "ssm_b_re", "ssm_b_im", "ssm_c_re", "ssm_c_im", "ssm_d", "w_glu", "b_glu",
                                     "w_branch", "w_out", "norm2_g", "w_gate_up", "w_down")}
    shared["final_g"] = f(inp["final_g"]).reshape(1, D)
    xpr = f(inp["x_prompt"])
    xsm = f(inp["x_sample"])
    sc = f(inp["state_conv"])
    sr = f(inp["state_ssm_re"])
    si = f(inp["state_ssm_im"])
    in_maps = []
    for c in range(8):
        m = dict(shared)
        m["xp"] = xpr[c]
        m["xs"] = np.ascontiguousarray(xsm[16 * c:16 * c + 16, 0, :])
        m["sconv"] = np.ascontiguousarray(sc[:, 16 * c:16 * c + 16])
        m["sre"] = np.ascontiguousarray(sr[:, 16 * c:16 * c + 16].reshape(DEPTH, NS, 2048))
        m["sim"] = np.ascontiguousarray(si[:, 16 * c:16 * c + 16].reshape(DEPTH, NS, 2048))
        in_maps.append(m)
    res = run_bass_kernel_spmd(nc, in_maps, core_ids=list(range(8)))
    rs = res.results
    y_prompt = np.stack([r["yp"] for r in rs], 0)
    y_sample = np.concatenate([r["ysm"] for r in rs], 0).reshape(128, 1, D)
    prompt_conv = np.stack([r["pconv"] for r in rs], 1)
    prompt_re = np.stack([r["pre"].reshape(DEPTH, 32, 64) for r in rs], 1)
    prompt_im = np.stack([r["pim"].reshape(DEPTH, 32, 64) for r in rs], 1)
    sample_conv = np.concatenate([r["sconv_o"] for r in rs], 1)
    sample_re = np.concatenate([r["sre_o"].reshape(DEPTH, NS, 32, 64) for r in rs], 1)
    sample_im = np.concatenate([r["sim_o"].reshape(DEPTH, NS, 32, 64) for r in rs], 1)
    return (y_prompt.astype(np.float32), y_sample.astype(np.float32), prompt_conv.astype(np.float32),
            prompt_re.astype(np.float32), prompt_im.astype(np.float32), sample_conv.astype(np.float32),
            sample_re.astype(np.float32), sample_im.astype(np.float32))
```

```python
import math
from contextlib import ExitStack
import numpy as np
import concourse.bass as bass
import concourse.mybir as mybir
from concourse.bass_utils import run_bass_kernel_spmd

F32 = mybir.dt.float32
BF16 = mybir.dt.bfloat16
ALU = mybir.AluOpType
AF = mybir.ActivationFunctionType

D = 1024
DEPTH = 4
SEQ = 2048
NS = 16
CW = 512
DFF = 2816
T = 8
NC2 = 1040
EPS = 1e-6
PI = math.pi
DBG = {"layers": DEPTH, "phases": None, "halves": 2}


class Op:
    __slots__ = ("eng", "fn", "deps", "needed", "val", "sem")


class Sched:
    ENGS = ("pe", "act", "dve", "pool", "sp")

    def __init__(self):
        self.prog = {e: [] for e in self.ENGS}
        self.last_w = {}
        self.readers = {}
        self.last_dma = {}

    def op(self, eng, fn, reads=(), writes=(), deps=(), dma_sem=None, mode=None):
        d = list(deps)
        forced = None
        if eng == "pe" and dma_sem is None:
            lp = getattr(self, "last_pe", None)
            if lp is not None and mode != self.last_pe_mode:
                forced = lp
        for k in reads:
            w = self.last_w.get(k)
            if w is not None:
                d.append(w)
        for k in writes:
            w = self.last_w.get(k)
            if w is not None:
                d.append(w)
            d.extend(self.readers.get(k, {}).values())
        if dma_sem is not None and dma_sem in self.last_dma:
            d.append(self.last_dma[dma_sem])
        o = Op()
        o.eng = eng
        o.fn = fn
        o.needed = False
        o.val = None
        o.sem = dma_sem
        dd = []
        seen = set()
        for x in d:
            if id(x) in seen:
                continue
            seen.add(id(x))
            if x.sem is None and x.eng == "pe" and eng == "pe" and dma_sem is None:
                continue
            x.needed = True
            dd.append(x)
        if forced is not None and id(forced) not in seen:
            forced.needed = True
            dd.append(forced)
        o.deps = dd
        if eng == "pe" and dma_sem is None:
            self.last_pe = o
            self.last_pe_mode = mode
        if dma_sem is not None:
            o.needed = True
            self.last_dma[dma_sem] = o
        rk = dma_sem if dma_sem is not None else eng
        for k in reads:
            self.readers.setdefault(k, {})[rk] = o
        for k in writes:
            self.last_w[k] = o
            self.readers[k] = {}
        self.prog[eng].append(o)
        return o

    def finalize(self):
        cnt = {e: 0 for e in self.ENGS}
        dcnt = {}
        for e in self.ENGS:
            for o in self.prog[e]:
                if o.sem is not None:
                    dcnt[o.sem] = dcnt.get(o.sem, 0) + 16
                    o.val = dcnt[o.sem]
                elif o.needed:
                    cnt[e] += 1
                    o.val = cnt[e]

    def runner(self, e, sems, dsems):
        def body(eng):
            waited = {}
            for o in self.prog[e]:
                for x in o.deps:
                    if x.sem is not None:
                        key = ("d", x.sem)
                        sh = dsems[x.sem]
                    else:
                        key = ("e", x.eng)
                        sh = sems[x.eng]
                    if waited.get(key, 0) >= x.val:
                        continue
                    waited[key] = x.val
                    eng.wait_ge(sh, x.val)
                ins = o.fn(eng)
                if o.sem is not None:
                    ins.then_inc(dsems[o.sem], 16)
                elif o.needed:
                    ins.then_inc(sems[e], 1)
        return body


def build():
    nc = bass.Bass("TRN2", target_bir_lowering=False)

    def din(name, shape):
        return nc.dram_tensor(name, list(shape), F32, kind="ExternalInput").ap()

    def dout(name, shape):
        return nc.dram_tensor(name, list(shape), F32, kind="ExternalOutput").ap()

    xp = din("xp", [SEQ, D])
    xs = din("xs", [NS, D])
    sconv_in = din("sconv", [DEPTH, NS, 2, CW])
    sre_in = din("sre", [DEPTH, NS, 2048])
    sim_in = din("sim", [DEPTH, NS, 2048])
    norm1_g = din("norm1_g", [DEPTH, D])
    w_in = din("w_in", [DEPTH, D, 4096])
    conv_w = din("conv_w", [DEPTH, 3, CW])
    conv_b = din("conv_b", [DEPTH, CW])
    lam_re = din("ssm_lam_re", [DEPTH, 32, 64])
    lam_im = din("ssm_lam_im", [DEPTH, 32, 64])
    log_dt = din("ssm_log_dt", [DEPTH, 32])
    b_re = din("ssm_b_re", [DEPTH, 32, 64, 16])
    b_im = din("ssm_b_im", [DEPTH, 32, 64, 16])
    c_re = din("ssm_c_re", [DEPTH, 32, 16, 64])
    c_im = din("ssm_c_im", [DEPTH, 32, 16, 64])
    ssm_d = din("ssm_d", [DEPTH, CW])
    w_glu = din("w_glu", [DEPTH, CW, CW])
    b_glu = din("b_glu", [DEPTH, CW])
    w_branch = din("w_branch", [DEPTH, D, D])
    w_out = din("w_out", [DEPTH, D, D])
    norm2_g = din("norm2_g", [DEPTH, D])
    w_gate_up = din("w_gate_up", [DEPTH, D, 2 * DFF])
    w_down = din("w_down", [DEPTH, DFF, D])
    final_g = din("final_g", [1, D])

    yp = dout("yp", [SEQ, D])
    ysm = dout("ysm", [NS, D])
    pconv_o = dout("pconv", [DEPTH, 2, CW])
    pre_o = dout("pre", [DEPTH, 16, 128])
    pim_o = dout("pim", [DEPTH, 16, 128])
    sconv_o = dout("sconv_o", [DEPTH, NS, 2, CW])
    sre_o = dout("sre_o", [DEPTH, NS, 2048])
    sim_o = dout("sim_o", [DEPTH, NS, 2048])

    es = ExitStack()

    def sb(name, shape, dt=F32):
        return es.enter_context(nc.sbuf_tensor(name, list(shape), dt))

    NSLOT = 5
    X = sb("X", [128, 8, NC2])
    H = sb("H", [128, 8, NC2], BF16)
    R = sb("R", [128, 16, NC2], BF16)
    RING = sb("RING", [128, NSLOT, 2048], BF16)
    NSCR = 8
    SCR = sb("SCR", [128, NSCR, 512])
    NSCB = 5
    SCB = sb("SCB", [128, NSCB, 512], BF16)
    GS = sb("GS", [128, 2, 16, 128])
    HBF = sb("HBF", [128, 2, 16, 129], BF16)
    TW = sb("TW", [128, 2, 16, 128])
    R8T = sb("R8T", [128, 16])
    TP = sb("TP", [128, 160])
    HC = sb("HC", [128, DEPTH, 2, 16])
    CONVST = sb("CONVST", [128, DEPTH, 2, 4])
    PT = sb("PT", [128, 256])
    LAMT = sb("LAMT", [128, 2, 64])
    LDT = sb("LDT", [128, 64])
    IDF = sb("IDF", [128, 128])
    IDB = sb("IDB", [128, 128], BF16)
    ONESB = sb("ONESB", [128, 128], BF16)
    MASK2 = sb("MASK2", [128, 2])
    MASK4 = sb("MASK4", [128, 4])
    MASKC = sb("MASKC", [128, 128])
    MV = sb("MV", [128, 9, 16])
    PWR = sb("PWR", [128, 2, 9, 16])
    BXF = sb("BXF", [128, 2, 16, 32])
    BXB = sb("BXB", [128, 2, 16, 32], BF16)
    CXF = sb("CXF", [128, 2, 4, 128])
    WBF = [sb(f"WBF{i}", [128, 8, 2, 128], BF16) for i in range(2)]
    CAF = [sb(f"CAF{i}", [128, 2, 9, 4, 32], BF16) for i in range(2)]
    KF = [sb(f"KF{i}", [128, 8, 128], BF16) for i in range(2)]
    SST = sb("SST", [128, 2, 16, NS])
    SSTB = sb("SSTB", [128, 2, 16, NS], BF16)
    SNEW = sb("SNEW", [128, 2, 16, NS])
    SCV = sb("SCV", [128, 2, 4, NS])
    CINS = sb("CINS", [128, 4, NS])
    PSUM = [es.enter_context(nc.psum_tensor(f"ps{i}", [128, 512], F32)) for i in range(8)]
    NLD = 4
    sems = {e: es.enter_context(nc.semaphore(f"s_{e}")) for e in Sched.ENGS}
    dsem_names = [f"w{i}" for i in range(NSLOT)] + [f"ld{i}" for i in range(NLD)] + [f"st{i}" for i in range(4)]
    dsems = {k: es.enter_context(nc.semaphore(f"d_{k}")) for k in dsem_names}

    S = Sched()
    PHN = {"p": "init"}
    st = {"bank": 0, "scr": 0, "scb": 0, "ld": 0, "st": 0}

    def bank():
        b = st["bank"]
        st["bank"] = (b + 1) % 8
        return b

    def scr(n=1):
        if st.get("ffn"):
            if st.get("in_pf"):
                i = st.get("scr_pf", 0)
                if i + n > 6:
                    i = 0
                st["scr_pf"] = (i + n) % 6
                return i
            assert n == 1
            i = st.get("scr_ffn", 6)
            st["scr_ffn"] = 13 - i
            return i
        i = st["scr"]
        if i + n > NSCR:
            i = 0
        st["scr"] = (i + n) % NSCR
        return i

    def scb():
        i = st["scb"]
        st["scb"] = (i + 1) % NSCB
        return i

    def ldsem():
        i = st["ld"]
        st["ld"] = (i + 1) % NLD
        return f"ld{i}"

    def stsem():
        i = st["st"]
        st["st"] = (i + 1) % 4
        return f"st{i}"

    def sk(i, n=1):
        if isinstance(i, str):
            return [i]
        return [("scr", i + t) for t in range(n)]

    def load(out_ap, in_ap, writes, nonc=False):
        return S.op("sp", lambda e: e.dma_start(out=out_ap, in_=in_ap, allow_slow_non_contiguous=nonc),
                    writes=writes, dma_sem=ldsem())

    def store(out_ap, in_ap, reads, nonc=False, writes=()):
        return S.op("sp", lambda e: e.dma_start(out=out_ap, in_=in_ap, allow_slow_non_contiguous=nonc),
                    reads=reads, writes=writes, dma_sem=stsem())

    STG = R[:, 12:16, :].rearrange("p a b -> p (a b)").bitcast(F32)
    STGK = [("R", t_, c_) for t_ in range(12, 16) for c_ in (0, 512, 1024)]
    STGB = ["STGB"]
    STGC = ["STGC"]

    def pmode(lhsT):
        sh = lhsT.shape
        k_ = sh[0]
        m_ = 1
        for x_ in sh[1:]:
            m_ *= x_
        r_ = lambda v: 32 if v <= 32 else (64 if v <= 64 else 128)
        return (r_(k_), r_(m_), str(lhsT.dtype))

    def mm(out, lhsT, rhs, start, stop, reads, writes, **kw):
        ph_ = PHN["p"]
        return S.op("pe", lambda e: e.matmul(out, lhsT=lhsT, rhs=rhs, start=start, stop=stop, **kw).annotate(ph_),
                    reads=reads, writes=writes, mode=pmode(lhsT))

    def dve(fn, reads, writes):
        return S.op("dve", fn, reads=reads, writes=writes)

    def act(fn, reads, writes):
        return S.op("act", fn, reads=reads, writes=writes)

    def tt(out, a, b, op, reads, writes, eng="dve"):
        return S.op(eng, lambda e: e.tensor_tensor(out=out, in0=a, in1=b, op=op), reads=reads, writes=writes)

    def bc(ap, dims):
        return bass.AP(ap.tensor, ap.offset, [list(ap.ap[0])] + [list(x) for x in dims])

    S.op("pool", lambda e: e.memset(IDF[:], 0.0), writes=["IDF"])
    S.op("pool", lambda e: e.affine_select(out=IDF[:], in_=IDF[:], pattern=[[-1, 128]], compare_op=ALU.not_equal,
                                           fill=1.0, base=0, channel_multiplier=1), reads=["IDF"], writes=["IDF"])
    S.op("pool", lambda e: e.tensor_copy(IDB[:], IDF[:]), reads=["IDF"], writes=["IDB"])
    S.op("pool", lambda e: e.memset(ONESB[:], 1.0), writes=["ONESB"])
    S.op("dve", lambda e: e.tensor_reduce(out=MASK4[:], in_=IDF[:].rearrange("p (a b) -> p a b", b=32),
                                          axis=mybir.AxisListType.X, op=ALU.add), reads=["IDF"], writes=["MASK4"])
    S.op("dve", lambda e: e.tensor_reduce(out=MASK2[:], in_=IDF[:].rearrange("p (a b) -> p a b", b=64),
                                          axis=mybir.AxisListType.X, op=ALU.add), reads=["IDF"], writes=["MASK2"])
    for g2 in range(2):
        S.op("dve", lambda e, g2=g2: e.tensor_copy(
            MASKC[:].rearrange("p (a b c) -> p a b c", b=2, c=16)[:, :, g2, :],
            bc(MASK2[:, g2:g2 + 1], [[0, 4], [0, 16]])), reads=["MASK2"], writes=["MASKC"])
    for m_ in range(9):
        S.op("pool", lambda e, m_=m_: e.memset(MV[:, m_, :], float(m_)), writes=["MV"])
    S.op("pool", lambda e: e.memset(HC[:], 0.0), writes=[("HCq", f_) for f_ in range(4)])
    S.op("pool", lambda e: e.memset(CONVST[:], 0.0), writes=["CONVST"])

    PROW = {"n1": 0, "n2": 32, "fg": 64, "cw": 72, "cb": 120, "d": 136, "bg": 152}
    i0 = scr(2)
    praw = SCR[:, i0:i0 + 2, :].rearrange("p a b -> p (a b)")
    S.op("pool", lambda e: e.memset(praw[:, 0:256], 0.0), writes=sk(i0, 2))
    load(praw[0:32, 0:128], norm1_g.rearrange("l (k p) -> (l k) p", p=128), sk(i0, 2))
    load(praw[32:64, 0:128], norm2_g.rearrange("l (k p) -> (l k) p", p=128), sk(i0, 2))
    load(praw[64:72, 0:128], final_g.rearrange("l (k p) -> (l k) p", p=128), sk(i0, 2))
    load(praw[72:120, 0:128], conv_w.rearrange("l k (f p) -> (l k f) p", p=128), sk(i0, 2))
    load(praw[120:128, 0:128], conv_b.rearrange("l (f p) -> (l f) p", p=128)[0:8], sk(i0, 2))
    load(praw[0:8, 128:256], conv_b.rearrange("l (f p) -> (l f) p", p=128)[8:16], sk(i0, 2))
    load(praw[8:24, 128:256], ssm_d.rearrange("l (f p) -> (l f) p", p=128), sk(i0, 2))
    load(praw[24:40, 128:256], b_glu.rearrange("l (f p) -> (l f) p", p=128), sk(i0, 2))
    for hf in range(2):
        b = bank()
        S.op("pe", lambda e, b=b, hf=hf: e.matmul(PSUM[b][:, 0:128], lhsT=praw[:, hf * 128:(hf + 1) * 128], rhs=IDF[:],
                                                  start=True, stop=True), reads=sk(i0, 2) + ["IDF"], writes=[("ps", b)], mode=pmode(praw[:, hf * 128:(hf + 1) * 128]))
        dve(lambda e, b=b, hf=hf: e.tensor_copy(PT[:, hf * 128:(hf + 1) * 128], PSUM[b][:, 0:128]), [("ps", b)], ["PT"])

    def pcol(name, idx):
        base = {"n1": 0, "n2": 32, "fg": 64, "cw": 72, "cb": 120, "d": 136, "bg": 152}[name]
        r = base + idx
        return PT[:, r:r + 1]

    i1 = scr(1)
    lraw = SCR[:, i1, :]
    load(lraw[0:64, 0:128], lam_re.rearrange("l (q g) p -> (l q) (g p)", g=2), sk(i1))
    load(lraw[0:64, 128:256], lam_im.rearrange("l (q g) p -> (l q) (g p)", g=2), sk(i1))
    load(lraw[0:2, 256:320].rearrange("p (l q) -> p l q", l=4), log_dt.rearrange("l (q g) -> g l q", g=2), sk(i1), nonc=True)
    for ri in range(2):
        b = bank()
        S.op("pe", lambda e, b=b, ri=ri: e.matmul(PSUM[b][:, 0:64], lhsT=lraw[0:64, ri * 128:(ri + 1) * 128], rhs=IDF[0:64, 0:64],
                                                  start=True, stop=True), reads=sk(i1) + ["IDF"], writes=[("ps", b)], mode=pmode(lraw[0:64, ri * 128:(ri + 1) * 128]))
        dve(lambda e, b=b, ri=ri: e.tensor_copy(LAMT[:, ri, :], PSUM[b][:, 0:64]), [("ps", b)], ["LAMT"])
    i2 = scr(1)
    sel = SCR[:, i2, :]
    b = bank()
    S.op("pe", lambda e, b=b: e.matmul(PSUM[b][0:2, 0:128], lhsT=MASK2[:], rhs=IDF[:], start=True, stop=True),
         reads=["MASK2", "IDF"], writes=[("ps", b)], mode=pmode(MASK2[:]))
    dve(lambda e, b=b: e.tensor_copy(sel[0:2, 0:128], PSUM[b][0:2, 0:128]), [("ps", b)], sk(i2))
    b = bank()
    S.op("pe", lambda e, b=b: e.matmul(PSUM[b][:, 0:64], lhsT=sel[0:2, 0:128], rhs=lraw[0:2, 256:320], start=True, stop=True),
         reads=sk(i1) + sk(i2), writes=[("ps", b)], mode=pmode(sel[0:2, 0:128]))
    dve(lambda e, b=b: e.tensor_copy(LDT[:], PSUM[b][:, 0:64]), [("ps", b)], ["LDT"])

    units = []

    def plan_layer(l):
        u = []
        def win(c0):
            return [(w_in[l, :, c0:c0 + 256], 8, 256)]
        for f0 in range(2):
            u.append((win(1536 + 256 * f0), "zu"))
        for f0 in range(2):
            u.append((win(512 + 256 * f0), "zc"))
            u.append((win(1024 + 256 * f0), "zv"))
            u.append((win(256 * f0), "zb"))
        u.append(([(w_glu[l, :, :], 4, 512)], "glu"))
        for jh in range(4):
            u.append((win(2048 + 256 * jh), "ga"))
            u.append((win(3072 + 256 * jh), "gs"))
            u.append(([(w_branch[l, :, 256 * jh:256 * jh + 256], 8, 256)], "wb"))
        for jh in range(4):
            u.append(([(w_out[l, :, 256 * jh:256 * jh + 256], 8, 256)], "wo"))
        for (t0, nt) in ((0, 12), (12, 10)):
            for pr in range(nt // 2):
                c0 = (t0 + 2 * pr) * 128
                u.append(([(w_gate_up[l, :, c0:c0 + 256], 8, 256)], "fg"))
                u.append(([(w_gate_up[l, :, DFF + c0:DFF + c0 + 256], 8, 256)], "fu"))
            for jq in range(4):
                k0 = t0
                rem = nt
                while rem > 0:
                    kk = min(8, rem)
                    u.append(([(w_down[l, k0 * 128:(k0 + kk) * 128, 256 * jq:256 * jq + 256], kk, 256)], "fd"))
                    k0 += kk
                    rem -= kk
        return u

    for half in range(2):
        for l in range(DEPTH):
            units.extend(plan_layer(l))
    ust = {"issued": 0, "next": 0}

    def issue_unit():
        i = ust["issued"]
        if i >= len(units):
            return
        pieces, tag = units[i]
        slot = i % NSLOT
        (ap, kt, ncols) = pieces[0]
        S.op("pool", lambda e, ap=ap, kt=kt, ncols=ncols, slot=slot: e.dma_start(
            out=RING[:, slot, 0:kt * ncols].rearrange("p (k c) -> p k c", k=kt),
            in_=ap.rearrange("(k p) c -> p k c", p=128)), writes=[("w", slot)], dma_sem=f"w{slot}")
        ust["issued"] = i + 1

    for _ in range(NSLOT):
        issue_unit()

    class WU:
        pass

    def next_unit(tag):
        i = ust["next"]
        pieces, t = units[i]
        assert t == tag, (t, tag, i)
        ust["next"] = i + 1
        w = WU()
        w.slot = i % NSLOT
        w.kt = pieces[0][1]
        w.nc = pieces[0][2]
        w.key = ("w", w.slot)
        w.v = RING[:, w.slot, 0:w.kt * w.nc].rearrange("p (k c) -> p k c", k=w.kt)
        return w

    def done_unit(w):
        issue_unit()

    def blocks(half):
        bl = [(0, 512, "p"), (512, 512, "p")]
        if half == 0:
            bl.append((1024, NS, "s"))
        return bl

    def proj(w, mt, src, srckey, c0, n, kts=None):
        b = bank()
        kts = list(range(w.kt)) if kts is None else kts
        for i, (kw, ks) in enumerate(kts if isinstance(kts[0], tuple) else [(k, k) for k in kts]):
            mm(PSUM[b][:, 0:n], w.v[:, kw, mt * 128:(mt + 1) * 128], src(ks)[:, c0:c0 + n], i == 0, i == len(kts) - 1,
               [w.key, srckey(ks, c0)], [("ps", b)])
        return b

    def bkey(name, t, c0):
        return (name, t, c0)

    def rmsnorm(half, gname, gidx0, out_h):
        for (c0, n, kind) in blocks(half):
            b = bank()
            for kt in range(8):
                j = scb()
                act(lambda e, kt=kt, j=j, c0=c0, n=n: e.activation(out=SCB[:, j, 0:n], in_=X[:, kt, c0:c0 + n], func=AF.Square),
                    [bkey("X", kt, c0)], [("scb", j)])
                mm(PSUM[b][:, 0:n], ONESB[:], SCB[:, j, 0:n], kt == 0, kt == 7, [("scb", j), "ONESB"], [("ps", b)])
            i = scr()
            act(lambda e, b=b, i=i, n=n: e.activation(out=SCR[:, i, 0:n], in_=PSUM[b][:, 0:n], func=AF.Ln, bias=EPSC[:, 0:1], scale=1.0 / D),
                [("ps", b), "EPSC"], sk(i))
            act(lambda e, i=i, n=n: e.activation(out=SCR[:, i, 0:n], in_=SCR[:, i, 0:n], func=AF.Exp, scale=-0.5), sk(i), sk(i))
            for kt in range(8):
                if out_h:
                    dve(lambda e, kt=kt, i=i, c0=c0, n=n: e.scalar_tensor_tensor(
                        out=H[:, kt, c0:c0 + n], in0=X[:, kt, c0:c0 + n], scalar=pcol(gname, gidx0 + kt), in1=SCR[:, i, 0:n],
                        op0=ALU.mult, op1=ALU.mult), [bkey("X", kt, c0), "PT"] + sk(i), [bkey("H", kt, c0)])
                else:
                    dve(lambda e, kt=kt, i=i, c0=c0, n=n: e.scalar_tensor_tensor(
                        out=X[:, kt, c0:c0 + n], in0=X[:, kt, c0:c0 + n], scalar=pcol(gname, gidx0 + kt), in1=SCR[:, i, 0:n],
                        op0=ALU.mult, op1=ALU.mult), [bkey("X", kt, c0), "PT"] + sk(i), [bkey("X", kt, c0)])

    EPSC = sb("EPSC", [128, 4])
    S.op("pool", lambda e: e.memset(EPSC[:, 0:1], EPS), writes=["EPSC"])
    S.op("pool", lambda e: e.memset(EPSC[:, 1:2], PI / 2.0), writes=["EPSC"])

    Hs = lambda k: H[:, k, :]
    Hk = lambda k, c0: bkey("H", k, c0)
    Rt = lambda base: (lambda k: R[:, base + k, :])
    Rk = lambda base: (lambda k, c0: bkey("R", base + k, c0))
    YA, YS, UT, MM_ = 0, 4, 8, 8

    def ssm_prep_layer_g(l):
        q0 = l * 16
        i = "TPk"
        t = TP[:, :]
        act(lambda e: e.activation(out=t[:, 0:16], in_=LDT[:, q0:q0 + 16], func=AF.Exp), ["LDT"], sk(i))
        yield
        tt(t[:, 16:32], LAMT[:, 0, q0:q0 + 16], t[:, 0:16], ALU.mult, sk(i) + ["LAMT"], sk(i))
        yield
        tt(t[:, 32:48], LAMT[:, 1, q0:q0 + 16], t[:, 0:16], ALU.mult, sk(i) + ["LAMT"], sk(i))
        yield
        i2 = scr(1)
        u = SCR[:, i2, :]
        k2 = sk(i2)
        dve(lambda e: e.tensor_tensor(out=u[:, 0:144].rearrange("p (m q) -> p m q", q=16), in0=MV[:], in1=bc(t[:, 16:32], [[0, 9], [1, 16]]), op=ALU.mult),
            sk(i) + ["MV"], k2)
        yield
        act(lambda e: e.activation(out=u[:, 0:144], in_=u[:, 0:144], func=AF.Exp), k2, k2)
        yield
        cA, sA, cB, sB, x1, x2 = (u[:, 144 + 16 * z:160 + 16 * z] for z in range(6))
        act(lambda e: e.activation(out=sA, in_=t[:, 32:48], func=AF.Sin, scale=1.0 / 16.0), sk(i), k2)
        yield
        act(lambda e: e.activation(out=cA, in_=t[:, 32:48], func=AF.Sin, bias=EPSC[:, 1:2], scale=1.0 / 16.0), sk(i) + ["EPSC"], k2)
        yield
        cur = (cA, sA)
        nxt = (cB, sB)
        for _sq in range(4):
            c_, s_ = cur
            tt(x1, c_, c_, ALU.mult, k2, k2)
            yield
            tt(x2, s_, s_, ALU.mult, k2, k2)
            yield
            tt(nxt[0], x1, x2, ALU.subtract, k2, k2)
            yield
            tt(x1, c_, s_, ALU.mult, k2, k2)
            yield
            dve(lambda e, o=nxt[1]: e.tensor_scalar(out=o, in0=x1, scalar1=2.0, scalar2=None, op0=ALU.mult), k2, k2)
            yield
            cur, nxt = nxt, cur
        i5 = scr(1)
        w_ = SCR[:, i5, :]
        k5 = sk(i5)
        ER = w_[:, 0:144].rearrange("p (m q) -> p m q", q=16)
        EI = w_[:, 144:288].rearrange("p (m q) -> p m q", q=16)
        y1 = w_[:, 288:352].rearrange("p (m q) -> p m q", q=16)
        y2 = w_[:, 352:416].rearrange("p (m q) -> p m q", q=16)
        S.op("dve", lambda e: e.memset(w_[:, 0:16], 1.0), writes=k5)
        S.op("dve", lambda e: e.memset(w_[:, 144:160], 0.0), writes=k5)
        dve(lambda e: e.tensor_copy(ER[:, 1, :], cur[0]), k2, k5)
        yield
        dve(lambda e: e.tensor_copy(EI[:, 1, :], cur[1]), k2, k5)
        yield

        def cmul_rng(dst0, src0, cnt, mul):
            ar_, ai_ = ER[:, src0:src0 + cnt, :], EI[:, src0:src0 + cnt, :]
            br_ = bc(ER[:, mul, :], [[0, cnt], [1, 16]])
            bi_ = bc(EI[:, mul, :], [[0, cnt], [1, 16]])
            z1, z2 = y1[:, 0:cnt, :], y2[:, 0:cnt, :]
            tt(z1, ar_, br_, ALU.mult, k5, k5)
            tt(z2, ai_, bi_, ALU.mult, k5, k5)
            tt(ER[:, dst0:dst0 + cnt, :], z1, z2, ALU.subtract, k5, k5)
            tt(z1, ar_, bi_, ALU.mult, k5, k5)
            tt(z2, ai_, br_, ALU.mult, k5, k5)
            tt(EI[:, dst0:dst0 + cnt, :], z1, z2, ALU.add, k5, k5)
        cmul_rng(2, 1, 1, 1)
        yield
        cmul_rng(3, 1, 2, 2)
        yield
        cmul_rng(5, 1, 4, 4)
        yield
        pw = PWR[:].rearrange("p r m q -> p r (m q)")
        tt(pw[:, 0, :], u[:, 0:144], w_[:, 0:144], ALU.mult, k2 + k5, ["PWR"])
        yield
        tt(pw[:, 1, :], u[:, 0:144], w_[:, 144:288], ALU.mult, k2 + k5, ["PWR"])
        yield
        dve(lambda e: e.tensor_copy(R8T[:], u[:, 128:144]), k2, ["R8T"])
        yield
        S.op("dve", lambda e: e.tensor_copy(TW[:, 0, :, 0:1], ER[:, 8, :].rearrange("p (q o) -> p q o", o=1)), reads=k5, writes=["TW"])
        S.op("dve", lambda e: e.tensor_copy(TW[:, 1, :, 0:1], EI[:, 8, :].rearrange("p (q o) -> p q o", o=1)), reads=k5, writes=["TW"])
        i6 = scr(4)
        tb = SCR[:, i6:i6 + 4, :].rearrange("p a b -> p (a b)")
        k6 = sk(i6, 4)
        n_ = 1
        while n_ < 128:
            z1 = tb[:, 0:16 * n_].rearrange("p (q k) -> p q k", q=16)
            z2 = tb[:, 1024:1024 + 16 * n_].rearrange("p (q k) -> p q k", q=16)
            ar_, ai_ = TW[:, 0, :, 0:n_], TW[:, 1, :, 0:n_]
            br_ = bc(TW[:, 0, :, n_ - 1:n_], [[128, 16], [0, n_]])
            bi_ = bc(TW[:, 1, :, n_ - 1:n_], [[128, 16], [0, n_]])
            tt(z1, ar_, br_, ALU.mult, ["TW"], k6)
            yield
            tt(z2, ai_, bi_, ALU.mult, ["TW"], k6)
            yield
            tt(TW[:, 0, :, n_:2 * n_], z1, z2, ALU.subtract, k6, ["TW"])
            yield
            tt(z1, ar_, bi_, ALU.mult, ["TW"], k6)
            yield
            tt(z2, ai_, br_, ALU.mult, ["TW"], k6)
            yield
            tt(TW[:, 1, :, n_:2 * n_], z1, z2, ALU.add, k6, ["TW"])
            yield
            n_ *= 2
        lr = LAMT[:, 0, q0:q0 + 16]
        li = LAMT[:, 1, q0:q0 + 16]
        v = t[:, 64:160]
        tt(t[:, 48:64], lr, lr, ALU.mult, ["LAMT"], sk(i))
        yield
        tt(v[:, 0:16], li, li, ALU.mult, ["LAMT"], sk(i))
        yield
        tt(t[:, 48:64], t[:, 48:64], v[:, 0:16], ALU.add, sk(i), sk(i))
        yield
        dve(lambda e: e.reciprocal(t[:, 48:64], t[:, 48:64]), sk(i), sk(i))
        yield
        dve(lambda e: e.tensor_scalar(out=v[:, 16:32], in0=PWR[:, 0, 1, :], scalar1=-1.0, scalar2=None, op0=ALU.add), ["PWR"], sk(i))
        yield
        tt(v[:, 32:48], v[:, 16:32], lr, ALU.mult, sk(i) + ["LAMT"], sk(i))
        yield
        tt(v[:, 48:64], PWR[:, 1, 1, :], li, ALU.mult, ["PWR", "LAMT"], sk(i))
        yield
        tt(v[:, 32:48], v[:, 32:48], v[:, 48:64], ALU.add, sk(i), sk(i))
        yield
        tt(v[:, 32:48], v[:, 32:48], t[:, 48:64], ALU.mult, sk(i), sk(i))
        yield
        tt(v[:, 64:80], PWR[:, 1, 1, :], lr, ALU.mult, ["PWR", "LAMT"], sk(i))
        yield
        tt(v[:, 80:96], v[:, 16:32], li, ALU.mult, sk(i) + ["LAMT"], sk(i))
        yield
        tt(v[:, 64:80], v[:, 64:80], v[:, 80:96], ALU.subtract, sk(i), sk(i))
        yield
        tt(v[:, 64:80], v[:, 64:80], t[:, 48:64], ALU.mult, sk(i), sk(i))
        yield
        fr = v[:, 32:48]
        fi = v[:, 64:80]
        i3 = scr(1)
        bb = STG
        BR = bb[:, 0:256].rearrange("p (q c) -> p q c", c=16)
        BI = bb[:, 256:512].rearrange("p (q c) -> p q c", c=16)
        frb = bc(fr, [[1, 16], [0, 16]])
        fib = bc(fi, [[1, 16], [0, 16]])
        w1 = SCR[:, i3, 0:256].rearrange("p (q c) -> p q c", c=16)
        w2 = SCR[:, i3, 256:512].rearrange("p (q c) -> p q c", c=16)
        kk = sk(i3) + sk(i) + STGB
        tt(w1, BR, frb, ALU.mult, kk, sk(i3))
        yield
        tt(w2, BI, fib, ALU.mult, kk, sk(i3))
        yield
        tt(w1, w1, w2, ALU.subtract, kk, sk(i3))
        yield
        tt(w2, BR, fib, ALU.mult, kk, sk(i3))
        yield
        tt(BR, BI, frb, ALU.mult, kk, STGB)
        yield
        tt(w2, w2, BR, ALU.add, kk, sk(i3))
        yield
        for ri, src in ((0, w1), (1, w2)):
            for g2 in range(2):
                dve(lambda e, ri=ri, src=src, g2=g2: e.tensor_scalar(
                    out=BXF[:, ri, :, g2 * 16:(g2 + 1) * 16], in0=src, scalar1=MASK2[:, g2:g2 + 1], scalar2=None, op0=ALU.mult),
                    sk(i3) + ["MASK2"], ["BXF"])
                yield
        dve(lambda e: e.tensor_copy(BXB[:], BXF[:]), ["BXF"], ["BXB"])
        yield
        cc = STG[:, 512:1536]
        for ri in range(2):
            for f in range(4):
                b = bank()
                S.op("pe", lambda e, b=b, ri=ri, f=f: e.matmul(PSUM[b][:, 0:128], lhsT=cc[:, ri * 512 + f * 128: ri * 512 + (f + 1) * 128],
                                                            rhs=IDF[:], start=True, stop=True), reads=STGK + STGC + ["IDF"], writes=[("ps", b)], mode=pmode(cc[:, ri * 512 + f * 128: ri * 512 + (f + 1) * 128]))
                tt(CXF[:, ri, f, :], PSUM[b][:, 0:128], MASKC[:], ALU.mult, [("ps", b), "MASKC"], ["CXF"])
                yield


    def ssm_prep_layer(l):
        for _ in ssm_prep_layer_g(l):
            pass

    WBPEND = {}

    def ssm_prep_wb_g(l, f, buf, part="all"):
        i = scr(4)
        big = SCR[:, i:i + 4, :].rearrange("p a b -> p (a b)")
        P1 = big[:, 0:1024].rearrange("p (m q c) -> p m q c", m=8, q=4)
        P2 = big[:, 1024:2048].rearrange("p (m q c) -> p m q c", m=8, q=4)
        keys = sk(i, 4)
        def apw(ri):
            a = PWR[:, ri, 0:8, 4 * f:4 * f + 4]
            return bc(a, [[16, 8], [1, 4], [0, 32]])
        def bx(ri):
            a = BXF[:, ri, 4 * f:4 * f + 4, :]
            return bc(a, [[0, 8], [32, 4], [1, 32]])
        jr = [scb(), scb()]
        ji = [scb(), scb()]
        tt(P1, apw(0), bx(0), ALU.mult, ["PWR", "BXF"], keys)
        yield
        tt(P2, apw(1), bx(1), ALU.mult, ["PWR", "BXF"], keys)
        yield
        for mh in range(2):
            dve(lambda e, mh=mh: e.tensor_tensor(out=SCB[:, jr[mh], :].rearrange("p (m q c) -> p m q c", m=4, q=4),
                                                 in0=P1[:, 4 * mh:4 * mh + 4], in1=P2[:, 4 * mh:4 * mh + 4], op=ALU.subtract),
                keys, [("scb", jr[mh])])
            yield
        tt(P1, apw(0), bx(1), ALU.mult, ["PWR", "BXF"] + [("scb", jr[0]), ("scb", jr[1])], keys)
        yield
        tt(P2, apw(1), bx(0), ALU.mult, ["PWR", "BXF"], keys)
        yield
        for mh in range(2):
            dve(lambda e, mh=mh: e.tensor_tensor(out=SCB[:, ji[mh], :].rearrange("p (m q c) -> p m q c", m=4, q=4),
                                                 in0=P1[:, 4 * mh:4 * mh + 4], in1=P2[:, 4 * mh:4 * mh + 4], op=ALU.add),
                keys, [("scb", ji[mh])])
            yield
        WBPEND[(f, buf)] = (jr, ji)
        if part == "all":
            yield from ssm_prep_wb_pe_g(l, f, buf)


    def ssm_prep_wb(l, f, buf, part="all"):
        for _ in ssm_prep_wb_g(l, f, buf, part):
            pass

    def ssm_prep_wb_pe_g(l, f, buf):
        jr, ji = WBPEND.pop((f, buf))
        for mh in range(2):
            for ri, jj in ((0, jr[mh]), (1, ji[mh])):
                b = bank()
                src = SCB[:, jj, :].rearrange("p (m q c) -> p m q c", m=4, q=4)
                for mm_ in range(4):
                    for rg in range(4):
                        S.op("pe", lambda e, b=b, src=src, mm_=mm_, rg=rg: e.matmul(
                            PSUM[b][32 * rg:32 * rg + 32, mm_ * 128:(mm_ + 1) * 128], lhsT=src[:, mm_, rg, :], rhs=IDB[:],
                            start=True, stop=True, tile_position=(0, 32 * rg)),
                            reads=[("scb", jj), "IDB"], writes=[("ps", b)], mode=pmode(src[:, mm_, rg, :]))
                act(lambda e, b=b, ri=ri, mh=mh: e.activation(out=WBF[buf][:, 4 * mh:4 * mh + 4, ri, :],
                                                             in_=PSUM[b][:, :].rearrange("p (m c) -> p m c", m=4), func=AF.Copy),
                    [("ps", b)], [("WBF", buf)])
                yield


    def ssm_prep_wb_pe(l, f, buf):
        for _ in ssm_prep_wb_pe_g(l, f, buf):
            pass

    def ssm_prep_ck_g(l, f, buf, part="all"):
        i = scr(6)
        big = SCR[:, i:i + 6, :].rearrange("p a b -> p (a b)")
        P1 = big[:, 0:1152].rearrange("p (m q c) -> p m q c", m=9, q=4)
        P2 = big[:, 1536:2688].rearrange("p (m q c) -> p m q c", m=9, q=4)
        keys = sk(i, 6)
        def apw(ri):
            return bc(PWR[:, ri, :, 4 * f:4 * f + 4], [[16, 9], [1, 4], [0, 32]])
        def cx(ri):
            return bc(CXF[:, ri, f, :], [[0, 9], [32, 4], [1, 32]])
        tt(P1, apw(0), cx(0), ALU.mult, ["PWR", "CXF"], keys)
        yield
        tt(P2, apw(1), cx(1), ALU.mult, ["PWR", "CXF"], keys)
        yield
        tt(CAF[buf][:, 0], P1, P2, ALU.subtract, keys, [("CAF", buf)])
        yield
        tt(P1, apw(1), cx(0), ALU.mult, ["PWR", "CXF"], keys)
        yield
        tt(P2, apw(0), cx(1), ALU.mult, ["PWR", "CXF"], keys)
        yield
        tt(P1, P1, P2, ALU.add, keys, keys)
        yield
        dve(lambda e: e.tensor_scalar(out=CAF[buf][:, 1], in0=P1, scalar1=-1.0, scalar2=None, op0=ALU.mult), keys, [("CAF", buf)])
        yield
        if part == "all":
            yield from ssm_prep_ck_pe_g(l, f, buf)


    def ssm_prep_ck(l, f, buf, part="all"):
        for _ in ssm_prep_ck_g(l, f, buf, part):
            pass

    def ssm_prep_ck_pe_g(l, f, buf):
        b = bank()
        for rg in range(4):
            q = 4 * f + rg
            for ri in range(2):
                S.op("pe", lambda e, b=b, rg=rg, q=q, ri=ri: e.matmul(
                    PSUM[b][32 * rg:32 * rg + 32, 0:256], lhsT=BXB[:, ri, q, :], rhs=CAF[buf][:, ri, 0:8, rg, :],
                    start=(ri == 0), stop=(ri == 1), tile_position=(0, 32 * rg)),
                    reads=["BXB", ("CAF", buf)], writes=[("ps", b)], mode=pmode(BXB[:, ri, q, :]))
        psv = PSUM[b][:, 0:256]
        dve(lambda e, psv=psv: e.tensor_tensor(
            out=KF[buf][:].rearrange("p t (r c) -> p t r c", r=4),
            in0=bc(psv, [[32, 8], [0, 4], [1, 32]]), in1=bc(MASK4[:], [[0, 8], [1, 4], [0, 32]]), op=ALU.mult),
            [("ps", b), "MASK4"], [("KF", buf)])
        yield
        dve(lambda e: e.scalar_tensor_tensor(out=KF[buf][:, 0, :], in0=IDF[:], scalar=pcol("d", l * 4 + f), in1=KF[buf][:, 0, :],
                                             op0=ALU.mult, op1=ALU.add), ["IDF", "PT", ("KF", buf)], [("KF", buf)])
        yield


    def ssm_prep_ck_pe(l, f, buf):
        for _ in ssm_prep_ck_pe_g(l, f, buf):
            pass

    def sample_loads_g(l):
        for (src_d, ri) in ((sre_in, 0), (sim_in, 1)):
            i = scr(4)
            raw = SCR[:, i:i + 4, :].rearrange("p a b -> p (a b)")
            load(raw[0:NS, :], src_d[l], sk(i, 4))
            for qh in range(2):
                b = bank()
                for qq in range(8):
                    q = qh * 8 + qq
                    S.op("pe", lambda e, b=b, q=q, qq=qq, raw=raw: e.matmul(
                        PSUM[b][:, qq * NS:(qq + 1) * NS], lhsT=raw[0:NS, q * 128:(q + 1) * 128], rhs=IDF[0:NS, 0:NS],
                        start=True, stop=True), reads=sk(i, 4) + ["IDF"], writes=[("ps", b)], mode=pmode(raw[0:NS, q * 128:(q + 1) * 128]))
                dve(lambda e, b=b, ri=ri, qh=qh: e.tensor_copy(SST[:, ri, qh * 8:(qh + 1) * 8, :],
                                                             PSUM[b][:, 0:8 * NS].rearrange("p (q t) -> p q t", t=NS)),
                    [("ps", b)], ["SST"])
                yield
        dve(lambda e: e.tensor_copy(SSTB[:], SST[:]), ["SST"], ["SSTB"])
        yield
        i = scr(2)
        raw = SCR[:, i:i + 2, :].rearrange("p a b -> p (a b)")
        load(raw[0:NS, :], sconv_in[l].rearrange("t k c -> t (k c)"), sk(i, 2))
        b = bank()
        for kf in range(8):
            S.op("pe", lambda e, b=b, kf=kf, raw=raw: e.matmul(
                PSUM[b][:, kf * NS:(kf + 1) * NS], lhsT=raw[0:NS, kf * 128:(kf + 1) * 128], rhs=IDF[0:NS, 0:NS],
                start=True, stop=True), reads=sk(i, 2) + ["IDF"], writes=[("ps", b)], mode=pmode(raw[0:NS, kf * 128:(kf + 1) * 128]))
        dve(lambda e, b=b: e.tensor_copy(SCV[:].rearrange("p k f t -> p (k f t)"), PSUM[b][:, 0:8 * NS]), [("ps", b)], ["SCV"])
        yield
        store(sconv_o[l, :, 0, :], sconv_in[l, :, 1, :], [])


    def sample_loads(l):
        for _ in sample_loads_g(l):
            pass

    SEQ_LH = [(h_, l_) for h_ in range(2) for l_ in range(DEPTH)]

    def prefetch_loads(half, l):
        for ri_, bsrc in ((0, b_re), (1, b_im)):
            for qq_ in range(4):
                load(STG[:, ri_ * 256 + qq_ * 64: ri_ * 256 + (qq_ + 1) * 64].rearrange("p (q c) -> p q c", c=16),
                     bsrc[l].rearrange("(q g) p c -> (g p) q c", g=2)[:, 4 * qq_:4 * qq_ + 4, :], STGK + STGB, nonc=True)
        cc = STG[:, 512:1536]
        for ri, csrc in ((0, c_re), (1, c_im)):
            for dup in range(2):
                load(cc[:, ri * 512:(ri + 1) * 512].rearrange("p (f d x) -> p f d x", f=4, d=2)[:, :, dup, :],
                     csrc[l].rearrange("(f g) c p -> (g c) f p", f=4), STGK + STGC, nonc=True)

    def prefetch_gen(half, l):
        if half == 0:
            yield from sample_loads_g(l)
        yield from ssm_prep_layer_g(l)
        yield from ssm_prep_wb_g(l, 0, 0, "dve")
        yield from ssm_prep_ck_g(l, 0, 0, "dve")
        yield from ssm_prep_ck_g(l, 1, 1, "dve")
        for _ in range(12):
            yield
        yield from ssm_prep_wb_pe_g(l, 0, 0)
        yield from ssm_prep_ck_pe_g(l, 0, 0)
        yield from ssm_prep_ck_pe_g(l, 1, 1)
        yield from ssm_prep_wb_g(l, 1, 1, "dve")
        for _ in range(12):
            yield
        yield from ssm_prep_wb_pe_g(l, 1, 1)

    PF = {"g": None}

    def pf_start(half, l):
        PF["g"] = prefetch_gen(half, l)
        st["ffn"] = True

    def pf_step(n=1):
        g = PF["g"]
        if g is None:
            return
        st["in_pf"] = True
        for _ in range(n):
            try:
                next(g)
            except StopIteration:
                PF["g"] = None
                break
        st["in_pf"] = False

    def pf_finish():
        while PF["g"] is not None:
            pf_step(64)
        st["ffn"] = False

    def prefetch(half, l):
        pf_start(half, l)
        pf_finish()

    def layer(half, l):
        blks = blocks(half)
        has_s = (half == 0)
        PH = DBG["phases"]
        on = lambda name: (PH is None or name in PH)

        u_start = ust["next"]
        n_layer_units = len(units) // (2 * DEPTH)

        def bail(name):
            if DBG.get("stop") != name:
                return False
            while ust["next"] < u_start + n_layer_units:
                i_ = ust["next"]
                done_unit(next_unit(units[i_][1]))
            return True
        PHN["p"] = "after_samp"
        if bail("samp"):
            return
        rmsnorm(half, "n1", l * 8, True)
        PHN["p"] = "after_norm"
        if bail("norm"):
            return

        PHN["p"] = "after_prep"
        if bail("prep"):
            return
        for f0 in range(2):
            w = next_unit("zu")
            for mt in range(2):
                f = 2 * f0 + mt
                for (c0, n, kind) in blks:
                    b = proj(w, mt, Hs, Hk, c0, n)
                    act(lambda e, b=b, f=f, c0=c0, n=n: e.activation(out=R[:, UT + f, c0:c0 + n], in_=PSUM[b][:, 0:n], func=AF.Copy),
                        [("ps", b)], [bkey("R", UT + f, c0)])
            done_unit(w)
        PHN["p"] = "after_zu"
        if bail("zu"):
            return
        for f in range(4):
            buf = f % 2
            bs = [bank() for _ in range(4)]
            for s in range(T):
                m = T - 1 - s
                for rg in range(4):
                    for ri in range(2):
                        mm(PSUM[bs[rg]][:, ri * 128:(ri + 1) * 128], WBF[buf][32 * rg:32 * rg + 32, m, ri, :],
                           R[32 * rg:32 * rg + 32, UT + f, s:1024:T], (s == 0 and ri == 0), (s == T - 1),
                           [("WBF", buf), bkey("R", UT + f, 0), bkey("R", UT + f, 512)], [("ps", bs[rg])],
                           tile_position=(32 * rg, 0), skip_group_check=True)
            if has_s:
                bss = [bank() for _ in range(4)]
                for rg in range(4):
                    for ri in range(2):
                        mm(PSUM[bss[rg]][:, ri * NS:(ri + 1) * NS], WBF[buf][32 * rg:32 * rg + 32, 0, ri, :],
                           R[32 * rg:32 * rg + 32, UT + f, 1024:1024 + NS], (ri == 0), True,
                           [("WBF", buf), bkey("R", UT + f, 1024)], [("ps", bss[rg])],
                           tile_position=(32 * rg, 0), skip_group_check=True)
            for rg in range(4):
                q = 4 * f + rg
                act(lambda e, rg=rg, q=q, bs=bs: e.activation(out=GS[:, :, q, :],
                                                             in_=PSUM[bs[rg]][:, 0:256].rearrange("p (r k) -> p r k", r=2), func=AF.Copy),
                    [("ps", bs[rg])], [("GSq", f)])
                if has_s:
                    act(lambda e, rg=rg, q=q, bss=bss: e.activation(out=SNEW[:, :, q, :],
                                                                   in_=PSUM[bss[rg]][:, 0:2 * NS].rearrange("p (r t) -> p r t", r=2), func=AF.Copy),
                        [("ps", bss[rg])], ["SNEW"])
            if f + 2 < 4:
                ssm_prep_wb(l, f + 2, buf)

        if has_s:
            ar = bc(PWR[:, 0, 1, :], [[1, 16], [0, NS]])
            ai = bc(PWR[:, 1, 1, :], [[1, 16], [0, NS]])
            i = scr()
            t1 = SCR[:, i, 0:256].rearrange("p (q t) -> p q t", t=NS)
            t2 = SCR[:, i, 256:512].rearrange("p (q t) -> p q t", t=NS)
            kk = sk(i)
            tt(t1, SST[:, 0], ar, ALU.mult, ["SST", "PWR"], kk)
            tt(t2, SST[:, 1], ai, ALU.mult, ["SST", "PWR"], kk)
            tt(t1, t1, t2, ALU.subtract, kk, kk)
            tt(SNEW[:, 0], SNEW[:, 0], t1, ALU.add, kk + ["SNEW"], ["SNEW"])
            tt(t1, SST[:, 1], ar, ALU.mult, ["SST", "PWR"], kk)
            tt(t2, SST[:, 0], ai, ALU.mult, ["SST", "PWR"], kk)
            tt(t1, t1, t2, ALU.add, kk, kk)
            tt(SNEW[:, 1], SNEW[:, 1], t1, ALU.add, kk + ["SNEW"], ["SNEW"])
            for (dst, ri) in ((sre_o, 0), (sim_o, 1)):
                i = scr(4)
                stg = SCR[:, i:i + 4, :].rearrange("p a b -> p (a b)")
                for qh in range(4):
                    b = bank()
                    for qq in range(4):
                        q = qh * 4 + qq
                        S.op("pe", lambda e, b=b, q=q, qq=qq, ri=ri: e.matmul(PSUM[b][0:NS, qq * 128:(qq + 1) * 128], lhsT=SNEW[:, ri, q, :], rhs=IDF[:],
                                                                        start=True, stop=True), reads=["SNEW", "IDF"], writes=[("ps", b)], mode=pmode(SNEW[:, ri, q, :]))
                    dve(lambda e, b=b, qh=qh, stg=stg: e.tensor_copy(stg[0:NS, qh * 512:(qh + 1) * 512], PSUM[b][0:NS, :]), [("ps", b)], sk(i, 4))
                store(dst[l], stg[0:NS, :], sk(i, 4))

        def rot_q(qq, sign_in):
            Gr = GS[:, 0, 4 * qq:4 * qq + 4, :]
            Gi = GS[:, 1, 4 * qq:4 * qq + 4, :]
            Cc = TW[:, 0, 4 * qq:4 * qq + 4, :]
            Sn = TW[:, 1, 4 * qq:4 * qq + 4, :]
            j1, j2, j3, j4 = scr(), scr(), scr(), scr()
            v4 = lambda j_: SCR[:, j_, :].rearrange("p (q k) -> p q k", q=4)
            gk = ("GSq", qq)
            tt(v4(j1), Cc, Gr, ALU.mult, ["TW", gk], sk(j1))
            tt(v4(j2), Sn, Gi, ALU.mult, ["TW", gk], sk(j2))
            tt(v4(j3), Cc, Gi, ALU.mult, ["TW", gk], sk(j3))
            tt(v4(j4), Sn, Gr, ALU.mult, ["TW", gk], sk(j4))
            tt(Gr, v4(j1), v4(j2), ALU.add if sign_in else ALU.subtract, sk(j1) + sk(j2) + sk(j3) + sk(j4), [gk])
            tt(Gi, v4(j3), v4(j4), ALU.subtract if sign_in else ALU.add, sk(j3) + sk(j4) + sk(j1) + sk(j2), [gk])

        def scan_q(f):
            gk = ("GSq", f)
            rot_q(f, True)
            for ri in range(2):
                for q in range(4 * f, 4 * f + 4):
                    dve(lambda e, ri=ri, q=q: e.tensor_tensor_scan(out=GS[:, ri, q, :], data0=bc(R8T[:, q:q + 1], [[0, 128]]), data1=GS[:, ri, q, :],
                                                                   initial=HC[:, l, ri, q:q + 1], op0=ALU.mult, op1=ALU.add),
                        [gk, "R8T", ("HCq", f)], [gk])
            rot_q(f, False)
            qs = slice(4 * f, 4 * f + 4)
            dve(lambda e: e.tensor_copy(HBF[:, :, qs, 0:1], HC[:, l, :, qs].rearrange("p r (q o) -> p r q o", o=1)), [("HCq", f)], [("HBFq", f)])
            act(lambda e: e.activation(out=HBF[:, :, qs, 1:129], in_=GS[:, :, qs, :], func=AF.Copy), [gk], [("HBFq", f)])
            dve(lambda e: e.tensor_copy(HC[:, l, :, qs].rearrange("p r (q o) -> p r q o", o=1), GS[:, :, qs, 127:128]), [gk, ("HBFq", f)], [("HCq", f)])

        PHN["p"] = "after_b"
        if bail("b"):
            return
        for f0 in range(2):
            wc = next_unit("zc")
            wv = next_unit("zv")
            wb_ = next_unit("zb")
            for mt in range(2):
                f = 2 * f0 + mt
                w0, w1, w2, cb = pcol("cw", l * 12 + 0 * 4 + f), pcol("cw", l * 12 + 4 + f), pcol("cw", l * 12 + 8 + f), pcol("cb", l * 4 + f)
                prev_cin = None
                for (c0, n, kind) in blks:
                    bc_ = proj(wc, mt, Hs, Hk, c0, n)
                    bv = proj(wv, mt, Hs, Hk, c0, n)
                    bb_ = proj(wb_, mt, Hs, Hk, c0, n)
                    iv = scr()
                    act(lambda e, bv=bv, iv=iv, n=n: e.activation(out=SCR[:, iv, 0:n], in_=PSUM[bv][:, 0:n], func=AF.Copy), [("ps", bv)], sk(iv))
                    ia = scr()
                    if kind == "p":
                        ic = scr(2)
                        cin = SCR[:, ic:ic + 2, :].rearrange("p a b -> p (a b)")
                        ck = sk(ic, 2)
                        if prev_cin is None:
                            dve(lambda e, cin=cin, f=f: e.tensor_copy(cin[:, 0:2], CONVST[:, l, :, f]), ["CONVST"], ck)
                        else:
                            pc, pk = prev_cin
                            dve(lambda e, cin=cin, pc=pc: e.tensor_copy(cin[:, 0:2], pc[:, 512:514]), pk, ck)
                        tt(cin[:, 2:2 + n], PSUM[bc_][:, 0:n], SCR[:, iv, 0:n], ALU.mult, [("ps", bc_)] + sk(iv), ck)
                        acc = SCR[:, ia, 0:n]
                        act(lambda e, acc=acc, cin=cin, n=n, w2=w2, cb=cb: e.activation(out=acc, in_=cin[:, 2:2 + n], func=AF.Identity, bias=cb, scale=w2),
                            ck + ["PT"], sk(ia))
                        dve(lambda e, acc=acc, cin=cin, n=n, w1=w1: e.scalar_tensor_tensor(out=acc, in0=cin[:, 1:1 + n], scalar=w1, in1=acc, op0=ALU.mult, op1=ALU.add),
                            ck + sk(ia) + ["PT"], sk(ia))
                        dve(lambda e, acc=acc, cin=cin, n=n, w0=w0: e.scalar_tensor_tensor(out=acc, in0=cin[:, 0:n], scalar=w0, in1=acc, op0=ALU.mult, op1=ALU.add),
                            ck + sk(ia) + ["PT"], sk(ia))
                        tt(R[:, YA + f, c0:c0 + n], acc, PSUM[bb_][:, 0:n], ALU.mult, sk(ia) + [("ps", bb_)], [bkey("R", YA + f, c0)])
                        prev_cin = (cin, ck)
                        if c0 == 512:
                            dve(lambda e, cin=cin, f=f: e.tensor_copy(CONVST[:, l, :, f], cin[:, 512:514]), ck, ["CONVST"])
                    else:
                        tt(CINS[:, f, :], PSUM[bc_][:, 0:n], SCR[:, iv, 0:n], ALU.mult, [("ps", bc_)] + sk(iv), ["CINS"])
                        acc = SCR[:, ia, 0:n]
                        dve(lambda e, acc=acc, f=f, w2=w2, cb=cb: e.tensor_scalar(out=acc, in0=CINS[:, f, :], scalar1=w2, scalar2=cb, op0=ALU.mult, op1=ALU.add),
                            ["CINS", "PT"], sk(ia))
                        dve(lambda e, acc=acc, f=f, w1=w1: e.scalar_tensor_tensor(out=acc, in0=SCV[:, 1, f, :], scalar=w1, in1=acc, op0=ALU.mult, op1=ALU.add),
                            ["SCV", "PT"] + sk(ia), sk(ia))
                        dve(lambda e, acc=acc, f=f, w0=w0: e.scalar_tensor_tensor(out=acc, in0=SCV[:, 0, f, :], scalar=w0, in1=acc, op0=ALU.mult, op1=ALU.add),
                            ["SCV", "PT"] + sk(ia), sk(ia))
                        tt(R[:, YA + f, c0:c0 + n], acc, PSUM[bb_][:, 0:n], ALU.mult, sk(ia) + [("ps", bb_)], [bkey("R", YA + f, c0)])
            done_unit(wc)
            done_unit(wv)
            done_unit(wb_)
            scan_q(f0)
        if has_s:
            b = bank()
            for f in range(4):
                S.op("pe", lambda e, b=b, f=f: e.matmul(PSUM[b][0:NS, f * 128:(f + 1) * 128], lhsT=CINS[:, f, :], rhs=IDF[:],
                                                      start=True, stop=True), reads=["CINS", "IDF"], writes=[("ps", b)], mode=pmode(CINS[:, f, :]))
            i = scr()
            dve(lambda e, b=b, i=i: e.tensor_copy(SCR[0:NS, i, :], PSUM[b][0:NS, :]), [("ps", b)], sk(i))
            store(sconv_o[l, :, 1, :], SCR[0:NS, i, :], sk(i))

        PHN["p"] = "after_conv"
        if bail("conv"):
            return
        PHN["p"] = "after_scan"
        if bail("scan"):
            return
        ybank = {}

        def part_a(f):
            buf = f % 2
            for (c0, n, kind) in blks:
                b = bank()
                ybank[(f, c0)] = b
                if kind == "p":
                    first = True
                    for j in range(T):
                        for tau in range(j + 1):
                            for rg in range(4):
                                mm(PSUM[b][32 * rg:32 * rg + 32, j:512:T], KF[buf][:, tau, 32 * rg:32 * rg + 32], R[:, UT + f, c0 + j - tau:c0 + 512:T],
                                   first, False, [("KF", buf), bkey("R", UT + f, c0)], [("ps", b)], tile_position=(0, 32 * rg), skip_group_check=True)
                            first = False
                else:
                    for rg in range(4):
                        mm(PSUM[b][32 * rg:32 * rg + 32, 0:n], KF[buf][:, 0, 32 * rg:32 * rg + 32], R[:, UT + f, c0:c0 + n], True, False,
                           [("KF", buf), bkey("R", UT + f, c0)], [("ps", b)], tile_position=(0, 32 * rg), skip_group_check=True)

        def part_d(f):
            buf = f % 2
            for (c0, n, kind) in blks:
                b = ybank[(f, c0)]
                if kind == "p":
                    kb = c0 // T
                    for j in range(T):
                        for rg in range(4):
                            for ri in range(2):
                                mm(PSUM[b][32 * rg:32 * rg + 32, j:512:T], CAF[buf][:, ri, j + 1, rg, :], HBF[:, ri, 4 * f + rg, kb:kb + 64],
                                   False, (j == T - 1 and rg == 3 and ri == 1), [("CAF", buf), ("HBFq", f)], [("ps", b)],
                                   tile_position=(0, 32 * rg), skip_group_check=True)
                else:
                    for rg in range(4):
                        for ri in range(2):
                            mm(PSUM[b][32 * rg:32 * rg + 32, 0:n], CAF[buf][:, ri, 1, rg, :], SSTB[:, ri, 4 * f + rg, :],
                               False, (rg == 3 and ri == 1), [("CAF", buf), "SSTB"], [("ps", b)], tile_position=(0, 32 * rg), skip_group_check=True)
            for (c0, n, kind) in blks:
                b = ybank[(f, c0)]
                act(lambda e, b=b, f=f, c0=c0, n=n: e.activation(out=R[:, YS + f, c0:c0 + n], in_=PSUM[b][:, 0:n], func=AF.Gelu),
                    [("ps", b)], [bkey("R", YS + f, c0)])

        part_a(0)
        for f in range(4):
            if f >= 2:
                scan_q(f)
            if f + 1 < 4:
                part_a(f + 1)
            part_d(f)
            if f + 2 < 4:
                ssm_prep_ck(l, f + 2, f % 2)
        PHN["p"] = "after_ad"
        if bail("ad"):
            return
        w = next_unit("glu")
        for mo in range(4):
            for (c0, n, kind) in blks:
                b = proj(w, mo, Rt(YS), Rk(YS), c0, n)
                i = scr()
                act(lambda e, b=b, i=i, n=n, mo=mo: e.activation(out=SCR[:, i, 0:n], in_=PSUM[b][:, 0:n], func=AF.Sigmoid, bias=pcol("bg", l * 4 + mo), scale=1.0),
                    [("ps", b), "PT"], sk(i))
                tt(R[:, UT + mo, c0:c0 + n], R[:, YS + mo, c0:c0 + n], SCR[:, i, 0:n], ALU.mult, [bkey("R", YS + mo, c0)] + sk(i), [bkey("R", UT + mo, c0)])
        done_unit(w)
        MT = [YS + 0, YS + 1, YS + 2, YS + 3, 12, 13, 14, 15]
        PHN["p"] = "after_glu"
        if bail("glu"):
            return
        for jh in range(4):
            wga = next_unit("ga")
            wgs = next_unit("gs")
            wbr = next_unit("wb")
            for mt in range(2):
                j = 2 * jh + mt
                for (c0, n, kind) in blks:
                    bga = proj(wga, mt, Hs, Hk, c0, n)
                    bgs = proj(wgs, mt, Hs, Hk, c0, n)
                    boa = proj(wbr, mt, Rt(YA), Rk(YA), c0, n, kts=[(k, k) for k in range(4)])
                    bob = proj(wbr, mt, Rt(UT), Rk(UT), c0, n, kts=[(4 + k, k) for k in range(4)])
                    ia, ib = scr(), scr()
                    act(lambda e, bga=bga, ia=ia, n=n: e.activation(out=SCR[:, ia, 0:n], in_=PSUM[bga][:, 0:n], func=AF.Sigmoid), [("ps", bga)], sk(ia))
                    act(lambda e, bgs=bgs, ib=ib, n=n: e.activation(out=SCR[:, ib, 0:n], in_=PSUM[bgs][:, 0:n], func=AF.Sigmoid), [("ps", bgs)], sk(ib))
                    tt(SCR[:, ia, 0:n], SCR[:, ia, 0:n], PSUM[boa][:, 0:n], ALU.mult, sk(ia) + [("ps", boa)], sk(ia))
                    tt(SCR[:, ib, 0:n], SCR[:, ib, 0:n], PSUM[bob][:, 0:n], ALU.mult, sk(ib) + [("ps", bob)], sk(ib))
                    tt(R[:, MT[j], c0:c0 + n], SCR[:, ia, 0:n], SCR[:, ib, 0:n], ALU.add, sk(ia) + sk(ib), [bkey("R", MT[j], c0)])
            done_unit(wga)
            done_unit(wgs)
            done_unit(wbr)
        Ms = lambda k: R[:, MT[k], :]
        Mk = lambda k, c0: bkey("R", MT[k], c0)
        PHN["p"] = "after_gates"
        if bail("gates"):
            return
        for jh in range(4):
            w = next_unit("wo")
            for mt in range(2):
                j = 2 * jh + mt
                for (c0, n, kind) in blks:
                    b = proj(w, mt, Ms, Mk, c0, n)
                    tt(X[:, j, c0:c0 + n], X[:, j, c0:c0 + n], PSUM[b][:, 0:n], ALU.add, [bkey("X", j, c0), ("ps", b)], [bkey("X", j, c0)])
            done_unit(w)
        PHN["p"] = "after_wout"
        if bail("wout"):
            return
        PHN["p"] = "ffn"
        idx_ = SEQ_LH.index((half, l))
        if idx_ + 1 < len(SEQ_LH) and SEQ_LH[idx_ + 1][1] < DBG["layers"]:
            prefetch_loads(*SEQ_LH[idx_ + 1])
        rmsnorm(half, "n2", l * 8, True)
        As = lambda k: R[:, k, :]
        Ak = lambda k, c0: bkey("R", k, c0)
        for (t0, nt) in ((0, 12), (12, 10)):
            for pr in range(nt // 2):
                if pr == 1 and t0 == 0:
                    idx0_ = SEQ_LH.index((half, l))
                    nx0_ = SEQ_LH[idx0_ + 1] if (idx0_ + 1 < len(SEQ_LH) and SEQ_LH[idx0_ + 1][1] < DBG["layers"]) else None
                    if nx0_ is not None:
                        pf_start(nx0_[0], nx0_[1])
                wg = next_unit("fg")
                wu = next_unit("fu")
                for mt in range(2):
                    ti = 2 * pr + mt
                    for (c0, n, kind) in blks:
                        bg = proj(wg, mt, Hs, Hk, c0, n)
                        bu = proj(wu, mt, Hs, Hk, c0, n)
                        i = scr()
                        act(lambda e, bg=bg, i=i, n=n: e.activation(out=SCR[:, i, 0:n], in_=PSUM[bg][:, 0:n], func=AF.Silu), [("ps", bg)], sk(i))
                        tt(R[:, ti, c0:c0 + n], SCR[:, i, 0:n], PSUM[bu][:, 0:n], ALU.mult, sk(i) + [("ps", bu)], [bkey("R", ti, c0)])
                        pf_step(PFN)
                done_unit(wg)
                done_unit(wu)
            idx_ = SEQ_LH.index((half, l))
            nxt_ = SEQ_LH[idx_ + 1] if (idx_ + 1 < len(SEQ_LH) and SEQ_LH[idx_ + 1][1] < DBG["layers"]) else None
            for jq in range(4):
                ws = []
                rem = nt
                while rem > 0:
                    ws.append(next_unit("fd"))
                    rem -= ws[-1].kt
                for mt in range(2):
                    j = 2 * jq + mt
                    for (c0, n, kind) in blks:
                        b = bank()
                        kidx = 0
                        for wi, w in enumerate(ws):
                            for kw in range(w.kt):
                                mm(PSUM[b][:, 0:n], w.v[:, kw, mt * 128:(mt + 1) * 128], R[:, kidx, c0:c0 + n], kidx == 0, kidx == nt - 1,
                                   [w.key, bkey("R", kidx, c0)], [("ps", b)])
                                kidx += 1
                        tt(X[:, j, c0:c0 + n], X[:, j, c0:c0 + n], PSUM[b][:, 0:n], ALU.add, [bkey("X", j, c0), ("ps", b)], [bkey("X", j, c0)])
                        pf_step(PFN)
                for w in ws:
                    done_unit(w)

        pf_finish()

    PFN = 2

    def DBG_skip_layer():
        n_layer_units = len(units) // (2 * DEPTH)
        for _ in range(n_layer_units):
            i_ = ust["next"]
            done_unit(next_unit(units[i_][1]))

    prefetch_loads(0, 0)
    prefetch(0, 0)
    for half in range(2):
        tiles = [(xp, half * 1024 + 128 * t, 128, 128 * t) for t in range(8)]
        if half == 0:
            tiles.append((xs, 0, NS, 1024))
        for (src, r0, nr, c0) in tiles:
            i = scr(2)
            stg = SCR[:, i:i + 2, :].rearrange("p a b -> p (a b)")
            load(stg[0:nr, :], src[r0:r0 + nr, :], sk(i, 2))
            for kh in range(2):
                b = bank()
                for kk in range(4):
                    kt = kh * 4 + kk
                    S.op("pe", lambda e, b=b, kk=kk, kt=kt, stg=stg, nr=nr: e.matmul(
                        PSUM[b][:, kk * 128:kk * 128 + nr], lhsT=stg[0:nr, kt * 128:(kt + 1) * 128], rhs=IDF[0:nr, 0:nr], start=True, stop=True),
                        reads=sk(i, 2) + ["IDF"], writes=[("ps", b)], mode=pmode(stg[0:nr, kt * 128:(kt + 1) * 128]))
                cb0 = (c0 // 512) * 512
                dve(lambda e, b=b, kh=kh, c0=c0, nr=nr: e.tensor_copy(X[:, kh * 4:kh * 4 + 4, c0:c0 + nr],
                                                                   PSUM[b][:, :].rearrange("p (k c) -> p k c", k=4)[:, :, 0:nr]),
                    [("ps", b)], [bkey("X", kh * 4 + kk, cb0) for kk in range(4)])
        for l in range(DEPTH):
            if l < DBG["layers"]:
                layer(half, l)
            else:
                DBG_skip_layer()
        rmsnorm(half, "fg", 0, False)
        for (dst, r0, nr, c0) in [(yp, half * 1024 + 128 * t, 128, 128 * t) for t in range(8)] + ([(ysm, 0, NS, 1024)] if half == 0 else []):
            i = scr(2)
            stg = SCR[:, i:i + 2, :].rearrange("p a b -> p (a b)")
            cb0 = (c0 // 512) * 512
            for kh in range(2):
                b = bank()
                for kk in range(4):
                    kt = kh * 4 + kk
                    S.op("pe", lambda e, b=b, kk=kk, kt=kt, c0=c0, nr=nr: e.matmul(
                        PSUM[b][0:nr, kk * 128:(kk + 1) * 128], lhsT=X[:, kt, c0:c0 + nr], rhs=IDF[:], start=True, stop=True),
                        reads=[bkey("X", kt, cb0), "IDF"], writes=[("ps", b)], mode=pmode(X[:, kt, c0:c0 + nr]))
                dve(lambda e, b=b, kh=kh, stg=stg, nr=nr: e.tensor_copy(stg[0:nr, kh * 512:(kh + 1) * 512], PSUM[b][0:nr, :]), [("ps", b)], sk(i, 2))
            store(dst[r0:r0 + nr, :], stg[0:nr, :], sk(i, 2))

    for l in range(DEPTH):
        b = bank()
        S.op("pe", lambda e, b=b, l=l: e.matmul(PSUM[b][0:32, 0:128], lhsT=HC[:, l].rearrange("p r q -> p (r q)"), rhs=IDF[:], start=True, stop=True),
             reads=[("HCq", f_) for f_ in range(4)] + ["IDF"], writes=[("ps", b)], mode=pmode(HC[:, l].rearrange("p r q -> p (r q)")))
        S.op("pe", lambda e, b=b, l=l: e.matmul(PSUM[b][0:8, 128:256], lhsT=CONVST[:, l].rearrange("p k f -> p (k f)"), rhs=IDF[:], start=True, stop=True),
             reads=["CONVST", "IDF"], writes=[("ps", b)], mode=pmode(CONVST[:, l].rearrange("p k f -> p (k f)")))
        i = scr()
        dve(lambda e, b=b, i=i: e.tensor_copy(SCR[0:32, i, 0:256], PSUM[b][0:32, 0:256]), [("ps", b)], sk(i))
        store(pre_o[l], SCR[0:16, i, 0:128], sk(i))
        store(pim_o[l], SCR[16:32, i, 0:128], sk(i))
        store(pconv_o[l].rearrange("k (f p) -> (k f) p", p=128), SCR[0:8, i, 128:256], sk(i))
    fin = [S.last_dma[f"st{i}"] for i in range(4) if f"st{i}" in S.last_dma]
    S.op("sp", lambda e: (e.engine_nop() if hasattr(e, "engine_nop") else None), deps=fin)
    assert ust["next"] == len(units), (ust, len(units))

    S.finalize()
    with nc.Block() as block:
        block.sync(S.runner("sp", sems, dsems))
        block.tensor(S.runner("pe", sems, dsems))
        block.scalar(S.runner("act", sems, dsems))
        block.vector(S.runner("dve", sems, dsems))
        block.gpsimd(S.runner("pool", sems, dsems))
    es.close()
    return nc


_NC = None


def kernel(**inp):
    global _NC
    if _NC is None:
        _NC = build()
    nc = _NC
    f = lambda a: np.ascontiguousarray(np.asarray(a, dtype=np.float32))
    shared = {k: f(inp[k]) for k in ("norm1_g", "w_in", "conv_w", "conv_b", "ssm_lam_re", "ssm_lam_im", "ssm_log_dt",
                                     "ssm_b_re", "ssm_b_im", "ssm_c_re", "ssm_c_im", "ssm_d", "w_glu", "b_glu",
                                     "w_branch", "w_out", "norm2_g", "w_gate_up", "w_down")}
    shared["final_g"] = f(inp["final_g"]).reshape(1, D)
    xpr = f(inp["x_prompt"])
    xsm = f(inp["x_sample"])
    sc = f(inp["state_conv"])
    sr = f(inp["state_ssm_re"])
    si = f(inp["state_ssm_im"])
    in_maps = []
    for c in range(8):
        m = dict(shared)
        m["xp"] = xpr[c]
        m["xs"] = np.ascontiguousarray(xsm[16 * c:16 * c + 16, 0, :])
        m["sconv"] = np.ascontiguousarray(sc[:, 16 * c:16 * c + 16])
        m["sre"] = np.ascontiguousarray(sr[:, 16 * c:16 * c + 16].reshape(DEPTH, NS, 2048))
        m["sim"] = np.ascontiguousarray(si[:, 16 * c:16 * c + 16].reshape(DEPTH, NS, 2048))
        in_maps.append(m)
    res = run_bass_kernel_spmd(nc, in_maps, core_ids=list(range(8)))
    rs = res.results
    y_prompt = np.stack([r["yp"] for r in rs], 0)
    y_sample = np.concatenate([r["ysm"] for r in rs], 0).reshape(128, 1, D)
    prompt_conv = np.stack([r["pconv"] for r in rs], 1)
    prompt_re = np.stack([r["pre"].reshape(DEPTH, 32, 64) for r in rs], 1)
    prompt_im = np.stack([r["pim"].reshape(DEPTH, 32, 64) for r in rs], 1)
    sample_conv = np.concatenate([r["sconv_o"] for r in rs], 1)
    sample_re = np.concatenate([r["sre_o"].reshape(DEPTH, NS, 32, 64) for r in rs], 1)
    sample_im = np.concatenate([r["sim_o"].reshape(DEPTH, NS, 32, 64) for r in rs], 1)
    return (y_prompt.astype(np.float32), y_sample.astype(np.float32), prompt_conv.astype(np.float32),
            prompt_re.astype(np.float32), prompt_im.astype(np.float32), sample_conv.astype(np.float32),
            sample_re.astype(np.float32), sample_im.astype(np.float32))
```

```python
import math
from contextlib import ExitStack
import numpy as np
import concourse.bass as bass
import concourse.mybir as mybir
from concourse.bass_utils import run_bass_kernel_spmd

F32 = mybir.dt.float32
BF16 = mybir.dt.bfloat16
ALU = mybir.AluOpType
AF = mybir.ActivationFunctionType

D = 1024
DEPTH = 4
SEQ = 2048
NS = 16
CW = 512
DFF = 2816
T = 8
NC2 = 1040
EPS = 1e-6
PI = math.pi
DBG = {"layers": DEPTH, "phases": None, "halves": 2}


class Op:
    __slots__ = ("eng", "fn", "deps", "needed", "val", "sem")


class Sched:
    ENGS = ("pe", "act", "dve", "pool", "sp")

    def __init__(self):
        self.prog = {e: [] for e in self.ENGS}
        self.last_w = {}
        self.readers = {}
        self.last_dma = {}

    def op(self, eng, fn, reads=(), writes=(), deps=(), dma_sem=None, mode=None):
        d = list(deps)
        forced = None
        if eng == "pe" and dma_sem is None:
            lp = getattr(self, "last_pe", None)
            if lp is not None and mode != self.last_pe_mode:
                forced = lp
        for k in reads:
            w = self.last_w.get(k)
            if w is not None:
                d.append(w)
        for k in writes:
            w = self.last_w.get(k)
            if w is not None:
                d.append(w)
            d.extend(self.readers.get(k, {}).values())
        if dma_sem is not None and dma_sem in self.last_dma:
            d.append(self.last_dma[dma_sem])
        o = Op()
        o.eng = eng
        o.fn = fn
        o.needed = False
        o.val = None
        o.sem = dma_sem
        dd = []
        seen = set()
        for x in d:
            if id(x) in seen:
                continue
            seen.add(id(x))
            if x.sem is None and x.eng == "pe" and eng == "pe" and dma_sem is None:
                continue
            x.needed = True
            dd.append(x)
        if forced is not None and id(forced) not in seen:
            forced.needed = True
            dd.append(forced)
        o.deps = dd
        if eng == "pe" and dma_sem is None:
            self.last_pe = o
            self.last_pe_mode = mode
        if dma_sem is not None:
            o.needed = True
            self.last_dma[dma_sem] = o
        rk = dma_sem if dma_sem is not None else eng
        for k in reads:
            self.readers.setdefault(k, {})[rk] = o
        for k in writes:
            self.last_w[k] = o
            self.readers[k] = {}
        self.prog[eng].append(o)
        return o

    def finalize(self):
        cnt = {e: 0 for e in self.ENGS}
        dcnt = {}
        for e in self.ENGS:
            for o in self.prog[e]:
                if o.sem is not None:
                    dcnt[o.sem] = dcnt.get(o.sem, 0) + 16
                    o.val = dcnt[o.sem]
                elif o.needed:
                    cnt[e] += 1
                    o.val = cnt[e]

    def runner(self, e, sems, dsems):
        def body(eng):
            waited = {}
            for o in self.prog[e]:
                for x in o.deps:
                    if x.sem is not None:
                        key = ("d", x.sem)
                        sh = dsems[x.sem]
                    else:
                        key = ("e", x.eng)
                        sh = sems[x.eng]
                    if waited.get(key, 0) >= x.val:
                        continue
                    waited[key] = x.val
                    eng.wait_ge(sh, x.val)
                ins = o.fn(eng)
                if o.sem is not None:
                    ins.then_inc(dsems[o.sem], 16)
                elif o.needed:
                    ins.then_inc(sems[e], 1)
        return body


def build():
    nc = bass.Bass("TRN2", target_bir_lowering=False)

    def din(name, shape):
        return nc.dram_tensor(name, list(shape), F32, kind="ExternalInput").ap()

    def dout(name, shape):
        return nc.dram_tensor(name, list(shape), F32, kind="ExternalOutput").ap()

    xp = din("xp", [SEQ, D])
    xs = din("xs", [NS, D])
    sconv_in = din("sconv", [DEPTH, NS, 2, CW])
    sre_in = din("sre", [DEPTH, NS, 2048])
    sim_in = din("sim", [DEPTH, NS, 2048])
    norm1_g = din("norm1_g", [DEPTH, D])
    w_in = din("w_in", [DEPTH, D, 4096])
    conv_w = din("conv_w", [DEPTH, 3, CW])
    conv_b = din("conv_b", [DEPTH, CW])
    lam_re = din("ssm_lam_re", [DEPTH, 32, 64])
    lam_im = din("ssm_lam_im", [DEPTH, 32, 64])
    log_dt = din("ssm_log_dt", [DEPTH, 32])
    b_re = din("ssm_b_re", [DEPTH, 32, 64, 16])
    b_im = din("ssm_b_im", [DEPTH, 32, 64, 16])
    c_re = din("ssm_c_re", [DEPTH, 32, 16, 64])
    c_im = din("ssm_c_im", [DEPTH, 32, 16, 64])
    ssm_d = din("ssm_d", [DEPTH, CW])
    w_glu = din("w_glu", [DEPTH, CW, CW])
    b_glu = din("b_glu", [DEPTH, CW])
    w_branch = din("w_branch", [DEPTH, D, D])
    w_out = din("w_out", [DEPTH, D, D])
    norm2_g = din("norm2_g", [DEPTH, D])
    w_gate_up = din("w_gate_up", [DEPTH, D, 2 * DFF])
    w_down = din("w_down", [DEPTH, DFF, D])
    final_g = din("final_g", [1, D])

    yp = dout("yp", [SEQ, D])
    ysm = dout("ysm", [NS, D])
    pconv_o = dout("pconv", [DEPTH, 2, CW])
    pre_o = dout("pre", [DEPTH, 16, 128])
    pim_o = dout("pim", [DEPTH, 16, 128])
    sconv_o = dout("sconv_o", [DEPTH, NS, 2, CW])
    sre_o = dout("sre_o", [DEPTH, NS, 2048])
    sim_o = dout("sim_o", [DEPTH, NS, 2048])

    es = ExitStack()

    def sb(name, shape, dt=F32):
        return es.enter_context(nc.sbuf_tensor(name, list(shape), dt))

    NSLOT = 5
    X = sb("X", [128, 8, NC2])
    H = sb("H", [128, 8, NC2], BF16)
    R = sb("R", [128, 16, NC2], BF16)
    RING = sb("RING", [128, NSLOT, 2048], BF16)
    NSCR = 8
    SCR = sb("SCR", [128, NSCR, 512])
    NSCB = 5
    SCB = sb("SCB", [128, NSCB, 512], BF16)
    GS = sb("GS", [128, 2, 16, 128])
    HBF = sb("HBF", [128, 2, 16, 129], BF16)
    TW = sb("TW", [128, 2, 16, 128])
    R8T = sb("R8T", [128, 16])
    TP = sb("TP", [128, 160])
    HC = sb("HC", [128, DEPTH, 2, 16])
    CONVST = sb("CONVST", [128, DEPTH, 2, 4])
    PT = sb("PT", [128, 256])
    LAMT = sb("LAMT", [128, 2, 64])
    LDT = sb("LDT", [128, 64])
    IDF = sb("IDF", [128, 128])
    IDB = sb("IDB", [128, 128], BF16)
    ONESB = sb("ONESB", [128, 128], BF16)
    MASK2 = sb("MASK2", [128, 2])
    MASK4 = sb("MASK4", [128, 4])
    MASKC = sb("MASKC", [128, 128])
    MV = sb("MV", [128, 9, 16])
    PWR = sb("PWR", [128, 2, 9, 16])
    BXF = sb("BXF", [128, 2, 16, 32])
    BXB = sb("BXB", [128, 2, 16, 32], BF16)
    CXF = sb("CXF", [128, 2, 4, 128])
    WBF = [sb(f"WBF{i}", [128, 8, 2, 128], BF16) for i in range(2)]
    CAF = [sb(f"CAF{i}", [128, 2, 9, 4, 32], BF16) for i in range(2)]
    KF = [sb(f"KF{i}", [128, 8, 128], BF16) for i in range(2)]
    SST = sb("SST", [128, 2, 16, NS])
    SSTB = sb("SSTB", [128, 2, 16, NS], BF16)
    SNEW = sb("SNEW", [128, 2, 16, NS])
    SCV = sb("SCV", [128, 2, 4, NS])
    CINS = sb("CINS", [128, 4, NS])
    PSUM = [es.enter_context(nc.psum_tensor(f"ps{i}", [128, 512], F32)) for i in range(8)]
    NLD = 4
    sems = {e: es.enter_context(nc.semaphore(f"s_{e}")) for e in Sched.ENGS}
    dsem_names = [f"w{i}" for i in range(NSLOT)] + [f"ld{i}" for i in range(NLD)] + [f"st{i}" for i in range(4)]
    dsems = {k: es.enter_context(nc.semaphore(f"d_{k}")) for k in dsem_names}

    S = Sched()
    PHN = {"p": "init"}
    st = {"bank": 0, "scr": 0, "scb": 0, "ld": 0, "st": 0}

    def bank():
        b = st["bank"]
        st["bank"] = (b + 1) % 8
        return b

    def scr(n=1):
        if st.get("ffn"):
            if st.get("in_pf"):
                i = st.get("scr_pf", 0)
                if i + n > 6:
                    i = 0
                st["scr_pf"] = (i + n) % 6
                return i
            assert n == 1
            i = st.get("scr_ffn", 6)
            st["scr_ffn"] = 13 - i
            return i
        i = st["scr"]
        if i + n > NSCR:
            i = 0
        st["scr"] = (i + n) % NSCR
        return i

    def scb():
        i = st["scb"]
        st["scb"] = (i + 1) % NSCB
        return i

    def ldsem():
        i = st["ld"]
        st["ld"] = (i + 1) % NLD
        return f"ld{i}"

    def stsem():
        i = st["st"]
        st["st"] = (i + 1) % 4
        return f"st{i}"

    def sk(i, n=1):
        if isinstance(i, str):
            return [i]
        return [("scr", i + t) for t in range(n)]

    def load(out_ap, in_ap, writes, nonc=False):
        return S.op("sp", lambda e: e.dma_start(out=out_ap, in_=in_ap, allow_slow_non_contiguous=nonc),
                    writes=writes, dma_sem=ldsem())

    def store(out_ap, in_ap, reads, nonc=False, writes=()):
        return S.op("sp", lambda e: e.dma_start(out=out_ap, in_=in_ap, allow_slow_non_contiguous=nonc),
                    reads=reads, writes=writes, dma_sem=stsem())

    STG = R[:, 12:16, :].rearrange("p a b -> p (a b)").bitcast(F32)
    STGK = [("R", t_, c_) for t_ in range(12, 16) for c_ in (0, 512, 1024)]
    STGB = ["STGB"]
    STGC = ["STGC"]

    def pmode(lhsT):
        sh = lhsT.shape
        k_ = sh[0]
        m_ = 1
        for x_ in sh[1:]:
            m_ *= x_
        r_ = lambda v: 32 if v <= 32 else (64 if v <= 64 else 128)
        return (r_(k_), r_(m_), str(lhsT.dtype))

    def mm(out, lhsT, rhs, start, stop, reads, writes, **kw):
        ph_ = PHN["p"]
        return S.op("pe", lambda e: e.matmul(out, lhsT=lhsT, rhs=rhs, start=start, stop=stop, **kw).annotate(ph_),
                    reads=reads, writes=writes, mode=pmode(lhsT))

    def dve(fn, reads, writes):
        return S.op("dve", fn, reads=reads, writes=writes)

    def act(fn, reads, writes):
        return S.op("act", fn, reads=reads, writes=writes)

    def tt(out, a, b, op, reads, writes, eng="dve"):
        return S.op(eng, lambda e: e.tensor_tensor(out=out, in0=a, in1=b, op=op), reads=reads, writes=writes)

    def bc(ap, dims):
        return bass.AP(ap.tensor, ap.offset, [list(ap.ap[0])] + [list(x) for x in dims])

    S.op("pool", lambda e: e.memset(IDF[:], 0.0), writes=["IDF"])
    S.op("pool", lambda e: e.affine_select(out=IDF[:], in_=IDF[:], pattern=[[-1, 128]], compare_op=ALU.not_equal,
                                           fill=1.0, base=0, channel_multiplier=1), reads=["IDF"], writes=["IDF"])
    S.op("pool", lambda e: e.tensor_copy(IDB[:], IDF[:]), reads=["IDF"], writes=["IDB"])
    S.op("pool", lambda e: e.memset(ONESB[:], 1.0), writes=["ONESB"])
    S.op("dve", lambda e: e.tensor_reduce(out=MASK4[:], in_=IDF[:].rearrange("p (a b) -> p a b", b=32),
                                          axis=mybir.AxisListType.X, op=ALU.add), reads=["IDF"], writes=["MASK4"])
    S.op("dve", lambda e: e.tensor_reduce(out=MASK2[:], in_=IDF[:].rearrange("p (a b) -> p a b", b=64),
                                          axis=mybir.AxisListType.X, op=ALU.add), reads=["IDF"], writes=["MASK2"])
    for g2 in range(2):
        S.op("dve", lambda e, g2=g2: e.tensor_copy(
            MASKC[:].rearrange("p (a b c) -> p a b c", b=2, c=16)[:, :, g2, :],
            bc(MASK2[:, g2:g2 + 1], [[0, 4], [0, 16]])), reads=["MASK2"], writes=["MASKC"])
    for m_ in range(9):
        S.op("pool", lambda e, m_=m_: e.memset(MV[:, m_, :], float(m_)), writes=["MV"])
    S.op("pool", lambda e: e.memset(HC[:], 0.0), writes=[("HCq", f_) for f_ in range(4)])
    S.op("pool", lambda e: e.memset(CONVST[:], 0.0), writes=["CONVST"])

    PROW = {"n1": 0, "n2": 32, "fg": 64, "cw": 72, "cb": 120, "d": 136, "bg": 152}
    i0 = scr(2)
    praw = SCR[:, i0:i0 + 2, :].rearrange("p a b -> p (a b)")
    S.op("pool", lambda e: e.memset(praw[:, 0:256], 0.0), writes=sk(i0, 2))
    load(praw[0:32, 0:128], norm1_g.rearrange("l (k p) -> (l k) p", p=128), sk(i0, 2))
    load(praw[32:64, 0:128], norm2_g.rearrange("l (k p) -> (l k) p", p=128), sk(i0, 2))
    load(praw[64:72, 0:128], final_g.rearrange("l (k p) -> (l k) p", p=128), sk(i0, 2))
    load(praw[72:120, 0:128], conv_w.rearrange("l k (f p) -> (l k f) p", p=128), sk(i0, 2))
    load(praw[120:128, 0:128], conv_b.rearrange("l (f p) -> (l f) p", p=128)[0:8], sk(i0, 2))
    load(praw[0:8, 128:256], conv_b.rearrange("l (f p) -> (l f) p", p=128)[8:16], sk(i0, 2))
    load(praw[8:24, 128:256], ssm_d.rearrange("l (f p) -> (l f) p", p=128), sk(i0, 2))
    load(praw[24:40, 128:256], b_glu.rearrange("l (f p) -> (l f) p", p=128), sk(i0, 2))
    for hf in range(2):
        b = bank()
        S.op("pe", lambda e, b=b, hf=hf: e.matmul(PSUM[b][:, 0:128], lhsT=praw[:, hf * 128:(hf + 1) * 128], rhs=IDF[:],
                                                  start=True, stop=True), reads=sk(i0, 2) + ["IDF"], writes=[("ps", b)], mode=pmode(praw[:, hf * 128:(hf + 1) * 128]))
        dve(lambda e, b=b, hf=hf: e.tensor_copy(PT[:, hf * 128:(hf + 1) * 128], PSUM[b][:, 0:128]), [("ps", b)], ["PT"])

    def pcol(name, idx):
        base = {"n1": 0, "n2": 32, "fg": 64, "cw": 72, "cb": 120, "d": 136, "bg": 152}[name]
        r = base + idx
        return PT[:, r:r + 1]

    i1 = scr(1)
    lraw = SCR[:, i1, :]
    load(lraw[0:64, 0:128], lam_re.rearrange("l (q g) p -> (l q) (g p)", g=2), sk(i1))
    load(lraw[0:64, 128:256], lam_im.rearrange("l (q g) p -> (l q) (g p)", g=2), sk(i1))
    load(lraw[0:2, 256:320].rearrange("p (l q) -> p l q", l=4), log_dt.rearrange("l (q g) -> g l q", g=2), sk(i1), nonc=True)
    for ri in range(2):
        b = bank()
        S.op("pe", lambda e, b=b, ri=ri: e.matmul(PSUM[b][:, 0:64], lhsT=lraw[0:64, ri * 128:(ri + 1) * 128], rhs=IDF[0:64, 0:64],
                                                  start=True, stop=True), reads=sk(i1) + ["IDF"], writes=[("ps", b)], mode=pmode(lraw[0:64, ri * 128:(ri + 1) * 128]))
        dve(lambda e, b=b, ri=ri: e.tensor_copy(LAMT[:, ri, :], PSUM[b][:, 0:64]), [("ps", b)], ["LAMT"])
    i2 = scr(1)
    sel = SCR[:, i2, :]
    b = bank()
    S.op("pe", lambda e, b=b: e.matmul(PSUM[b][0:2, 0:128], lhsT=MASK2[:], rhs=IDF[:], start=True, stop=True),
         reads=["MASK2", "IDF"], writes=[("ps", b)], mode=pmode(MASK2[:]))
    dve(lambda e, b=b: e.tensor_copy(sel[0:2, 0:128], PSUM[b][0:2, 0:128]), [("ps", b)], sk(i2))
    b = bank()
    S.op("pe", lambda e, b=b: e.matmul(PSUM[b][:, 0:64], lhsT=sel[0:2, 0:128], rhs=lraw[0:2, 256:320], start=True, stop=True),
         reads=sk(i1) + sk(i2), writes=[("ps", b)], mode=pmode(sel[0:2, 0:128]))
    dve(lambda e, b=b: e.tensor_copy(LDT[:], PSUM[b][:, 0:64]), [("ps", b)], ["LDT"])

    units = []

    def plan_layer(l):
        u = []
        def win(c0):
            return [(w_in[l, :, c0:c0 + 256], 8, 256)]
        for f0 in range(2):
            u.append((win(1536 + 256 * f0), "zu"))
        for f0 in range(2):
            u.append((win(512 + 256 * f0), "zc"))
            u.append((win(1024 + 256 * f0), "zv"))
            u.append((win(256 * f0), "zb"))
        u.append(([(w_glu[l, :, :], 4, 512)], "glu"))
        for jh in range(4):
            u.append((win(2048 + 256 * jh), "ga"))
            u.append((win(3072 + 256 * jh), "gs"))
            u.append(([(w_branch[l, :, 256 * jh:256 * jh + 256], 8, 256)], "wb"))
        for jh in range(4):
            u.append(([(w_out[l, :, 256 * jh:256 * jh + 256], 8, 256)], "wo"))
        for (t0, nt) in ((0, 12), (12, 10)):
            for pr in range(nt // 2):
                c0 = (t0 + 2 * pr) * 128
                u.append(([(w_gate_up[l, :, c0:c0 + 256], 8, 256)], "fg"))
                u.append(([(w_gate_up[l, :, DFF + c0:DFF + c0 + 256], 8, 256)], "fu"))
            for jq in range(4):
                k0 = t0
                rem = nt
                while rem > 0:
                    kk = min(8, rem)
                    u.append(([(w_down[l, k0 * 128:(k0 + kk) * 128, 256 * jq:256 * jq + 256], kk, 256)], "fd"))
                    k0 += kk
                    rem -= kk
        return u

    for half in range(2):
        for l in range(DEPTH):
            units.extend(plan_layer(l))
    ust = {"issued": 0, "next": 0}

    def issue_unit():
        i = ust["issued"]
        if i >= len(units):
            return
        pieces, tag = units[i]
        slot = i % NSLOT
        (ap, kt, ncols) = pieces[0]
        S.op("pool", lambda e, ap=ap, kt=kt, ncols=ncols, slot=slot: e.dma_start(
            out=RING[:, slot, 0:kt * ncols].rearrange("p (k c) -> p k c", k=kt),
            in_=ap.rearrange("(k p) c -> p k c", p=128)), writes=[("w", slot)], dma_sem=f"w{slot}")
        ust["issued"] = i + 1

    for _ in range(NSLOT):
        issue_unit()

    class WU:
        pass

    def next_unit(tag):
        i = ust["next"]
        pieces, t = units[i]
        assert t == tag, (t, tag, i)
        ust["next"] = i + 1
        w = WU()
        w.slot = i % NSLOT
        w.kt = pieces[0][1]
        w.nc = pieces[0][2]
        w.key = ("w", w.slot)
        w.v = RING[:, w.slot, 0:w.kt * w.nc].rearrange("p (k c) -> p k c", k=w.kt)
        return w

    def done_unit(w):
        issue_unit()

    def blocks(half):
        bl = [(0, 512, "p"), (512, 512, "p")]
        if half == 0:
            bl.append((1024, NS, "s"))
        return bl

    def proj(w, mt, src, srckey, c0, n, kts=None):
        b = bank()
        kts = list(range(w.kt)) if kts is None else kts
        for i, (kw, ks) in enumerate(kts if isinstance(kts[0], tuple) else [(k, k) for k in kts]):
            mm(PSUM[b][:, 0:n], w.v[:, kw, mt * 128:(mt + 1) * 128], src(ks)[:, c0:c0 + n], i == 0, i == len(kts) - 1,
               [w.key, srckey(ks, c0)], [("ps", b)])
        return b

    def bkey(name, t, c0):
        return (name, t, c0)

    def rmsnorm(half, gname, gidx0, out_h):
        for (c0, n, kind) in blocks(half):
            b = bank()
            for kt in range(8):
                j = scb()
                act(lambda e, kt=kt, j=j, c0=c0, n=n: e.activation(out=SCB[:, j, 0:n], in_=X[:, kt, c0:c0 + n], func=AF.Square),
                    [bkey("X", kt, c0)], [("scb", j)])
                mm(PSUM[b][:, 0:n], ONESB[:], SCB[:, j, 0:n], kt == 0, kt == 7, [("scb", j), "ONESB"], [("ps", b)])
            i = scr()
            act(lambda e, b=b, i=i, n=n: e.activation(out=SCR[:, i, 0:n], in_=PSUM[b][:, 0:n], func=AF.Ln, bias=EPSC[:, 0:1], scale=1.0 / D),
                [("ps", b), "EPSC"], sk(i))
            act(lambda e, i=i, n=n: e.activation(out=SCR[:, i, 0:n], in_=SCR[:, i, 0:n], func=AF.Exp, scale=-0.5), sk(i), sk(i))
            for kt in range(8):
                if out_h:
                    dve(lambda e, kt=kt, i=i, c0=c0, n=n: e.scalar_tensor_tensor(
                        out=H[:, kt, c0:c0 + n], in0=X[:, kt, c0:c0 + n], scalar=pcol(gname, gidx0 + kt), in1=SCR[:, i, 0:n],
                        op0=ALU.mult, op1=ALU.mult), [bkey("X", kt, c0), "PT"] + sk(i), [bkey("H", kt, c0)])
                else:
                    dve(lambda e, kt=kt, i=i, c0=c0, n=n: e.scalar_tensor_tensor(
                        out=X[:, kt, c0:c0 + n], in0=X[:, kt, c0:c0 + n], scalar=pcol(gname, gidx0 + kt), in1=SCR[:, i, 0:n],
                        op0=ALU.mult, op1=ALU.mult), [bkey("X", kt, c0), "PT"] + sk(i), [bkey("X", kt, c0)])

    EPSC = sb("EPSC", [128, 4])
    S.op("pool", lambda e: e.memset(EPSC[:, 0:1], EPS), writes=["EPSC"])
    S.op("pool", lambda e: e.memset(EPSC[:, 1:2], PI / 2.0), writes=["EPSC"])

    Hs = lambda k: H[:, k, :]
    Hk = lambda k, c0: bkey("H", k, c0)
    Rt = lambda base: (lambda k: R[:, base + k, :])
    Rk = lambda base: (lambda k, c0: bkey("R", base + k, c0))
    YA, YS, UT, MM_ = 0, 4, 8, 8

    def ssm_prep_layer_g(l):
        q0 = l * 16
        i = "TPk"
        t = TP[:, :]
        act(lambda e: e.activation(out=t[:, 0:16], in_=LDT[:, q0:q0 + 16], func=AF.Exp), ["LDT"], sk(i))
        yield
        tt(t[:, 16:32], LAMT[:, 0, q0:q0 + 16], t[:, 0:16], ALU.mult, sk(i) + ["LAMT"], sk(i))
        yield
        tt(t[:, 32:48], LAMT[:, 1, q0:q0 + 16], t[:, 0:16], ALU.mult, sk(i) + ["LAMT"], sk(i))
        yield
        i2 = scr(1)
        u = SCR[:, i2, :]
        k2 = sk(i2)
        dve(lambda e: e.tensor_tensor(out=u[:, 0:144].rearrange("p (m q) -> p m q", q=16), in0=MV[:], in1=bc(t[:, 16:32], [[0, 9], [1, 16]]), op=ALU.mult),
            sk(i) + ["MV"], k2)
        yield
        act(lambda e: e.activation(out=u[:, 0:144], in_=u[:, 0:144], func=AF.Exp), k2, k2)
        yield
        cA, sA, cB, sB, x1, x2 = (u[:, 144 + 16 * z:160 + 16 * z] for z in range(6))
        act(lambda e: e.activation(out=sA, in_=t[:, 32:48], func=AF.Sin, scale=1.0 / 16.0), sk(i), k2)
        yield
        act(lambda e: e.activation(out=cA, in_=t[:, 32:48], func=AF.Sin, bias=EPSC[:, 1:2], scale=1.0 / 16.0), sk(i) + ["EPSC"], k2)
        yield
        cur = (cA, sA)
        nxt = (cB, sB)
        for _sq in range(4):
            c_, s_ = cur
            tt(x1, c_, c_, ALU.mult, k2, k2)
            yield
            tt(x2, s_, s_, ALU.mult, k2, k2)
            yield
            tt(nxt[0], x1, x2, ALU.subtract, k2, k2)
            yield
            tt(x1, c_, s_, ALU.mult, k2, k2)
            yield
            dve(lambda e, o=nxt[1]: e.tensor_scalar(out=o, in0=x1, scalar1=2.0, scalar2=None, op0=ALU.mult), k2, k2)
            yield
            cur, nxt = nxt, cur
        i5 = scr(1)
        w_ = SCR[:, i5, :]
        k5 = sk(i5)
        ER = w_[:, 0:144].rearrange("p (m q) -> p m q", q=16)
        EI = w_[:, 144:288].rearrange("p (m q) -> p m q", q=16)
        y1 = w_[:, 288:352].rearrange("p (m q) -> p m q", q=16)
        y2 = w_[:, 352:416].rearrange("p (m q) -> p m q", q=16)
        S.op("dve", lambda e: e.memset(w_[:, 0:16], 1.0), writes=k5)
        S.op("dve", lambda e: e.memset(w_[:, 144:160], 0.0), writes=k5)
        dve(lambda e: e.tensor_copy(ER[:, 1, :], cur[0]), k2, k5)
        yield
        dve(lambda e: e.tensor_copy(EI[:, 1, :], cur[1]), k2, k5)
        yield

        def cmul_rng(dst0, src0, cnt, mul):
            ar_, ai_ = ER[:, src0:src0 + cnt, :], EI[:, src0:src0 + cnt, :]
            br_ = bc(ER[:, mul, :], [[0, cnt], [1, 16]])
            bi_ = bc(EI[:, mul, :], [[0, cnt], [1, 16]])
            z1, z2 = y1[:, 0:cnt, :], y2[:, 0:cnt, :]
            tt(z1, ar_, br_, ALU.mult, k5, k5)
            tt(z2, ai_, bi_, ALU.mult, k5, k5)
            tt(ER[:, dst0:dst0 + cnt, :], z1, z2, ALU.subtract, k5, k5)
            tt(z1, ar_, bi_, ALU.mult, k5, k5)
            tt(z2, ai_, br_, ALU.mult, k5, k5)
            tt(EI[:, dst0:dst0 + cnt, :], z1, z2, ALU.add, k5, k5)
        cmul_rng(2, 1, 1, 1)
        yield
        cmul_rng(3, 1, 2, 2)
        yield
        cmul_rng(5, 1, 4, 4)
        yield
        pw = PWR[:].rearrange("p r m q -> p r (m q)")
        tt(pw[:, 0, :], u[:, 0:144], w_[:, 0:144], ALU.mult, k2 + k5, ["PWR"])
        yield
        tt(pw[:, 1, :], u[:, 0:144], w_[:, 144:288], ALU.mult, k2 + k5, ["PWR"])
        yield
        dve(lambda e: e.tensor_copy(R8T[:], u[:, 128:144]), k2, ["R8T"])
        yield
        S.op("dve", lambda e: e.tensor_copy(TW[:, 0, :, 0:1], ER[:, 8, :].rearrange("p (q o) -> p q o", o=1)), reads=k5, writes=["TW"])
        S.op("dve", lambda e: e.tensor_copy(TW[:, 1, :, 0:1], EI[:, 8, :].rearrange("p (q o) -> p q o", o=1)), reads=k5, writes=["TW"])
        i6 = scr(4)
        tb = SCR[:, i6:i6 + 4, :].rearrange("p a b -> p (a b)")
        k6 = sk(i6, 4)
        n_ = 1
        while n_ < 128:
            z1 = tb[:, 0:16 * n_].rearrange("p (q k) -> p q k", q=16)
            z2 = tb[:, 1024:1024 + 16 * n_].rearrange("p (q k) -> p q k", q=16)
            ar_, ai_ = TW[:, 0, :, 0:n_], TW[:, 1, :, 0:n_]
            br_ = bc(TW[:, 0, :, n_ - 1:n_], [[128, 16], [0, n_]])
            bi_ = bc(TW[:, 1, :, n_ - 1:n_], [[128, 16], [0, n_]])
            tt(z1, ar_, br_, ALU.mult, ["TW"], k6)
            yield
            tt(z2, ai_, bi_, ALU.mult, ["TW"], k6)
            yield
            tt(TW[:, 0, :, n_:2 * n_], z1, z2, ALU.subtract, k6, ["TW"])
            yield
            tt(z1, ar_, bi_, ALU.mult, ["TW"], k6)
            yield
            tt(z2, ai_, br_, ALU.mult, ["TW"], k6)
            yield
            tt(TW[:, 1, :, n_:2 * n_], z1, z2, ALU.add, k6, ["TW"])
            yield
            n_ *= 2
        lr = LAMT[:, 0, q0:q0 + 16]
        li = LAMT[:, 1, q0:q0 + 16]
        v = t[:, 64:160]
        tt(t[:, 48:64], lr, lr, ALU.mult, ["LAMT"], sk(i))
        yield
        tt(v[:, 0:16], li, li, ALU.mult, ["LAMT"], sk(i))
        yield
        tt(t[:, 48:64], t[:, 48:64], v[:, 0:16], ALU.add, sk(i), sk(i))
        yield
        dve(lambda e: e.reciprocal(t[:, 48:64], t[:, 48:64]), sk(i), sk(i))
        yield
        dve(lambda e: e.tensor_scalar(out=v[:, 16:32], in0=PWR[:, 0, 1, :], scalar1=-1.0, scalar2=None, op0=ALU.add), ["PWR"], sk(i))
        yield
        tt(v[:, 32:48], v[:, 16:32], lr, ALU.mult, sk(i) + ["LAMT"], sk(i))
        yield
        tt(v[:, 48:64], PWR[:, 1, 1, :], li, ALU.mult, ["PWR", "LAMT"], sk(i))
        yield
        tt(v[:, 32:48], v[:, 32:48], v[:, 48:64], ALU.add, sk(i), sk(i))
        yield
        tt(v[:, 32:48], v[:, 32:48], t[:, 48:64], ALU.mult, sk(i), sk(i))
        yield
        tt(v[:, 64:80], PWR[:, 1, 1, :], lr, ALU.mult, ["PWR", "LAMT"], sk(i))
        yield
        tt(v[:, 80:96], v[:, 16:32], li, ALU.mult, sk(i) + ["LAMT"], sk(i))
        yield
        tt(v[:, 64:80], v[:, 64:80], v[:, 80:96], ALU.subtract, sk(i), sk(i))
        yield
        tt(v[:, 64:80], v[:, 64:80], t[:, 48:64], ALU.mult, sk(i), sk(i))
        yield
        fr = v[:, 32:48]
        fi = v[:, 64:80]
        i3 = scr(1)
        bb = STG
        BR = bb[:, 0:256].rearrange("p (q c) -> p q c", c=16)
        BI = bb[:, 256:512].rearrange("p (q c) -> p q c", c=16)
        frb = bc(fr, [[1, 16], [0, 16]])
        fib = bc(fi, [[1, 16], [0, 16]])
        w1 = SCR[:, i3, 0:256].rearrange("p (q c) -> p q c", c=16)
        w2 = SCR[:, i3, 256:512].rearrange("p (q c) -> p q c", c=16)
        kk = sk(i3) + sk(i) + STGB
        tt(w1, BR, frb, ALU.mult, kk, sk(i3))
        yield
        tt(w2, BI, fib, ALU.mult, kk, sk(i3))
        yield
        tt(w1, w1, w2, ALU.subtract, kk, sk(i3))
        yield
        tt(w2, BR, fib, ALU.mult, kk, sk(i3))
        yield
        tt(BR, BI, frb, ALU.mult, kk, STGB)
        yield
        tt(w2, w2, BR, ALU.add, kk, sk(i3))
        yield
        for ri, src in ((0, w1), (1, w2)):
            for g2 in range(2):
                dve(lambda e, ri=ri, src=src, g2=g2: e.tensor_scalar(
                    out=BXF[:, ri, :, g2 * 16:(g2 + 1) * 16], in0=src, scalar1=MASK2[:, g2:g2 + 1], scalar2=None, op0=ALU.mult),
                    sk(i3) + ["MASK2"], ["BXF"])
                yield
        dve(lambda e: e.tensor_copy(BXB[:], BXF[:]), ["BXF"], ["BXB"])
        yield
        cc = STG[:, 512:1536]
        for ri in range(2):
            for f in range(4):
                b = bank()
                S.op("pe", lambda e, b=b, ri=ri, f=f: e.matmul(PSUM[b][:, 0:128], lhsT=cc[:, ri * 512 + f * 128: ri * 512 + (f + 1) * 128],
                                                            rhs=IDF[:], start=True, stop=True), reads=STGK + STGC + ["IDF"], writes=[("ps", b)], mode=pmode(cc[:, ri * 512 + f * 128: ri * 512 + (f + 1) * 128]))
                tt(CXF[:, ri, f, :], PSUM[b][:, 0:128], MASKC[:], ALU.mult, [("ps", b), "MASKC"], ["CXF"])
                yield


    def ssm_prep_layer(l):
        for _ in ssm_prep_layer_g(l):
            pass

    WBPEND = {}

    def ssm_prep_wb_g(l, f, buf, part="all"):
        i = scr(4)
        big = SCR[:, i:i + 4, :].rearrange("p a b -> p (a b)")
        P1 = big[:, 0:1024].rearrange("p (m q c) -> p m q c", m=8, q=4)
        P2 = big[:, 1024:2048].rearrange("p (m q c) -> p m q c", m=8, q=4)
        keys = sk(i, 4)
        def apw(ri):
            a = PWR[:, ri, 0:8, 4 * f:4 * f + 4]
            return bc(a, [[16, 8], [1, 4], [0, 32]])
        def bx(ri):
            a = BXF[:, ri, 4 * f:4 * f + 4, :]
            return bc(a, [[0, 8], [32, 4], [1, 32]])
        jr = [scb(), scb()]
        ji = [scb(), scb()]
        tt(P1, apw(0), bx(0), ALU.mult, ["PWR", "BXF"], keys)
        yield
        tt(P2, apw(1), bx(1), ALU.mult, ["PWR", "BXF"], keys)
        yield
        for mh in range(2):
            dve(lambda e, mh=mh: e.tensor_tensor(out=SCB[:, jr[mh], :].rearrange("p (m q c) -> p m q c", m=4, q=4),
                                                 in0=P1[:, 4 * mh:4 * mh + 4], in1=P2[:, 4 * mh:4 * mh + 4], op=ALU.subtract),
                keys, [("scb", jr[mh])])
            yield
        tt(P1, apw(0), bx(1), ALU.mult, ["PWR", "BXF"] + [("scb", jr[0]), ("scb", jr[1])], keys)
        yield
        tt(P2, apw(1), bx(0), ALU.mult, ["PWR", "BXF"], keys)
        yield
        for mh in range(2):
            dve(lambda e, mh=mh: e.tensor_tensor(out=SCB[:, ji[mh], :].rearrange("p (m q c) -> p m q c", m=4, q=4),
                                                 in0=P1[:, 4 * mh:4 * mh + 4], in1=P2[:, 4 * mh:4 * mh + 4], op=ALU.add),
                keys, [("scb", ji[mh])])
            yield
        WBPEND[(f, buf)] = (jr, ji)
        if part == "all":
            yield from ssm_prep_wb_pe_g(l, f, buf)


    def ssm_prep_wb(l, f, buf, part="all"):
        for _ in ssm_prep_wb_g(l, f, buf, part):
            pass

    def ssm_prep_wb_pe_g(l, f, buf):
        jr, ji = WBPEND.pop((f, buf))
        for mh in range(2):
            for ri, jj in ((0, jr[mh]), (1, ji[mh])):
                b = bank()
                src = SCB[:, jj, :].rearrange("p (m q c) -> p m q c", m=4, q=4)
                for mm_ in range(4):
                    for rg in range(4):
                        S.op("pe", lambda e, b=b, src=src, mm_=mm_, rg=rg: e.matmul(
                            PSUM[b][32 * rg:32 * rg + 32, mm_ * 128:(mm_ + 1) * 128], lhsT=src[:, mm_, rg, :], rhs=IDB[:],
                            start=True, stop=True, tile_position=(0, 32 * rg)),
                            reads=[("scb", jj), "IDB"], writes=[("ps", b)], mode=pmode(src[:, mm_, rg, :]))
                act(lambda e, b=b, ri=ri, mh=mh: e.activation(out=WBF[buf][:, 4 * mh:4 * mh + 4, ri, :],
                                                             in_=PSUM[b][:, :].rearrange("p (m c) -> p m c", m=4), func=AF.Copy),
                    [("ps", b)], [("WBF", buf)])
                yield


    def ssm_prep_wb_pe(l, f, buf):
        for _ in ssm_prep_wb_pe_g(l, f, buf):
            pass

    def ssm_prep_ck_g(l, f, buf, part="all"):
        i = scr(6)
        big = SCR[:, i:i + 6, :].rearrange("p a b -> p (a b)")
        P1 = big[:, 0:1152].rearrange("p (m q c) -> p m q c", m=9, q=4)
        P2 = big[:, 1536:2688].rearrange("p (m q c) -> p m q c", m=9, q=4)
        keys = sk(i, 6)
        def apw(ri):
            return bc(PWR[:, ri, :, 4 * f:4 * f + 4], [[16, 9], [1, 4], [0, 32]])
        def cx(ri):
            return bc(CXF[:, ri, f, :], [[0, 9], [32, 4], [1, 32]])
        tt(P1, apw(0), cx(0), ALU.mult, ["PWR", "CXF"], keys)
        yield
        tt(P2, apw(1), cx(1), ALU.mult, ["PWR", "CXF"], keys)
        yield
        tt(CAF[buf][:, 0], P1, P2, ALU.subtract, keys, [("CAF", buf)])
        yield
        tt(P1, apw(1), cx(0), ALU.mult, ["PWR", "CXF"], keys)
        yield
        tt(P2, apw(0), cx(1), ALU.mult, ["PWR", "CXF"], keys)
        yield
        tt(P1, P1, P2, ALU.add, keys, keys)
        yield
        dve(lambda e: e.tensor_scalar(out=CAF[buf][:, 1], in0=P1, scalar1=-1.0, scalar2=None, op0=ALU.mult), keys, [("CAF", buf)])
        yield
        if part == "all":
            yield from ssm_prep_ck_pe_g(l, f, buf)


    def ssm_prep_ck(l, f, buf, part="all"):
        for _ in ssm_prep_ck_g(l, f, buf, part):
            pass

    def ssm_prep_ck_pe_g(l, f, buf):
        b = bank()
        for rg in range(4):
            q = 4 * f + rg
            for ri in range(2):
                S.op("pe", lambda e, b=b, rg=rg, q=q, ri=ri: e.matmul(
                    PSUM[b][32 * rg:32 * rg + 32, 0:256], lhsT=BXB[:, ri, q, :], rhs=CAF[buf][:, ri, 0:8, rg, :],
                    start=(ri == 0), stop=(ri == 1), tile_position=(0, 32 * rg)),
                    reads=["BXB", ("CAF", buf)], writes=[("ps", b)], mode=pmode(BXB[:, ri, q, :]))
        psv = PSUM[b][:, 0:256]
        dve(lambda e, psv=psv: e.tensor_tensor(
            out=KF[buf][:].rearrange("p t (r c) -> p t r c", r=4),
            in0=bc(psv, [[32, 8], [0, 4], [1, 32]]), in1=bc(MASK4[:], [[0, 8], [1, 4], [0, 32]]), op=ALU.mult),
            [("ps", b), "MASK4"], [("KF", buf)])
        yield
        dve(lambda e: e.scalar_tensor_tensor(out=KF[buf][:, 0, :], in0=IDF[:], scalar=pcol("d", l * 4 + f), in1=KF[buf][:, 0, :],
                                             op0=ALU.mult, op1=ALU.add), ["IDF", "PT", ("KF", buf)], [("KF", buf)])
        yield


    def ssm_prep_ck_pe(l, f, buf):
        for _ in ssm_prep_ck_pe_g(l, f, buf):
            pass

    def sample_loads_g(l):
        for (src_d, ri) in ((sre_in, 0), (sim_in, 1)):
            i = scr(4)
            raw = SCR[:, i:i + 4, :].rearrange("p a b -> p (a b)")
            load(raw[0:NS, :], src_d[l], sk(i, 4))
            for qh in range(2):
                b = bank()
                for qq in range(8):
                    q = qh * 8 + qq
                    S.op("pe", lambda e, b=b, q=q, qq=qq, raw=raw: e.matmul(
                        PSUM[b][:, qq * NS:(qq + 1) * NS], lhsT=raw[0:NS, q * 128:(q + 1) * 128], rhs=IDF[0:NS, 0:NS],
                        start=True, stop=True), reads=sk(i, 4) + ["IDF"], writes=[("ps", b)], mode=pmode(raw[0:NS, q * 128:(q + 1) * 128]))
                dve(lambda e, b=b, ri=ri, qh=qh: e.tensor_copy(SST[:, ri, qh * 8:(qh + 1) * 8, :],
                                                             PSUM[b][:, 0:8 * NS].rearrange("p (q t) -> p q t", t=NS)),
                    [("ps", b)], ["SST"])
                yield
        dve(lambda e: e.tensor_copy(SSTB[:], SST[:]), ["SST"], ["SSTB"])
        yield
        i = scr(2)
        raw = SCR[:, i:i + 2, :].rearrange("p a b -> p (a b)")
        load(raw[0:NS, :], sconv_in[l].rearrange("t k c -> t (k c)"), sk(i, 2))
        b = bank()
        for kf in range(8):
            S.op("pe", lambda e, b=b, kf=kf, raw=raw: e.matmul(
                PSUM[b][:, kf * NS:(kf + 1) * NS], lhsT=raw[0:NS, kf * 128:(kf + 1) * 128], rhs=IDF[0:NS, 0:NS],
                start=True, stop=True), reads=sk(i, 2) + ["IDF"], writes=[("ps", b)], mode=pmode(raw[0:NS, kf * 128:(kf + 1) * 128]))
        dve(lambda e, b=b: e.tensor_copy(SCV[:].rearrange("p k f t -> p (k f t)"), PSUM[b][:, 0:8 * NS]), [("ps", b)], ["SCV"])
        yield
        store(sconv_o[l, :, 0, :], sconv_in[l, :, 1, :], [])


    def sample_loads(l):
        for _ in sample_loads_g(l):
            pass

    SEQ_LH = [(h_, l_) for h_ in range(2) for l_ in range(DEPTH)]

    def prefetch_loads(half, l):
        for ri_, bsrc in ((0, b_re), (1, b_im)):
            for qq_ in range(4):
                load(STG[:, ri_ * 256 + qq_ * 64: ri_ * 256 + (qq_ + 1) * 64].rearrange("p (q c) -> p q c", c=16),
                     bsrc[l].rearrange("(q g) p c -> (g p) q c", g=2)[:, 4 * qq_:4 * qq_ + 4, :], STGK + STGB, nonc=True)
        cc = STG[:, 512:1536]
        for ri, csrc in ((0, c_re), (1, c_im)):
            for dup in range(2):
                load(cc[:, ri * 512:(ri + 1) * 512].rearrange("p (f d x) -> p f d x", f=4, d=2)[:, :, dup, :],
                     csrc[l].rearrange("(f g) c p -> (g c) f p", f=4), STGK + STGC, nonc=True)

    def prefetch_gen(half, l):
        if half == 0:
            yield from sample_loads_g(l)
        yield from ssm_prep_layer_g(l)
        yield from ssm_prep_wb_g(l, 0, 0, "dve")
        yield from ssm_prep_ck_g(l, 0, 0, "dve")
        yield from ssm_prep_ck_g(l, 1, 1, "dve")
        for _ in range(12):
            yield
        yield from ssm_prep_wb_pe_g(l, 0, 0)
        yield from ssm_prep_ck_pe_g(l, 0, 0)
        yield from ssm_prep_ck_pe_g(l, 1, 1)
        yield from ssm_prep_wb_g(l, 1, 1, "dve")
        for _ in range(12):
            yield
        yield from ssm_prep_wb_pe_g(l, 1, 1)

    PF = {"g": None}

    def pf_start(half, l):
        PF["g"] = prefetch_gen(half, l)
        st["ffn"] = True

    def pf_step(n=1):
        g = PF["g"]
        if g is None:
            return
        st["in_pf"] = True
        for _ in range(n):
            try:
                next(g)
            except StopIteration:
                PF["g"] = None
                break
        st["in_pf"] = False

    def pf_finish():
        while PF["g"] is not None:
            pf_step(64)
        st["ffn"] = False

    def prefetch(half, l):
        pf_start(half, l)
        pf_finish()

    def layer(half, l):
        blks = blocks(half)
        has_s = (half == 0)
        PH = DBG["phases"]
        on = lambda name: (PH is None or name in PH)

        u_start = ust["next"]
        n_layer_units = len(units) // (2 * DEPTH)

        def bail(name):
            if DBG.get("stop") != name:
                return False
            while ust["next"] < u_start + n_layer_units:
                i_ = ust["next"]
                done_unit(next_unit(units[i_][1]))
            return True
        PHN["p"] = "after_samp"
        if bail("samp"):
            return
        rmsnorm(half, "n1", l * 8, True)
        PHN["p"] = "after_norm"
        if bail("norm"):
            return

        PHN["p"] = "after_prep"
        if bail("prep"):
            return
        for f0 in range(2):
            w = next_unit("zu")
            for mt in range(2):
                f = 2 * f0 + mt
                for (c0, n, kind) in blks:
                    b = proj(w, mt, Hs, Hk, c0, n)
                    act(lambda e, b=b, f=f, c0=c0, n=n: e.activation(out=R[:, UT + f, c0:c0 + n], in_=PSUM[b][:, 0:n], func=AF.Copy),
                        [("ps", b)], [bkey("R", UT + f, c0)])
            done_unit(w)
        PHN["p"] = "after_zu"
        if bail("zu"):
            return
        for f in range(4):
            buf = f % 2
            bs = [bank() for _ in range(4)]
            for s in range(T):
                m = T - 1 - s
                for rg in range(4):
                    for ri in range(2):
                        mm(PSUM[bs[rg]][:, ri * 128:(ri + 1) * 128], WBF[buf][32 * rg:32 * rg + 32, m, ri, :],
                           R[32 * rg:32 * rg + 32, UT + f, s:1024:T], (s == 0 and ri == 0), (s == T - 1),
                           [("WBF", buf), bkey("R", UT + f, 0), bkey("R", UT + f, 512)], [("ps", bs[rg])],
                           tile_position=(32 * rg, 0), skip_group_check=True)
            if has_s:
                bss = [bank() for _ in range(4)]
                for rg in range(4):
                    for ri in range(2):
                        mm(PSUM[bss[rg]][:, ri * NS:(ri + 1) * NS], WBF[buf][32 * rg:32 * rg + 32, 0, ri, :],
                           R[32 * rg:32 * rg + 32, UT + f, 1024:1024 + NS], (ri == 0), True,
                           [("WBF", buf), bkey("R", UT + f, 1024)], [("ps", bss[rg])],
                           tile_position=(32 * rg, 0), skip_group_check=True)
            for rg in range(4):
                q = 4 * f + rg
                act(lambda e, rg=rg, q=q, bs=bs: e.activation(out=GS[:, :, q, :],
                                                             in_=PSUM[bs[rg]][:, 0:256].rearrange("p (r k) -> p r k", r=2), func=AF.Copy),
                    [("ps", bs[rg])], [("GSq", f)])
                if has_s:
                    act(lambda e, rg=rg, q=q, bss=bss: e.activation(out=SNEW[:, :, q, :],
                                                                   in_=PSUM[bss[rg]][:, 0:2 * NS].rearrange("p (r t) -> p r t", r=2), func=AF.Copy),
                        [("ps", bss[rg])], ["SNEW"])
            if f + 2 < 4:
                ssm_prep_wb(l, f + 2, buf)

        if has_s:
            ar = bc(PWR[:, 0, 1, :], [[1, 16], [0, NS]])
            ai = bc(PWR[:, 1, 1, :], [[1, 16], [0, NS]])
            i = scr()
            t1 = SCR[:, i, 0:256].rearrange("p (q t) -> p q t", t=NS)
            t2 = SCR[:, i, 256:512].rearrange("p (q t) -> p q t", t=NS)
            kk = sk(i)
            tt(t1, SST[:, 0], ar, ALU.mult, ["SST", "PWR"], kk)
            tt(t2, SST[:, 1], ai, ALU.mult, ["SST", "PWR"], kk)
            tt(t1, t1, t2, ALU.subtract, kk, kk)
            tt(SNEW[:, 0], SNEW[:, 0], t1, ALU.add, kk + ["SNEW"], ["SNEW"])
            tt(t1, SST[:, 1], ar, ALU.mult, ["SST", "PWR"], kk)
            tt(t2, SST[:, 0], ai, ALU.mult, ["SST", "PWR"], kk)
            tt(t1, t1, t2, ALU.add, kk, kk)
            tt(SNEW[:, 1], SNEW[:, 1], t1, ALU.add, kk + ["SNEW"], ["SNEW"])
            for (dst, ri) in ((sre_o, 0), (sim_o, 1)):
                i = scr(4)
                stg = SCR[:, i:i + 4, :].rearrange("p a b -> p (a b)")
                for qh in range(4):
                    b = bank()
                    for qq in range(4):
                        q = qh * 4 + qq
                        S.op("pe", lambda e, b=b, q=q, qq=qq, ri=ri: e.matmul(PSUM[b][0:NS, qq * 128:(qq + 1) * 128], lhsT=SNEW[:, ri, q, :], rhs=IDF[:],
                                                                        start=True, stop=True), reads=["SNEW", "IDF"], writes=[("ps", b)], mode=pmode(SNEW[:, ri, q, :]))
                    dve(lambda e, b=b, qh=qh, stg=stg: e.tensor_copy(stg[0:NS, qh * 512:(qh + 1) * 512], PSUM[b][0:NS, :]), [("ps", b)], sk(i, 4))
                store(dst[l], stg[0:NS, :], sk(i, 4))

        def rot_q(qq, sign_in):
            Gr = GS[:, 0, 4 * qq:4 * qq + 4, :]
            Gi = GS[:, 1, 4 * qq:4 * qq + 4, :]
            Cc = TW[:, 0, 4 * qq:4 * qq + 4, :]
            Sn = TW[:, 1, 4 * qq:4 * qq + 4, :]
            j1, j2, j3, j4 = scr(), scr(), scr(), scr()
            v4 = lambda j_: SCR[:, j_, :].rearrange("p (q k) -> p q k", q=4)
            gk = ("GSq", qq)
            tt(v4(j1), Cc, Gr, ALU.mult, ["TW", gk], sk(j1))
            tt(v4(j2), Sn, Gi, ALU.mult, ["TW", gk], sk(j2))
            tt(v4(j3), Cc, Gi, ALU.mult, ["TW", gk], sk(j3))
            tt(v4(j4), Sn, Gr, ALU.mult, ["TW", gk], sk(j4))
            tt(Gr, v4(j1), v4(j2), ALU.add if sign_in else ALU.subtract, sk(j1) + sk(j2) + sk(j3) + sk(j4), [gk])
            tt(Gi, v4(j3), v4(j4), ALU.subtract if sign_in else ALU.add, sk(j3) + sk(j4) + sk(j1) + sk(j2), [gk])

        def scan_q(f):
            gk = ("GSq", f)
            rot_q(f, True)
            for ri in range(2):
                for q in range(4 * f, 4 * f + 4):
                    dve(lambda e, ri=ri, q=q: e.tensor_tensor_scan(out=GS[:, ri, q, :], data0=bc(R8T[:, q:q + 1], [[0, 128]]), data1=GS[:, ri, q, :],
                                                                   initial=HC[:, l, ri, q:q + 1], op0=ALU.mult, op1=ALU.add),
                        [gk, "R8T", ("HCq", f)], [gk])
            rot_q(f, False)
            qs = slice(4 * f, 4 * f + 4)
            dve(lambda e: e.tensor_copy(HBF[:, :, qs, 0:1], HC[:, l, :, qs].rearrange("p r (q o) -> p r q o", o=1)), [("HCq", f)], [("HBFq", f)])
            act(lambda e: e.activation(out=HBF[:, :, qs, 1:129], in_=GS[:, :, qs, :], func=AF.Copy), [gk], [("HBFq", f)])
            dve(lambda e: e.tensor_copy(HC[:, l, :, qs].rearrange("p r (q o) -> p r q o", o=1), GS[:, :, qs, 127:128]), [gk, ("HBFq", f)], [("HCq", f)])

        PHN["p"] = "after_b"
        if bail("b"):
            return
        for f0 in range(2):
            wc = next_unit("zc")
            wv = next_unit("zv")
            wb_ = next_unit("zb")
            for mt in range(2):
                f = 2 * f0 + mt
                w0, w1, w2, cb = pcol("cw", l * 12 + 0 * 4 + f), pcol("cw", l * 12 + 4 + f), pcol("cw", l * 12 + 8 + f), pcol("cb", l * 4 + f)
                prev_cin = None
                for (c0, n, kind) in blks:
                    bc_ = proj(wc, mt, Hs, Hk, c0, n)
                    bv = proj(wv, mt, Hs, Hk, c0, n)
                    bb_ = proj(wb_, mt, Hs, Hk, c0, n)
                    iv = scr()
                    act(lambda e, bv=bv, iv=iv, n=n: e.activation(out=SCR[:, iv, 0:n], in_=PSUM[bv][:, 0:n], func=AF.Copy), [("ps", bv)], sk(iv))
                    ia = scr()
                    if kind == "p":
                        ic = scr(2)
                        cin = SCR[:, ic:ic + 2, :].rearrange("p a b -> p (a b)")
                        ck = sk(ic, 2)
                        if prev_cin is None:
                            dve(lambda e, cin=cin, f=f: e.tensor_copy(cin[:, 0:2], CONVST[:, l, :, f]), ["CONVST"], ck)
                        else:
                            pc, pk = prev_cin
                            dve(lambda e, cin=cin, pc=pc: e.tensor_copy(cin[:, 0:2], pc[:, 512:514]), pk, ck)
                        tt(cin[:, 2:2 + n], PSUM[bc_][:, 0:n], SCR[:, iv, 0:n], ALU.mult, [("ps", bc_)] + sk(iv), ck)
                        acc = SCR[:, ia, 0:n]
                        dve(lambda e, acc=acc, cin=cin, n=n, w2=w2, cb=cb: e.tensor_scalar(out=acc, in0=cin[:, 2:2 + n], scalar1=w2, scalar2=cb, op0=ALU.mult, op1=ALU.add),
                            ck + ["PT"], sk(ia))
                        dve(lambda e, acc=acc, cin=cin, n=n, w1=w1: e.scalar_tensor_tensor(out=acc, in0=cin[:, 1:1 + n], scalar=w1, in1=acc, op0=ALU.mult, op1=ALU.add),
                            ck + sk(ia) + ["PT"], sk(ia))
                        dve(lambda e, acc=acc, cin=cin, n=n, w0=w0: e.scalar_tensor_tensor(out=acc, in0=cin[:, 0:n], scalar=w0, in1=acc, op0=ALU.mult, op1=ALU.add),
                            ck + sk(ia) + ["PT"], sk(ia))
                        tt(R[:, YA + f, c0:c0 + n], acc, PSUM[bb_][:, 0:n], ALU.mult, sk(ia) + [("ps", bb_)], [bkey("R", YA + f, c0)])
                        prev_cin = (cin, ck)
                        if c0 == 512:
                            dve(lambda e, cin=cin, f=f: e.tensor_copy(CONVST[:, l, :, f], cin[:, 512:514]), ck, ["CONVST"])
                    else:
                        tt(CINS[:, f, :], PSUM[bc_][:, 0:n], SCR[:, iv, 0:n], ALU.mult, [("ps", bc_)] + sk(iv), ["CINS"])
                        acc = SCR[:, ia, 0:n]
                        dve(lambda e, acc=acc, f=f, w2=w2, cb=cb: e.tensor_scalar(out=acc, in0=CINS[:, f, :], scalar1=w2, scalar2=cb, op0=ALU.mult, op1=ALU.add),
                            ["CINS", "PT"], sk(ia))
                        dve(lambda e, acc=acc, f=f, w1=w1: e.scalar_tensor_tensor(out=acc, in0=SCV[:, 1, f, :], scalar=w1, in1=acc, op0=ALU.mult, op1=ALU.add),
                            ["SCV", "PT"] + sk(ia), sk(ia))
                        dve(lambda e, acc=acc, f=f, w0=w0: e.scalar_tensor_tensor(out=acc, in0=SCV[:, 0, f, :], scalar=w0, in1=acc, op0=ALU.mult, op1=ALU.add),
                            ["SCV", "PT"] + sk(ia), sk(ia))
                        tt(R[:, YA + f, c0:c0 + n], acc, PSUM[bb_][:, 0:n], ALU.mult, sk(ia) + [("ps", bb_)], [bkey("R", YA + f, c0)])
            done_unit(wc)
            done_unit(wv)
            done_unit(wb_)
            scan_q(f0)
        if has_s:
            b = bank()
            for f in range(4):
                S.op("pe", lambda e, b=b, f=f: e.matmul(PSUM[b][0:NS, f * 128:(f + 1) * 128], lhsT=CINS[:, f, :], rhs=IDF[:],
                                                      start=True, stop=True), reads=["CINS", "IDF"], writes=[("ps", b)], mode=pmode(CINS[:, f, :]))
            i = scr()
            dve(lambda e, b=b, i=i: e.tensor_copy(SCR[0:NS, i, :], PSUM[b][0:NS, :]), [("ps", b)], sk(i))
            store(sconv_o[l, :, 1, :], SCR[0:NS, i, :], sk(i))

        PHN["p"] = "after_conv"
        if bail("conv"):
            return
        PHN["p"] = "after_scan"
        if bail("scan"):
            return
        ybank = {}

        def part_a(f):
            buf = f % 2
            for (c0, n, kind) in blks:
                b = bank()
                ybank[(f, c0)] = b
                if kind == "p":
                    first = True
                    for j in range(T):
                        for tau in range(j + 1):
                            for rg in range(4):
                                mm(PSUM[b][32 * rg:32 * rg + 32, j:512:T], KF[buf][:, tau, 32 * rg:32 * rg + 32], R[:, UT + f, c0 + j - tau:c0 + 512:T],
                                   first, False, [("KF", buf), bkey("R", UT + f, c0)], [("ps", b)], tile_position=(0, 32 * rg), skip_group_check=True)
                            first = False
                else:
                    for rg in range(4):
                        mm(PSUM[b][32 * rg:32 * rg + 32, 0:n], KF[buf][:, 0, 32 * rg:32 * rg + 32], R[:, UT + f, c0:c0 + n], True, False,
                           [("KF", buf), bkey("R", UT + f, c0)], [("ps", b)], tile_position=(0, 32 * rg), skip_group_check=True)

        def part_d(f):
            buf = f % 2
            for (c0, n, kind) in blks:
                b = ybank[(f, c0)]
                if kind == "p":
                    kb = c0 // T
                    for j in range(T):
                        for rg in range(4):
                            for ri in range(2):
                                mm(PSUM[b][32 * rg:32 * rg + 32, j:512:T], CAF[buf][:, ri, j + 1, rg, :], HBF[:, ri, 4 * f + rg, kb:kb + 64],
                                   False, (j == T - 1 and rg == 3 and ri == 1), [("CAF", buf), ("HBFq", f)], [("ps", b)],
                                   tile_position=(0, 32 * rg), skip_group_check=True)
                else:
                    for rg in range(4):
                        for ri in range(2):
                            mm(PSUM[b][32 * rg:32 * rg + 32, 0:n], CAF[buf][:, ri, 1, rg, :], SSTB[:, ri, 4 * f + rg, :],
                               False, (rg == 3 and ri == 1), [("CAF", buf), "SSTB"], [("ps", b)], tile_position=(0, 32 * rg), skip_group_check=True)
            for (c0, n, kind) in blks:
                b = ybank[(f, c0)]
                act(lambda e, b=b, f=f, c0=c0, n=n: e.activation(out=R[:, YS + f, c0:c0 + n], in_=PSUM[b][:, 0:n], func=AF.Gelu),
                    [("ps", b)], [bkey("R", YS + f, c0)])

        part_a(0)
        for f in range(4):
            if f >= 2:
                scan_q(f)
            if f + 1 < 4:
                part_a(f + 1)
            part_d(f)
            if f + 2 < 4:
                ssm_prep_ck(l, f + 2, f % 2)
        PHN["p"] = "after_ad"
        if bail("ad"):
            return
        w = next_unit("glu")
        for mo in range(4):
            for (c0, n, kind) in blks:
                b = proj(w, mo, Rt(YS), Rk(YS), c0, n)
                i = scr()
                act(lambda e, b=b, i=i, n=n, mo=mo: e.activation(out=SCR[:, i, 0:n], in_=PSUM[b][:, 0:n], func=AF.Sigmoid, bias=pcol("bg", l * 4 + mo), scale=1.0),
                    [("ps", b), "PT"], sk(i))
                tt(R[:, UT + mo, c0:c0 + n], R[:, YS + mo, c0:c0 + n], SCR[:, i, 0:n], ALU.mult, [bkey("R", YS + mo, c0)] + sk(i), [bkey("R", UT + mo, c0)])
        done_unit(w)
        MT = [YS + 0, YS + 1, YS + 2, YS + 3, 12, 13, 14, 15]
        PHN["p"] = "after_glu"
        if bail("glu"):
            return
        for jh in range(4):
            wga = next_unit("ga")
            wgs = next_unit("gs")
            wbr = next_unit("wb")
            for mt in range(2):
                j = 2 * jh + mt
                for (c0, n, kind) in blks:
                    bga = proj(wga, mt, Hs, Hk, c0, n)
                    bgs = proj(wgs, mt, Hs, Hk, c0, n)
                    boa = proj(wbr, mt, Rt(YA), Rk(YA), c0, n, kts=[(k, k) for k in range(4)])
                    bob = proj(wbr, mt, Rt(UT), Rk(UT), c0, n, kts=[(4 + k, k) for k in range(4)])
                    ia, ib = scr(), scr()
                    act(lambda e, bga=bga, ia=ia, n=n: e.activation(out=SCR[:, ia, 0:n], in_=PSUM[bga][:, 0:n], func=AF.Sigmoid), [("ps", bga)], sk(ia))
                    act(lambda e, bgs=bgs, ib=ib, n=n: e.activation(out=SCR[:, ib, 0:n], in_=PSUM[bgs][:, 0:n], func=AF.Sigmoid), [("ps", bgs)], sk(ib))
                    tt(SCR[:, ia, 0:n], SCR[:, ia, 0:n], PSUM[boa][:, 0:n], ALU.mult, sk(ia) + [("ps", boa)], sk(ia))
                    tt(SCR[:, ib, 0:n], SCR[:, ib, 0:n], PSUM[bob][:, 0:n], ALU.mult, sk(ib) + [("ps", bob)], sk(ib))
                    tt(R[:, MT[j], c0:c0 + n], SCR[:, ia, 0:n], SCR[:, ib, 0:n], ALU.add, sk(ia) + sk(ib), [bkey("R", MT[j], c0)])
            done_unit(wga)
            done_unit(wgs)
            done_unit(wbr)
        Ms = lambda k: R[:, MT[k], :]
        Mk = lambda k, c0: bkey("R", MT[k], c0)
        PHN["p"] = "after_gates"
        if bail("gates"):
            return
        for jh in range(4):
            w = next_unit("wo")
            for mt in range(2):
                j = 2 * jh + mt
                for (c0, n, kind) in blks:
                    b = proj(w, mt, Ms, Mk, c0, n)
                    tt(X[:, j, c0:c0 + n], X[:, j, c0:c0 + n], PSUM[b][:, 0:n], ALU.add, [bkey("X", j, c0), ("ps", b)], [bkey("X", j, c0)])
            done_unit(w)
        PHN["p"] = "after_wout"
        if bail("wout"):
            return
        PHN["p"] = "ffn"
        idx_ = SEQ_LH.index((half, l))
        if idx_ + 1 < len(SEQ_LH) and SEQ_LH[idx_ + 1][1] < DBG["layers"]:
            prefetch_loads(*SEQ_LH[idx_ + 1])
        rmsnorm(half, "n2", l * 8, True)
        As = lambda k: R[:, k, :]
        Ak = lambda k, c0: bkey("R", k, c0)
        for (t0, nt) in ((0, 12), (12, 10)):
            for pr in range(nt // 2):
                if pr == 1 and t0 == 0:
                    idx0_ = SEQ_LH.index((half, l))
                    nx0_ = SEQ_LH[idx0_ + 1] if (idx0_ + 1 < len(SEQ_LH) and SEQ_LH[idx0_ + 1][1] < DBG["layers"]) else None
                    if nx0_ is not None:
                        pf_start(nx0_[0], nx0_[1])
                wg = next_unit("fg")
                wu = next_unit("fu")
                for mt in range(2):
                    ti = 2 * pr + mt
                    for (c0, n, kind) in blks:
                        bg = proj(wg, mt, Hs, Hk, c0, n)
                        bu = proj(wu, mt, Hs, Hk, c0, n)
                        i = scr()
                        act(lambda e, bg=bg, i=i, n=n: e.activation(out=SCR[:, i, 0:n], in_=PSUM[bg][:, 0:n], func=AF.Silu), [("ps", bg)], sk(i))
                        tt(R[:, ti, c0:c0 + n], SCR[:, i, 0:n], PSUM[bu][:, 0:n], ALU.mult, sk(i) + [("ps", bu)], [bkey("R", ti, c0)])
                        pf_step(PFN)
                done_unit(wg)
                done_unit(wu)
            idx_ = SEQ_LH.index((half, l))
            nxt_ = SEQ_LH[idx_ + 1] if (idx_ + 1 < len(SEQ_LH) and SEQ_LH[idx_ + 1][1] < DBG["layers"]) else None
            for jq in range(4):
                ws = []
                rem = nt
                while rem > 0:
                    ws.append(next_unit("fd"))
                    rem -= ws[-1].kt
                for mt in range(2):
                    j = 2 * jq + mt
                    for (c0, n, kind) in blks:
                        b = bank()
                        kidx = 0
                        for wi, w in enumerate(ws):
                            for kw in range(w.kt):
                                mm(PSUM[b][:, 0:n], w.v[:, kw, mt * 128:(mt + 1) * 128], R[:, kidx, c0:c0 + n], kidx == 0, kidx == nt - 1,
                                   [w.key, bkey("R", kidx, c0)], [("ps", b)])
                                kidx += 1
                        tt(X[:, j, c0:c0 + n], X[:, j, c0:c0 + n], PSUM[b][:, 0:n], ALU.add, [bkey("X", j, c0), ("ps", b)], [bkey("X", j, c0)])
                        pf_step(PFN)
                for w in ws:
                    done_unit(w)

        pf_finish()

    PFN = 3

    def DBG_skip_layer():
        n_layer_units = len(units) // (2 * DEPTH)
        for _ in range(n_layer_units):
            i_ = ust["next"]
            done_unit(next_unit(units[i_][1]))

    prefetch_loads(0, 0)
    prefetch(0, 0)
    for half in range(2):
        tiles = [(xp, half * 1024 + 128 * t, 128, 128 * t) for t in range(8)]
        if half == 0:
            tiles.append((xs, 0, NS, 1024))
        for (src, r0, nr, c0) in tiles:
            i = scr(2)
            stg = SCR[:, i:i + 2, :].rearrange("p a b -> p (a b)")
            load(stg[0:nr, :], src[r0:r0 + nr, :], sk(i, 2))
            for kh in range(2):
                b = bank()
                for kk in range(4):
                    kt = kh * 4 + kk
                    S.op("pe", lambda e, b=b, kk=kk, kt=kt, stg=stg, nr=nr: e.matmul(
                        PSUM[b][:, kk * 128:kk * 128 + nr], lhsT=stg[0:nr, kt * 128:(kt + 1) * 128], rhs=IDF[0:nr, 0:nr], start=True, stop=True),
                        reads=sk(i, 2) + ["IDF"], writes=[("ps", b)], mode=pmode(stg[0:nr, kt * 128:(kt + 1) * 128]))
                cb0 = (c0 // 512) * 512
                dve(lambda e, b=b, kh=kh, c0=c0, nr=nr: e.tensor_copy(X[:, kh * 4:kh * 4 + 4, c0:c0 + nr],
                                                                   PSUM[b][:, :].rearrange("p (k c) -> p k c", k=4)[:, :, 0:nr]),
                    [("ps", b)], [bkey("X", kh * 4 + kk, cb0) for kk in range(4)])
        for l in range(DEPTH):
            if l < DBG["layers"]:
                layer(half, l)
            else:
                DBG_skip_layer()
        rmsnorm(half, "fg", 0, False)
        for (dst, r0, nr, c0) in [(yp, half * 1024 + 128 * t, 128, 128 * t) for t in range(8)] + ([(ysm, 0, NS, 1024)] if half == 0 else []):
            i = scr(2)
            stg = SCR[:, i:i + 2, :].rearrange("p a b -> p (a b)")
            cb0 = (c0 // 512) * 512
            for kh in range(2):
                b = bank()
                for kk in range(4):
                    kt = kh * 4 + kk
                    S.op("pe", lambda e, b=b, kk=kk, kt=kt, c0=c0, nr=nr: e.matmul(
                        PSUM[b][0:nr, kk * 128:(kk + 1) * 128], lhsT=X[:, kt, c0:c0 + nr], rhs=IDF[:], start=True, stop=True),
                        reads=[bkey("X", kt, cb0), "IDF"], writes=[("ps", b)], mode=pmode(X[:, kt, c0:c0 + nr]))
                dve(lambda e, b=b, kh=kh, stg=stg, nr=nr: e.tensor_copy(stg[0:nr, kh * 512:(kh + 1) * 512], PSUM[b][0:nr, :]), [("ps", b)], sk(i, 2))
            store(dst[r0:r0 + nr, :], stg[0:nr, :], sk(i, 2))

    for l in range(DEPTH):
        b = bank()
        S.op("pe", lambda e, b=b, l=l: e.matmul(PSUM[b][0:32, 0:128], lhsT=HC[:, l].rearrange("p r q -> p (r q)"), rhs=IDF[:], start=True, stop=True),
             reads=[("HCq", f_) for f_ in range(4)] + ["IDF"], writes=[("ps", b)], mode=pmode(HC[:, l].rearrange("p r q -> p (r q)")))
        S.op("pe", lambda e, b=b, l=l: e.matmul(PSUM[b][0:8, 128:256], lhsT=CONVST[:, l].rearrange("p k f -> p (k f)"), rhs=IDF[:], start=True, stop=True),
             reads=["CONVST", "IDF"], writes=[("ps", b)], mode=pmode(CONVST[:, l].rearrange("p k f -> p (k f)")))
        i = scr()
        dve(lambda e, b=b, i=i: e.tensor_copy(SCR[0:32, i, 0:256], PSUM[b][0:32, 0:256]), [("ps", b)], sk(i))
        store(pre_o[l], SCR[0:16, i, 0:128], sk(i))
        store(pim_o[l], SCR[16:32, i, 0:128], sk(i))
        store(pconv_o[l].rearrange("k (f p) -> (k f) p", p=128), SCR[0:8, i, 128:256], sk(i))
    fin = [S.last_dma[f"st{i}"] for i in range(4) if f"st{i}" in S.last_dma]
    S.op("sp", lambda e: (e.engine_nop() if hasattr(e, "engine_nop") else None), deps=fin)
    assert ust["next"] == len(units), (ust, len(units))

    S.finalize()
    with nc.Block() as block:
        block.sync(S.runner("sp", sems, dsems))
        block.tensor(S.runner("pe", sems, dsems))
        block.scalar(S.runner("act", sems, dsems))
        block.vector(S.runner("dve", sems, dsems))
        block.gpsimd(S.runner("pool", sems, dsems))
    es.close()
    return nc


_NC = None


def kernel(**inp):
    global _NC
    if _NC is None:
        _NC = build()
    nc = _NC
    f = lambda a: np.ascontiguousarray(np.asarray(a, dtype=np.float32))
    shared = {k: f(inp[k]) for k in ("norm1_g", "w_in", "conv_w", "conv_b", "ssm_lam_re", "ssm_lam_im", "ssm_log_dt",
                                     "ssm_b_re", "ssm_b_im", "ssm_c_re", "ssm_c_im", "ssm_d", "w_glu", "b_glu",
                                     "w_branch", "w_out", "norm2_g", "w_gate_up", "w_down")}
    shared["final_g"] = f(inp["final_g"]).reshape(1, D)
    xpr = f(inp["x_prompt"])
    xsm = f(inp["x_sample"])
    sc = f(inp["state_conv"])
    sr = f(inp["state_ssm_re"])
    si = f(inp["state_ssm_im"])
    in_maps = []
    for c in range(8):
        m = dict(shared)
        m["xp"] = xpr[c]
        m["xs"] = np.ascontiguousarray(xsm[16 * c:16 * c + 16, 0, :])
        m["sconv"] = np.ascontiguousarray(sc[:, 16 * c:16 * c + 16])
        m["sre"] = np.ascontiguousarray(sr[:, 16 * c:16 * c + 16].reshape(DEPTH, NS, 2048))
        m["sim"] = np.ascontiguousarray(si[:, 16 * c:16 * c + 16].reshape(DEPTH, NS, 2048))
        in_maps.append(m)
    res = run_bass_kernel_spmd(nc, in_maps, core_ids=list(range(8)))
    rs = res.results
    y_prompt = np.stack([r["yp"] for r in rs], 0)
    y_sample = np.concatenate([r["ysm"] for r in rs], 0).reshape(128, 1, D)
    prompt_conv = np.stack([r["pconv"] for r in rs], 1)
    prompt_re = np.stack([r["pre"].reshape(DEPTH, 32, 64) for r in rs], 1)
    prompt_im = np.stack([r["pim"].reshape(DEPTH, 32, 64) for r in rs], 1)
    sample_conv = np.concatenate([r["sconv_o"] for r in rs], 1)
    sample_re = np.concatenate([r["sre_o"].reshape(DEPTH, NS, 32, 64) for r in rs], 1)
    sample_im = np.concatenate([r["sim_o"].reshape(DEPTH, NS, 32, 64) for r in rs], 1)
    return (y_prompt.astype(np.float32), y_sample.astype(np.float32), prompt_conv.astype(np.float32),
            prompt_re.astype(np.float32), prompt_im.astype(np.float32), sample_conv.astype(np.float32),
            sample_re.astype(np.float32), sample_im.astype(np.float32))
```
